# Optimizing a Trainium2 kernel written in Bass

```python
import jax, jax.numpy as jnp
from jax import lax
import numpy as np

D_MODEL = 1024
BATCH = 4
SEQ = 8192
DEPTH = 2

N_META = 16
N_HEADS = 16
HEAD_DIM = 64
N_KV_HEADS = 4
ATTN_WIDTH = N_HEADS * HEAD_DIM
KV_WIDTH = N_KV_HEADS * HEAD_DIM
IDX_HEADS = 8
IDX_DIM = 64
TOPK_MAX = 256
Q_BLOCK = 128
ROPE_THETA = 500000.0
ROT_DIM = HEAD_DIM // 4
CONV_CH = D_MODEL
CONV_WIDTH = 31
FFN_HIDDEN = 2816
FFN_CONV_WIDTH = 3
RMS_EPS = 1e-6
LN_EPS = 1e-5
SPLIT_SIZES = (ATTN_WIDTH, KV_WIDTH, KV_WIDTH, IDX_HEADS * IDX_DIM, IDX_DIM, IDX_HEADS,
               2 * CONV_CH, D_MODEL, D_MODEL)
IN_COLS = int(sum(SPLIT_SIZES))
SPLIT_POINTS = tuple(int(v) for v in np.cumsum(SPLIT_SIZES)[:-1])

kernel_name = "hybrid_dsa_conformer_gated_trunk"


def rmsnorm(x, g):
    x32 = x.astype(jnp.float32)
    y = x32 * lax.rsqrt(jnp.mean(x32 * x32, axis=-1, keepdims=True) + RMS_EPS)
    return (y * g.astype(jnp.float32)).astype(x.dtype)


def layernorm(x, g, b):
    x32 = x.astype(jnp.float32)
    mu = jnp.mean(x32, axis=-1, keepdims=True)
    var = jnp.mean(jnp.square(x32 - mu), axis=-1, keepdims=True)
    y = (x32 - mu) * lax.rsqrt(var + LN_EPS)
    return (y * g.astype(jnp.float32) + b.astype(jnp.float32)).astype(x.dtype)


def causal_dwconv(x, w, b):
    K, C = w.shape
    y = lax.conv_general_dilated(x, w[:, None, :].astype(x.dtype), window_strides=(1,),
                                 padding=[(K - 1, 0)],
                                 dimension_numbers=('NWC', 'WIO', 'NWC'),
                                 feature_group_count=C)
    return y + b.astype(x.dtype)


def rope_tables(T):
    pos = jnp.arange(T, dtype=jnp.float32)
    inv_freq = jnp.power(jnp.float32(ROPE_THETA),
                         -jnp.arange(0, ROT_DIM, 2, dtype=jnp.float32) / ROT_DIM)
    ang = pos[:, None] * inv_freq[None, :]
    return jnp.cos(ang), jnp.sin(ang)


def apply_partial_rope(x, cos, sin):
    half = ROT_DIM // 2
    x1 = x[..., :half].astype(jnp.float32)
    x2 = x[..., half:ROT_DIM].astype(jnp.float32)
    c = cos[:, None, :]
    s = sin[:, None, :]
    rot = jnp.concatenate([x1 * c - x2 * s, x2 * c + x1 * s], axis=-1).astype(x.dtype)
    return jnp.concatenate([rot, x[..., ROT_DIM:]], axis=-1)


def dsa_attention(q, k, v, qi, ki, wi):
    B, T = q.shape[0], q.shape[1]
    k_sel = min(TOPK_MAX, T // 4)
    n_blocks = -(-T // Q_BLOCK)
    pad = n_blocks * Q_BLOCK - T
    q = jnp.pad(q, ((0, 0), (0, pad), (0, 0), (0, 0)))
    qi = jnp.pad(qi, ((0, 0), (0, pad), (0, 0), (0, 0)))
    wi = jnp.pad(wi, ((0, 0), (0, pad), (0, 0)))
    key_pos = jnp.arange(T)
    rep = N_HEADS // N_KV_HEADS
    attn_scale = HEAD_DIM ** -0.5
    idx_scale = IDX_DIM ** -0.5
    gather = jax.vmap(lambda table, idx: table[idx])

    def block(i):
        start = i * Q_BLOCK
        qb = lax.dynamic_slice_in_dim(q, start, Q_BLOCK, axis=1)
        qib = lax.dynamic_slice_in_dim(qi, start, Q_BLOCK, axis=1)
        wib = lax.dynamic_slice_in_dim(wi, start, Q_BLOCK, axis=1)
        q_pos = start + jnp.arange(Q_BLOCK)
        causal = key_pos[None, :] <= q_pos[:, None]
        logits = jnp.einsum('bqhd,bkd->bhqk', qib, ki).astype(jnp.float32) * idx_scale
        score = jnp.einsum('bhqk,bqh->bqk', jax.nn.relu(logits), wib.astype(jnp.float32))
        score = jnp.where(causal[None], score, -jnp.inf)
        _, sel = lax.top_k(score, k_sel)
        valid = sel <= q_pos[None, :, None]
        kg = gather(k, sel)
        vg = gather(v, sel)
        qg = qb.reshape(B, Q_BLOCK, N_KV_HEADS, rep, HEAD_DIM)
        s = jnp.einsum('bqgrd,bqsgd->bqgrs', qg, kg).astype(jnp.float32) * attn_scale
        s = jnp.where(valid[:, :, None, None, :], s, -jnp.inf)
        p = jax.nn.softmax(s, axis=-1).astype(vg.dtype)
        o = jnp.einsum('bqgrs,bqsgd->bqgrd', p, vg)
        return o.reshape(B, Q_BLOCK, ATTN_WIDTH)

    out = lax.map(block, jnp.arange(n_blocks))
    out = jnp.transpose(out, (1, 0, 2, 3)).reshape(B, n_blocks * Q_BLOCK, ATTN_WIDTH)
    return out[:, :T]


def hybrid_mixer(xn, cos, sin, w_in, w_attn_out, conv_dw_w, conv_dw_b, conv_ln_g, conv_ln_b,
                 w_conv_out, w_o):
    B, T, _ = xn.shape
    proj = xn @ w_in
    q, k, v, qi, ki, wi, glu_in, gate_a, gate_c = jnp.split(proj, SPLIT_POINTS, axis=-1)
    q = apply_partial_rope(q.reshape(B, T, N_HEADS, HEAD_DIM), cos, sin)
    k = apply_partial_rope(k.reshape(B, T, N_KV_HEADS, HEAD_DIM), cos, sin)
    v = v.reshape(B, T, N_KV_HEADS, HEAD_DIM)
    qi = apply_partial_rope(qi.reshape(B, T, IDX_HEADS, IDX_DIM), cos, sin)
    ki = apply_partial_rope(ki.reshape(B, T, 1, IDX_DIM), cos, sin)[:, :, 0]
    wi = wi * (IDX_HEADS ** -0.5)
    y_attn = dsa_attention(q, k, v, qi, ki, wi) @ w_attn_out
    a, b = jnp.split(glu_in, 2, axis=-1)
    u = a * jax.nn.sigmoid(b)
    u = causal_dwconv(u, conv_dw_w, conv_dw_b)
    u = jax.nn.silu(layernorm(u, conv_ln_g, conv_ln_b))
    y_conv = u @ w_conv_out
    merged = jax.nn.sigmoid(gate_a) * y_attn + jax.nn.sigmoid(gate_c) * y_conv
    return merged @ w_o


def conv_ffn(xn, w_up, ffn_dw_w, ffn_dw_b, w_down):
    h = causal_dwconv(xn @ w_up, ffn_dw_w, ffn_dw_b)
    g, u = jnp.split(h, 2, axis=-1)
    return (jax.nn.silu(g) * u) @ w_down


def setup_inputs(seed: int = 0) -> dict:
    key = jax.random.key(seed)
    ks = jax.random.split(key, 20)
    f32 = jnp.float32
    n = lambda k, shape, s: jax.random.normal(k, shape, f32) * s
    return {
        "x": n(ks[0], (BATCH, SEQ, D_MODEL), 1.0),
        "meta_tokens": n(ks[1], (N_META, D_MODEL), 1.0),
        "attn_norm_g": 1.0 + n(ks[2], (DEPTH, D_MODEL), 0.01),
        "w_in": n(ks[3], (DEPTH, D_MODEL, IN_COLS), D_MODEL ** -0.5),
        "w_attn_out": n(ks[4], (DEPTH, ATTN_WIDTH, D_MODEL), ATTN_WIDTH ** -0.5),
        "conv_dw_w": n(ks[5], (DEPTH, CONV_WIDTH, CONV_CH), CONV_WIDTH ** -0.5),
        "conv_dw_b": n(ks[6], (DEPTH, CONV_CH), 0.01),
        "conv_ln_g": 1.0 + n(ks[7], (DEPTH, CONV_CH), 0.01),
        "conv_ln_b": n(ks[8], (DEPTH, CONV_CH), 0.01),
        "w_conv_out": n(ks[9], (DEPTH, CONV_CH, D_MODEL), CONV_CH ** -0.5),
        "w_o": n(ks[10], (DEPTH, D_MODEL, D_MODEL), D_MODEL ** -0.5),
        "ffn_norm_g": 1.0 + n(ks[11], (DEPTH, D_MODEL), 0.01),
        "w_up": n(ks[12], (DEPTH, D_MODEL, 2 * FFN_HIDDEN), D_MODEL ** -0.5),
        "ffn_dw_w": n(ks[13], (DEPTH, FFN_CONV_WIDTH, 2 * FFN_HIDDEN), FFN_CONV_WIDTH ** -0.5),
        "ffn_dw_b": n(ks[14], (DEPTH, 2 * FFN_HIDDEN), 0.01),
        "w_down": n(ks[15], (DEPTH, FFN_HIDDEN, D_MODEL), FFN_HIDDEN ** -0.5),
        "final_norm_g": 1.0 + n(ks[16], (D_MODEL,), 0.01),
    }


def reference(x, meta_tokens, attn_norm_g, w_in, w_attn_out, conv_dw_w, conv_dw_b, conv_ln_g,
              conv_ln_b, w_conv_out, w_o, ffn_norm_g, w_up, ffn_dw_w, ffn_dw_b, w_down,
              final_norm_g):
    B = x.shape[0]
    T = N_META + x.shape[1]
    meta = jnp.broadcast_to(meta_tokens[None].astype(x.dtype), (B, N_META, D_MODEL))
    h = jnp.concatenate([meta, x], axis=1)
    cos, sin = rope_tables(T)
    for l in range(DEPTH):
        h = h + hybrid_mixer(rmsnorm(h, attn_norm_g[l]), cos, sin, w_in[l], w_attn_out[l],
                             conv_dw_w[l], conv_dw_b[l], conv_ln_g[l], conv_ln_b[l],
                             w_conv_out[l], w_o[l])
        h = h + conv_ffn(rmsnorm(h, ffn_norm_g[l]), w_up[l], ffn_dw_w[l], ffn_dw_b[l], w_down[l])
    h = rmsnorm(h, final_norm_g)
    return h[:, N_META:]
```

```python
from contextlib import ExitStack
import numpy as np
import concourse.bass as bass
import concourse.mybir as mybir
from concourse.bass_utils import run_bass_kernel_spmd

F32 = mybir.dt.float32
BF16 = mybir.dt.bfloat16
U8 = mybir.dt.uint8
AF = mybir.ActivationFunctionType
ALU = mybir.AluOpType
AX = mybir.AxisListType

EPOCH = 4096
NDMA_SLOTS = 12


class Buf:
    __slots__ = ("name", "w", "r")

    def __init__(self, name):
        self.name = name
        self.w = {}
        self.r = {}


def _merge(d, s):
    for k, v in s.items():
        if d.get(k, 0) < v:
            d[k] = v


class Prog:
    ENGS = ("pe", "act", "dve", "pool", "sp")

    def __init__(self, nc):
        self.nc = nc
        self.q = {e: [] for e in self.ENGS}
        self.count = {e: 0 for e in self.ENGS}
        self.known = {e: {} for e in self.ENGS}
        self.dma_next = {"sp": 0, "pool": 0}
        self.dma_uses = {}
        self.semkeys = set()

    def alias(self, new_bufs, old_bufs):
        d = {}
        for b in old_bufs:
            _merge(d, b.w)
            _merge(d, b.r)
        for b in new_bufs:
            _merge(b.w, d)

    def _add(self, eng, fn, reads, writes, tok, inc, extra_waits=()):
        deps = {}
        for b in reads:
            _merge(deps, b.w)
        for b in writes:
            _merge(deps, b.w)
            _merge(deps, b.r)
        for k, v in extra_waits:
            if deps.get(k, 0) < v:
                deps[k] = v
        waits = []
        kn = self.known[eng]
        for k, v in deps.items():
            if eng == "pe" and k[0] == "pe":
                continue
            if kn.get(k, 0) >= v:
                continue
            kn[k] = v
            waits.append((k, v))
        semkey, val = tok
        self.semkeys.add(semkey)
        for b in writes:
            b.w = {semkey: val}
            b.r = {}
        for b in reads:
            if b not in writes:
                if b.r.get(semkey, 0) < val:
                    b.r[semkey] = val
        self.q[eng].append((waits, fn, semkey, inc))

    def op(self, eng, fn, reads=(), writes=()):
        idx = self.count[eng]
        self.count[eng] += 1
        tok = ((eng, idx // EPOCH), idx % EPOCH + 1)
        self._add(eng, fn, reads, writes, tok, 1)

    def dma(self, queue, fn, reads=(), writes=()):
        slot = self.dma_next[queue]
        self.dma_next[queue] = (slot + 1) % NDMA_SLOTS
        semkey = ("dma" + queue, slot)
        prev = self.dma_uses.get(semkey, 0)
        val = prev + 16
        self.dma_uses[semkey] = val
        extra = [(semkey, prev)] if prev > 0 else []
        self._add(queue, fn, reads, writes, (semkey, val), 16, extra)

    def emit(self, final_bufs=()):
        nc = self.nc
        deps = {}
        for b in final_bufs:
            _merge(deps, b.w)
        fw = list(deps.items())
        with ExitStack() as es:
            sems = {}
            for k in sorted(self.semkeys, key=str):
                nm = "s_" + "_".join(str(x) for x in k)
                sems[k] = es.enter_context(nc.semaphore(nm))
            block = es.enter_context(nc.Block())
            q = self.q

            def body_for(engname):
                def body(e):
                    for waits, fn, semkey, inc in q[engname]:
                        for (k, v) in waits:
                            e.wait_ge(sems[k], v)
                        inst = fn(e)
                        inst.then_inc(sems[semkey], inc)
                    if engname == "sp":
                        for (k, v) in fw:
                            e.wait_ge(sems[k], v)
                return body

            block.tensor(body_for("pe"))
            block.scalar(body_for("act"))
            block.vector(body_for("dve"))
            block.gpsimd(body_for("pool"))
            block.sync(body_for("sp"))


D = 1024
NCOL = 6216
FH = 2816
PADN = 496
NMETA = 16
C_Q, C_K, C_V, C_QI, C_KI, C_WI, C_A, C_B, C_GA, C_GC = 0, 1024, 1280, 1536, 2048, 2112, 2120, 3144, 4168, 5192
NBIS = 16
NEG = -1.0e30

V_ANG = 0
V_FNG = 16
V_CDB = 32
V_LNG = 48
V_LNB = 64
V_CDW = 80
V_FDW = 576
V_FDB = 840
V_RM0 = 928
V_BIS = 932
NV = 960


def build(NS, NL=2, dbg=None):
    TP = NS * 512
    KSEL = min(256, (NMETA + TP - 512) // 4)
    NT = NS * 4
    nc = bass.Bass("TRN2", target_bir_lowering=False)
    P = Prog(nc)

    def din(name, shape, dt=F32):
        return nc.dram_tensor(name, list(shape), dt, kind="ExternalInput").ap()

    def dint(name, shape, dt=F32):
        return nc.dram_tensor(name, list(shape), dt, kind="Internal").ap()

    xp = din("xp", [TP, D])
    cs_d = din("cs", [TP, 16])
    vec_d = din("vec", [128, NV])
    caus_d = din("caus", [128, 128])
    ident_d = din("ident", [128, 128])
    fing_d = din("fing", [128, D])
    w_in = din("w_in", [2, D, NCOL])
    w_ao = din("w_attn_out", [2, D, D])
    w_co = din("w_conv_out", [2, D, D])
    w_o = din("w_o", [2, D, D])
    w_up = din("w_up", [2, D, 2 * FH])
    w_dn = din("w_down", [2, FH, D])
    out_d = nc.dram_tensor("out", [TP - 512, D], F32, kind="ExternalOutput").ap()

    b_in = dint("b_in", [2, D, NCOL], BF16)
    b_ao = dint("b_ao", [2, D, D], BF16)
    b_co = dint("b_co", [2, D, D], BF16)
    b_o = dint("b_o", [2, D, D], BF16)
    b_up = dint("b_up", [2, D, 2 * FH], BF16)
    b_dn = dint("b_dn", [2, FH, D], BF16)
    hA = dint("hA", [TP, D])
    ga_d = dint("ga_s", [8, 128, 512])
    mc_d = dint("mc_s", [8, 128, 512])

    dbg_out = {}
    if dbg:
        for name, shape in dbg.items():
            dbg_out[name] = nc.dram_tensor("dbg_" + name, list(shape), F32, kind="ExternalOutput").ap()

    B_w = {k: Buf("w_" + k) for k in ["in", "ao", "co", "o", "up", "dn"]}
    B_hA = [Buf("hA%d" % s) for s in range(NS)]
    B_gad = [Buf("gad%d" % i) for i in range(8)]
    B_mcd = [Buf("mcd%d" % i) for i in range(8)]
    B_out = Buf("out")
    B_dbg = Buf("dbg")
    B_ext = Buf("ext")

    es = ExitStack()
    with es:
        def sb(name, shape, dt):
            return es.enter_context(nc.sbuf_tensor(name, list(shape), dt))[:]

        KT = sb("KT", [128, 2, TP], BF16)
        VV = sb("VV", [128, NT, 4, 65], BF16)
        KIT = sb("KIT", [128, TP], BF16)
        VEC = sb("VEC", [128, NV], F32)
        IDF = sb("IDF", [128, 128], F32)
        IDB = sb("IDB", [128, 128], BF16)
        CAUS = sb("CAUS", [128, 128], F32)
        ONESF = sb("ONESF", [128, 128], F32)
        ONES1 = sb("ONES1", [128, 128], F32)
        XNT = sb("XNT", [128, 8, 512], BF16)
        QT = sb("QT", [128, 8, 512], BF16)
        QIT = sb("QIT", [128, 4, 512], BF16)
        WI = sb("WI", [128, 4, 8], F32)
        UH = sb("UH", [128, 8, 30], BF16)
        FHL = sb("FHL", [128, 44, 2], F32)
        NWB = 4
        WBs = [sb("WB%d" % i, [128, 8, 256], BF16) for i in range(NWB)]
        ARW = 14592
        ARENA = sb("ARENA", [128, ARW], F32)
        PS = es.enter_context(nc.psum_tensor("PS", [128, 4096], F32))[:]

        bKT, bVV, bKIT, bVEC, bID, bXNT, bUH, bFHL = [Buf(n) for n in "KT VV KIT VEC ID XNT UH FHL".split()]
        bQT = [Buf("QT%d" % i) for i in range(4)]
        bQIT = [Buf("QIT%d" % i) for i in range(4)]
        bWI = [Buf("WI%d" % i) for i in range(4)]
        bXNTt = [Buf("XNT%d" % i) for i in range(4)]
        bWB = [Buf("WB%d" % i) for i in range(NWB)]
        bPS = [Buf("PS%d" % i) for i in range(8)]
        wb_rr = [0]

        def bank(i, w=512, off=0):
            return PS[:, i * 512 + off:i * 512 + off + w]

        def bank_bf(i):
            return PS[:, i * 512:(i + 1) * 512].bitcast(BF16)

        def mm(out, lhsT, rhs, st, sp_, R, W):
            P.op("pe", lambda e: e.matmul(out, lhsT=lhsT, rhs=rhs, start=st, stop=sp_), R, W)

        def tr(out, in_, ident, R, W):
            P.op("pe", lambda e: e.transpose(out=out, in_=in_, identity=ident), R, W)

        def act(out, in_, func, R, W, bias=None, scale=None, accum=None):
            kw = {}
            if bias is not None:
                kw["bias"] = bias
            if scale is not None:
                kw["scale"] = scale
            if accum is not None:
                kw["accum_out"] = accum
            P.op("act", lambda e: e.activation(out=out, in_=in_, func=func, **kw), R, W)

        def ts(out, in0, s1, s2, op0, op1, R, W, accum=None, eng="dve"):
            if op1 is None:
                P.op(eng, lambda e: e.tensor_scalar(out=out, in0=in0, scalar1=s1, scalar2=None, op0=op0), R, W)
            elif accum is None:
                P.op(eng, lambda e: e.tensor_scalar(out=out, in0=in0, scalar1=s1, scalar2=s2, op0=op0, op1=op1), R, W)
            else:
                P.op(eng, lambda e: e.tensor_scalar(out=out, in0=in0, scalar1=s1, scalar2=s2, op0=op0, op1=op1, accum_out=accum), R, W)

        def tt(out, in0, in1, op, R, W, eng="dve"):
            P.op(eng, lambda e: e.tensor_tensor(out=out, in0=in0, in1=in1, op=op), R, W)

        def stt(out, in0, scalar, in1, op0, op1, R, W):
            P.op("dve", lambda e: e.scalar_tensor_tensor(out=out, in0=in0, scalar=scalar, in1=in1, op0=op0, op1=op1), R, W)

        def cp(out, in_, R, W, eng="dve"):
            if eng == "act":
                P.op(eng, lambda e: e.activation(out=out, in_=in_, func=AF.Copy), R, W)
            else:
                P.op(eng, lambda e: e.tensor_copy(out=out, in_=in_), R, W)

        def recip(out, in_, R, W):
            P.op("dve", lambda e: e.reciprocal(out=out, in_=in_), R, W)

        def memset(ap, val, W, eng="dve"):
            P.op(eng, lambda e: e.memset(ap, val), [], W)

        def dma(out, in_, R, W, queue="sp", maxlast=None):
            if maxlast is None:
                P.dma(queue, lambda e: e.dma_start(out=out, in_=in_), R, W)
            else:
                P.dma(queue, lambda e: e.dma_start(out=out, in_=in_, max_dma_last_dim=maxlast), R, W)

        def load_w(src_ap, shape_view, R):
            i = wb_rr[0]
            wb_rr[0] = (i + 1) % NWB
            v = shape_view(WBs[i])
            dma(v, src_ap, R, [bWB[i]])
            return v, bWB[i]

        def dump(name, ap_sb, R, rows=None):
            if name in dbg_out:
                dst = dbg_out[name]
                dma(dst, ap_sb, R, [B_dbg])

        def carve(off, nwords):
            assert off + nwords <= ARW, (off, nwords, ARW)
            return ARENA[:, off:off + nwords]

        dma(VEC, vec_d, [B_ext], [bVEC])
        dma(IDF, ident_d, [B_ext], [bID])
        dma(CAUS, caus_d, [B_ext], [bID])
        cp(IDB, IDF, [bID], [bID])
        memset(ONESF, 1.0 / 1024.0, [bID])
        memset(ONES1, 1.0, [bID])
        memset(VV, 1.0, [bVV])
        for (src, dst, key, rows) in [(w_in, b_in, "in", D), (w_ao, b_ao, "ao", D), (w_co, b_co, "co", D),
                                      (w_o, b_o, "o", D), (w_up, b_up, "up", D), (w_dn, b_dn, "dn", FH)]:
            for l in range(NL):
                for r0 in range(0, rows, 128):
                    dma(dst[l, r0:r0 + 128, :], src[l, r0:r0 + 128, :], [B_ext], [B_w[key]], queue="pool", maxlast=4096)

        o = 0
        HT = carve(o, 1024); o += 1024
        XS = carve(o, 512).bitcast(BF16); o += 512
        TM = [carve(o, 512), carve(o + 512, 512)]; o += 1024
        TMB = carve(o, 256).bitcast(BF16); o += 256
        CS = carve(o, 16); o += 16
        RT = carve(o, 384).rearrange("p (a b) -> p a b", b=128); o += 384
        SSv = carve(o, 8); o += 8
        UT = carve(o, 8 * 272).bitcast(BF16).rearrange("p (a b) -> p a b", b=544); o += 8 * 272
        CV = carve(o, 4096).rearrange("p (a b) -> p a b", b=512); o += 4096
        SGP = [carve(o, 512), carve(o + 512, 512)]; o += 1024
        MEAN = carve(o, 512); o += 512
        RSD = carve(o, 512); o += 512
        SQ = [carve(o, 512), carve(o + 512, 512)]; o += 1024
        DGC = carve(o, 31 * 64).bitcast(BF16).rearrange("p (a b) -> p a b", b=128); o += 31 * 64
        MCT = SQ
        P_END = o
        CVN = UT
        bHT, bXS, bTMB, bCS, bRT, bSS, bMEAN, bRSD, bDGC = [Buf(n) for n in "HT XS TMB CS RT SS MEAN RSD DGC".split()]
        bTM = [Buf("TM0"), Buf("TM1")]
        bUT = [Buf("UT%d" % i) for i in range(8)]
        bCV = [Buf("CV%d" % i) for i in range(8)]
        bSGP = [Buf("SGP0"), Buf("SGP1")]
        bSQ = [Buf("SQ0"), Buf("SQ1")]
        bMCT = bSQ
        P_BUFS = [bHT, bXS, bTMB, bCS, bRT, bSS, bMEAN, bRSD, bDGC] + bTM + bUT + bCV + bSGP + bSQ
        o = 0
        SC = carve(o, TP); o += TP
        RR_ = carve(o, 2048).bitcast(BF16).rearrange("p (a b) -> p a b", b=512); o += 2048
        JUNK = carve(o - 2048, (TP + 3) // 4).bitcast(U8)
        if (TP + 3) // 4 > 2048:
            o = o - 2048 + (TP + 3) // 4
        PT = [carve(o + i * 256, 256).bitcast(BF16) for i in range(3)]; o += 768
        MT = [carve(o + i * 256, 256).bitcast(BF16).rearrange("p (a b) -> p a b", b=128) for i in range(2)]; o += 512
        DG = carve(o, 512).bitcast(BF16).rearrange("p (a b) -> p a b", b=128); o += 512
        THB = carve(o, 128); o += 128
        DLO = carve(o, 128); o += 128
        AT4 = carve(o, 1024).bitcast(BF16).rearrange("p (a b) -> p a b", b=128); o += 1024
        RRW = carve(o, 512); o += 512
        RBS = RRW
        SM = carve(o, 64); o += 64
        A_END = o
        bSC = Buf("SC")
        bRR = [Buf("RR%d" % i) for i in range(4)]
        bPT = [Buf("PT%d" % i) for i in range(3)]
        bMT = [Buf("MT0"), Buf("MT1")]
        bDG, bTHB, bAT4, bRRW, bRBS, bSM = [Buf(n) for n in "DG THB AT4 RRW RBS SM".split()]
        A_BUFS = [bSC] + bRR + bPT + bMT + [bDG, bTHB, bAT4, bRRW, bRBS, bSM]
        o = 0
        HM = carve(o, 4096).rearrange("p (a b) -> p a b", b=1024); o += 4096
        HR = [carve(o, 512), carve(o + 512, 512)]; o += 1024
        GAT = [carve(o, 512), carve(o + 512, 512)]; o += 1024
        MCL = [carve(o, 512), carve(o + 512, 512)]; o += 1024
        TMPF = carve(o, 512); o += 512
        XS2 = carve(o, 512).bitcast(BF16); o += 512
        RAW = [carve(o, 520), carve(o + 520, 520)]; o += 1040
        CGU = [carve(o, 512), carve(o + 512, 512)]; o += 1024
        SGF = carve(o, 512); o += 512
        ACTT = carve(o, 2048).bitcast(BF16).rearrange("p (a b) -> p a b", b=512); o += 2048
        SS2 = carve(o, 8); o += 8
        FING = carve(o, 1024); o += 1024
        F_END = o
        bHM = [Buf("HM%d" % i) for i in range(4)]
        bHR = [Buf("HR0"), Buf("HR1")]
        bGAT = [Buf("GAT0"), Buf("GAT1")]
        bMCL = [Buf("MCL0"), Buf("MCL1")]
        bRAW = [Buf("RAW0"), Buf("RAW1")]
        bCGU = [Buf("CGU0"), Buf("CGU1")]
        bTMPF, bXS2, bSGF, bACTT, bSS2, bFING = [Buf(n) for n in "TMPF XS2 SGF ACTT SS2 FING".split()]
        F_BUFS = bHM + bHR + bGAT + bMCL + bRAW + bCGU + [bTMPF, bXS2, bSGF, bACTT, bSS2, bFING]
        assert max(P_END, A_END, F_END) <= ARW, (P_END, A_END, F_END)

        def vcol(off, n=1):
            return VEC[:, off:off + n]


        def rmsnorm_to_T(src_tile, bsrc, xs, bxs, ssv, bss, g_off, dst_T, bdst, tt_i, psb):
            act(xs, src_tile, AF.Square, [bsrc], [bxs, bss], accum=ssv[:, 0:1])
            act(ssv[:, 1:2], ssv[:, 0:1], AF.Sqrt, [bss], [bss], bias=1e-6, scale=1.0 / 1024.0)
            recip(ssv[:, 2:3], ssv[:, 1:2], [bss], [bss])
            ts(xs, src_tile, ssv[:, 2:3], None, ALU.mult, None, [bsrc, bss], [bxs])
            pv = bank_bf(psb).rearrange("p (a b) -> p a b", b=128)
            for kc in range(8):
                tr(pv[:, kc, :], xs[:, kc * 128:(kc + 1) * 128], IDB, [bxs, bID], [bPS[psb]])
            gb = VEC[:, g_off:g_off + 8].unsqueeze(2).to_broadcast([128, 8, 128])
            tt(dst_T[:, :, tt_i * 128:(tt_i + 1) * 128], pv, gb, ALU.mult, [bPS[psb], bVEC], [bdst])

        def rope(v, nh, bv, cs):
            x1 = v[:, :, 0:8]
            x2 = v[:, :, 8:16]
            c = cs[:, 0:8].unsqueeze(1).to_broadcast([128, nh, 8])
            s = cs[:, 8:16].unsqueeze(1).to_broadcast([128, nh, 8])
            ta = RT[:, 0, 0:nh * 8].rearrange("p (a b) -> p a b", b=8)
            tb = RT[:, 1, 0:nh * 8].rearrange("p (a b) -> p a b", b=8)
            tc_ = RT[:, 2, 0:nh * 8].rearrange("p (a b) -> p a b", b=8)
            tt(ta, x1, c, ALU.mult, [bv, bCS], [bRT])
            tt(tb, x2, s, ALU.mult, [bv, bCS], [bRT])
            tt(tc_, x1, s, ALU.mult, [bv, bCS], [bRT])
            tt(x1, ta, tb, ALU.subtract, [bRT], [bv])
            tt(x2, x2, c, ALU.mult, [bv, bCS], [bv])
            tt(x2, x2, tc_, ALU.add, [bv, bRT], [bv])

        for l in range(NL):
            h_in = xp if l == 0 else hA
            B_hin = [B_ext] * NS if l == 0 else B_hA
            last = (l == NL - 1)
            memset(UH, 0.0, [bUH])
            memset(FHL, 0.0, [bFHL])
            wl_in = b_in[l].rearrange("(kc p) n -> p kc n", p=128)
            wl_co = b_co[l].rearrange("(kc p) n -> p kc n", p=128)
            wl_o = b_o[l].rearrange("(kc p) n -> p kc n", p=128)
            wl_up = b_up[l].rearrange("(kc p) n -> p kc n", p=128)
            wl_dn = b_dn[l].rearrange("(kc p) n -> p kc n", p=128)
            wl_ao = b_ao[l].rearrange("(h d) n -> d h n", d=64)

            for s in range(NS):
                p0 = s * 512
                P.alias(P_BUFS, F_BUFS + A_BUFS)
                for t4 in range(4):
                    dma(HT, h_in[p0 + t4 * 128:p0 + (t4 + 1) * 128, :], [B_hin[s]], [bHT])
                    rmsnorm_to_T(HT, bHT, XS, bXS, SSv, bSS, V_ANG + l * 8, XNT, bXNTt[t4], t4, 6 + (t4 % 2))
                chunks = [("q", C_Q, 512), ("q", C_Q + 512, 512), ("kv", C_K, 512), ("qi", C_QI, 512), ("kw", C_KI, 72)]
                for ci, (kind, c0, wd) in enumerate(chunks):
                    halves = [(0, min(256, wd))] + ([(256, wd - 256)] if wd > 256 else [])
                    for (ho, hw) in halves:
                        wv, wbuf = load_w(wl_in[:, :, c0 + ho:c0 + ho + hw], lambda t, hw=hw: t[:, :, 0:hw], [B_w["in"]])
                        for t4 in range(4):
                            for kc in range(8):
                                mm(bank(t4, hw, ho), XNT[:, kc, t4 * 128:(t4 + 1) * 128], wv[:, kc, :], kc == 0, kc == 7,
                                   [bXNTt[t4], wbuf], [bPS[t4]])
                    for t4 in range(4):
                        pos = p0 + t4 * 128
                        tile_i = s * 4 + t4
                        tm = TM[t4 % 2]
                        btm = bTM[t4 % 2]
                        psb = 4 + (t4 % 2)
                        pvT = bank_bf(psb).rearrange("p (a b) -> p a b", b=128)
                        if kind == "q":
                            src = bank(t4).rearrange("p (a b d) -> p a b d", a=2, b=4)
                            dstv = tm.rearrange("p (b a d) -> p a b d", a=2, b=4)
                            act(dstv, src, AF.Copy, [bPS[t4]], [btm], scale=0.125)
                            dma(CS, cs_d[pos:pos + 128, :], [B_ext], [bCS])
                            rope(tm.rearrange("p (h d) -> p h d", d=64), 8, btm, CS)
                            cp(TMB, tm, [btm], [bTMB])
                            for j in range(4):
                                tr(pvT[:, j, :], TMB[:, j * 128:(j + 1) * 128], IDB, [bTMB, bID], [bPS[psb]])
                            blk0 = (c0 // 512) * 4
                            cp(QT[:, blk0:blk0 + 4, t4 * 128:(t4 + 1) * 128], pvT[:, 0:4, :], [bPS[psb]], [bQT[t4]], eng="act")
                        elif kind == "kv":
                            act(tm[:, 0:256], bank(t4, 256, 0), AF.Copy, [bPS[t4]], [btm])
                            act(VV[:, tile_i, :, 0:64], bank(t4, 256, 256).rearrange("p (g d) -> p g d", d=64), AF.Copy,
                                [bPS[t4]], [bVV])
                            dma(CS, cs_d[pos:pos + 128, :], [B_ext], [bCS])
                            rope(tm[:, 0:256].rearrange("p (h d) -> p h d", d=64), 4, btm, CS)
                            cp(TMB[:, 0:256], tm[:, 0:256], [btm], [bTMB])
                            for j in range(2):
                                tr(pvT[:, j, :], TMB[:, j * 128:(j + 1) * 128], IDB, [bTMB, bID], [bPS[psb]])
                            cp(KT[:, :, pos:pos + 128], pvT[:, 0:2, :], [bPS[psb]], [bKT], eng="act")
                        elif kind == "qi":
                            act(tm, bank(t4), AF.Copy, [bPS[t4]], [btm])
                            dma(CS, cs_d[pos:pos + 128, :], [B_ext], [bCS])
                            rope(tm.rearrange("p (h d) -> p h d", d=64), 8, btm, CS)
                            cp(TMB, tm, [btm], [bTMB])
                            for j in range(4):
                                tr(pvT[:, j, :], TMB[:, j * 128:(j + 1) * 128], IDB, [bTMB, bID], [bPS[psb]])
                            cp(QIT[:, :, t4 * 128:(t4 + 1) * 128], pvT[:, 0:4, :], [bPS[psb]], [bQIT[t4]], eng="act")
                        else:
                            act(tm[:, 0:64], bank(t4, 64, 0), AF.Copy, [bPS[t4]], [btm])
                            act(tm[:, 64:128], bank(t4, 64, 0), AF.Copy, [bPS[t4]], [btm])
                            act(WI[:, t4, :], bank(t4, 8, 64), AF.Copy, [bPS[t4]], [bWI[t4]], scale=0.125 * (8.0 ** -0.5))
                            dma(CS, cs_d[pos:pos + 128, :], [B_ext], [bCS])
                            rope(tm[:, 0:128].rearrange("p (h d) -> p h d", d=64), 2, btm, CS)
                            cp(TMB[:, 0:128], tm[:, 0:128], [btm], [bTMB])
                            tr(pvT[:, 0, :], TMB[:, 0:128], IDB, [bTMB, bID], [bPS[psb]])
                            cp(KIT[:, pos:pos + 128], pvT[:, 0, :], [bPS[psb]], [bKIT], eng="act")
                if dbg and s == dbg.get("_s", 0) and l == dbg.get("_l", 0):
                    pass
                cp(UT[:, :, 0:30], UH, [bUH], bUT)
                for j2 in range(4):
                    wa, ba_ = load_w(wl_in[:, :, C_A + j2 * 256:C_A + (j2 + 1) * 256], lambda t: t, [B_w["in"]])
                    wbv, bb_ = load_w(wl_in[:, :, C_B + j2 * 256:C_B + (j2 + 1) * 256], lambda t: t, [B_w["in"]])
                    for j in range(2):
                        oc = j2 * 2 + j
                        pa, pb = (0, 1) if oc % 2 == 0 else (2, 3)
                        for kc in range(8):
                            mm(bank(pa), wa[:, kc, j * 128:(j + 1) * 128], XNT[:, kc, :], kc == 0, kc == 7, [ba_] + bXNTt, [bPS[pa]])
                        for kc in range(8):
                            mm(bank(pb), wbv[:, kc, j * 128:(j + 1) * 128], XNT[:, kc, :], kc == 0, kc == 7, [bb_] + bXNTt, [bPS[pb]])
                        sg = SGP[oc % 2]
                        act(sg, bank(pb), AF.Sigmoid, [bPS[pb]], [bSGP[oc % 2]])
                        tt(UT[:, oc, 30:542], bank(pa), sg, ALU.mult, [bPS[pa], bSGP[oc % 2]], [bUT[oc]])
                cp(UH, UT[:, :, 512:542], bUT, [bUH])
                for j2 in range(4):
                    wg, bg_ = load_w(wl_in[:, :, C_GA + j2 * 256:C_GA + (j2 + 1) * 256], lambda t: t, [B_w["in"]])
                    for j in range(2):
                        oc = j2 * 2 + j
                        pg = 4 + (oc % 2)
                        for kc in range(8):
                            mm(bank(pg), wg[:, kc, j * 128:(j + 1) * 128], XNT[:, kc, :], kc == 0, kc == 7, [bg_] + bXNTt, [bPS[pg]])
                        sg = SGP[oc % 2]
                        act(sg, bank(pg), AF.Sigmoid, [bPS[pg]], [bSGP[oc % 2]])
                        dma(ga_d[oc], sg, [bSGP[oc % 2]], [B_gad[oc]])
                for oc in range(8):
                    for k in range(31):
                        ts(DGC[:, k, :], IDB, vcol(V_CDW + (l * 8 + oc) * 31 + k), None, ALU.mult, None, [bID, bVEC], [bDGC])
                    pc = 6 + (oc % 2)
                    for k in range(31):
                        mm(bank(pc), DGC[:, k, :], UT[:, oc, k:k + 512], k == 0, k == 30, [bDGC, bUT[oc]], [bPS[pc]])
                    act(CV[:, oc, :], bank(pc), AF.Identity, [bPS[pc], bVEC], [bCV[oc]], bias=vcol(V_CDB + l * 8 + oc))
                for oc in range(8):
                    mm(bank(0), ONESF, CV[:, oc, :], oc == 0, oc == 7, [bID, bCV[oc]], [bPS[0]])
                for oc in range(8):
                    sq = SQ[oc % 2]
                    act(sq, CV[:, oc, :], AF.Square, [bCV[oc]], [bSQ[oc % 2]])
                    mm(bank(1), ONESF, sq, oc == 0, oc == 7, [bID, bSQ[oc % 2]], [bPS[1]])
                act(MEAN, bank(0), AF.Copy, [bPS[0]], [bMEAN])
                act(SQ[0], bank(0), AF.Square, [bPS[0]], [bSQ[0]])
                tt(RSD, bank(1), SQ[0], ALU.subtract, [bPS[1], bSQ[0]], [bRSD])
                act(RSD, RSD, AF.Sqrt, [bRSD], [bRSD], bias=1e-5, scale=1.0)
                recip(RSD, RSD, [bRSD], [bRSD])
                for oc in range(8):
                    tt(CV[:, oc, :], CV[:, oc, :], MEAN, ALU.subtract, [bCV[oc], bMEAN], [bCV[oc]])
                    tt(CV[:, oc, :], CV[:, oc, :], RSD, ALU.mult, [bCV[oc], bRSD], [bCV[oc]])
                    act(CVN[:, oc, 0:512], CV[:, oc, :], AF.Silu, [bCV[oc], bVEC], [bUT[oc]],
                        bias=vcol(V_LNB + l * 8 + oc), scale=vcol(V_LNG + l * 8 + oc))
                for j2 in range(4):
                    wy, by_ = load_w(wl_co[:, :, j2 * 256:(j2 + 1) * 256], lambda t: t, [B_w["co"]])
                    wg, bg_ = load_w(wl_in[:, :, C_GC + j2 * 256:C_GC + (j2 + 1) * 256], lambda t: t, [B_w["in"]])
                    for j in range(2):
                        oc = j2 * 2 + j
                        py, pg = (0, 1) if oc % 2 == 0 else (2, 3)
                        for kc in range(8):
                            mm(bank(py), wy[:, kc, j * 128:(j + 1) * 128], CVN[:, kc, 0:512], kc == 0, kc == 7, [by_] + bUT, [bPS[py]])
                        for kc in range(8):
                            mm(bank(pg), wg[:, kc, j * 128:(j + 1) * 128], XNT[:, kc, :], kc == 0, kc == 7, [bg_] + bXNTt, [bPS[pg]])
                        sg = SGP[oc % 2]
                        act(sg, bank(pg), AF.Sigmoid, [bPS[pg]], [bSGP[oc % 2]])
                        mt_ = MCT[oc % 2]
                        tt(mt_, bank(py), sg, ALU.mult, [bPS[py], bSGP[oc % 2]], [bMCT[oc % 2]])
                        dma(mc_d[oc], mt_, [bMCT[oc % 2]], [B_mcd[oc]])

                P.alias(A_BUFS, P_BUFS)
                bYAT = bXNTt
                for qb in range(4):
                    ti = s * 4 + qb
                    L = (ti + 1) * 128
                    nch = (L + 511) // 512
                    for h in range(8):
                        ts(DG[:, h, :], IDB, WI[:, qb, h:h + 1], None, ALU.mult, None, [bID, bWI[qb]], [bDG])
                    for c in range(nch):
                        k0 = c * 512
                        w = min(512, L - k0)
                        pacc = 6 + (c % 2)
                        for hp in range(4):
                            pbk = [(0, 1), (2, 3), (4, 5)][(c * 4 + hp) % 3]
                            for j in range(2):
                                h = hp * 2 + j
                                hf = (h % 2) * 64
                                mm(bank(pbk[j], w), QIT[hf:hf + 64, h // 2, qb * 128:(qb + 1) * 128], KIT[hf:hf + 64, k0:k0 + w],
                                   True, True, [bQIT[qb], bKIT], [bPS[pbk[j]]])
                            src = PS[:, pbk[0] * 512:pbk[0] * 512 + 1024].rearrange("p (a b) -> p a b", b=512)[:, :, 0:w]
                            act(RR_[:, hp * 2:hp * 2 + 2, 0:w], src, AF.Relu, [bPS[pbk[0]], bPS[pbk[1]]], [bRR[hp]])
                        for h in range(8):
                            mm(bank(pacc, w), DG[:, h, :], RR_[:, h, 0:w], h == 0, h == 7, [bDG, bRR[h // 2]], [bPS[pacc]])
                        cp(SC[:, k0:k0 + w], bank(pacc, w), [bPS[pacc]], [bSC])
                    P.op("dve", lambda e, L=L: e.tensor_reduce(out=SM[:, 0:1], in_=SC[:, 0:L], axis=AX.X, op=ALU.max,
                                                               apply_absolute_value=True), [bSC], [bSM])
                    if ti * 128 < PADN + NMETA:
                        pass
                    memset(SC[:, 0:PADN], NEG, [bSC])
                    tt(SC[:, L - 128:L], SC[:, L - 128:L], CAUS, ALU.add, [bSC, bID], [bSC])
                    amax, lo, mid, cnt, ge = SM[:, 0:1], SM[:, 1:2], SM[:, 2:3], SM[:, 3:4], SM[:, 4:5]
                    STP = SM[:, 8:8 + NBIS]
                    ts(lo, amax, -1.0, -1e-6, ALU.mult, ALU.add, [bSM], [bSM])
                    ts(STP, VEC[:, V_BIS:V_BIS + NBIS], amax, 2.0, ALU.mult, ALU.mult, [bSM, bVEC], [bSM])
                    for it in range(NBIS):
                        tt(mid, lo, STP[:, it:it + 1], ALU.add, [bSM], [bSM])
                        ts(JUNK[:, 0:L], SC[:, 0:L], mid, 0.0, ALU.is_ge, ALU.add, [bSC, bSM], bRR + [bSM], accum=cnt)
                        ts(ge, cnt, KSEL - 0.5, None, ALU.is_ge, None, [bSM], [bSM])
                        stt(lo, ge, STP[:, it:it + 1], lo, ALU.mult, ALU.add, [bSM], [bSM])
                    ts(DLO, IDF, lo, None, ALU.mult, None, [bID, bSM], [bTHB])
                    mm(bank(7, 128), ONES1, DLO, True, True, [bID, bTHB], [bPS[7]])
                    cp(THB, bank(7, 128), [bPS[7]], [bTHB])
                    nblk = ti + 1
                    rot = 0
                    for c in range(nch):
                        nb = min(4, nblk - c * 4)
                        pT = bank(7).rearrange("p (a b) -> p a b", b=128)
                        for j in range(nb):
                            kb = c * 4 + j
                            tr(pT[:, j, :], SC[:, kb * 128:(kb + 1) * 128], IDF, [bSC, bID], [bPS[7]])
                        mt = MT[c % 2]
                        tt(mt[:, 0:nb, :], pT[:, 0:nb, :], THB.unsqueeze(1).to_broadcast([128, nb, 128]), ALU.is_ge,
                           [bPS[7], bTHB], [bMT[c % 2]])
                        for j in range(nb):
                            kb = c * 4 + j
                            for g in range(4):
                                e_, gp = g % 2, g // 2
                                psb = 4 + (rot % 3)
                                ptb = rot % 3
                                rot += 1
                                mm(bank(psb), KT[e_ * 64:e_ * 64 + 64, gp, kb * 128:(kb + 1) * 128],
                                   QT[e_ * 64:e_ * 64 + 64, gp * 4:gp * 4 + 4, qb * 128:(qb + 1) * 128],
                                   True, True, [bKT, bQT[qb]], [bPS[psb]])
                                act(PT[ptb], bank(psb), AF.Exp, [bPS[psb]], [bPT[ptb]])
                                pv4 = PT[ptb].rearrange("p (a b) -> p a b", b=128)
                                tt(pv4, pv4, mt[:, j, :].unsqueeze(1).to_broadcast([128, 4, 128]), ALU.mult,
                                   [bPT[ptb], bMT[c % 2]], [bPT[ptb]])
                                mm(PS[0:65, g * 512:(g + 1) * 512], VV[:, kb, g, :], PT[ptb], kb == 0, kb == nblk - 1,
                                   [bVV, bPT[ptb]], [bPS[g]])
                    for g in range(4):
                        ts(RRW[64:65, :], PS[64:65, g * 512:(g + 1) * 512], 1e-30, None, ALU.add, None, [bPS[g]], [bRRW])
                        recip(RRW[64:65, :], RRW[64:65, :], [bRRW], [bRRW])
                        psb = 4 + (g % 2)
                        mm(PS[0:64, psb * 512:(psb + 1) * 512], ONES1[64:65, 0:64], RRW[64:65, :], True, True, [bID, bRRW], [bPS[psb]])
                        act(RBS[0:64, :], PS[0:64, psb * 512:(psb + 1) * 512], AF.Copy, [bPS[psb]], [bRBS])
                        tt(AT4[0:64, g * 4:(g + 1) * 4, :], PS[0:64, g * 512:(g + 1) * 512].rearrange("p (a b) -> p a b", b=128),
                           RBS[0:64, :].rearrange("p (a b) -> p a b", b=128), ALU.mult, [bPS[g], bRBS], [bAT4])
                    for oc in range(8):
                        wv, wbuf = load_w(wl_ao[:, :, oc * 128:(oc + 1) * 128],
                                          lambda t: t.rearrange("p a b -> p (a b)")[0:64, :].rearrange("p (h c) -> p h c", c=128),
                                          [B_w["ao"]])
                        psb = 6 + (oc // 4) % 2
                        po = (oc % 4) * 128
                        for h in range(16):
                            mm(bank(psb, 128, po), wv[:, h, :], AT4[0:64, h, :], h == 0, h == 15, [wbuf, bAT4], [bPS[psb]])
                        if oc % 4 == 3:
                            o4 = oc - 3
                            cp(XNT[:, o4:o4 + 4, qb * 128:(qb + 1) * 128], bank(psb).rearrange("p (a b) -> p a b", b=128),
                               [bPS[psb]], [bYAT[qb]], eng="act")

                P.alias(F_BUFS, A_BUFS)
                for oc in range(8):
                    i2 = oc % 2
                    dma(GAT[i2], ga_d[oc], [B_gad[oc]], [bGAT[i2]])
                    dma(MCL[i2], mc_d[oc], [B_mcd[oc]], [bMCL[i2]])
                    tt(TMPF, XNT[:, oc, :], GAT[i2], ALU.mult, bYAT + [bGAT[i2]], [bTMPF])
                    tt(XNT[:, oc, :], TMPF, MCL[i2], ALU.add, [bTMPF, bMCL[i2]], bYAT)
                for c2 in range(2):
                    for hf in range(2):
                        col = c2 * 512 + hf * 256
                        wv, wbuf = load_w(wl_o[:, :, col:col + 256], lambda t: t, [B_w["o"]])
                        for t4 in range(4):
                            for kc in range(8):
                                mm(bank(t4, 256, hf * 256), XNT[:, kc, t4 * 128:(t4 + 1) * 128], wv[:, kc, :], kc == 0, kc == 7,
                                   [bYAT[t4], wbuf], [bPS[t4]])
                    for t4 in range(4):
                        i2 = t4 % 2
                        dma(HR[i2], h_in[p0 + t4 * 128:p0 + (t4 + 1) * 128, c2 * 512:(c2 + 1) * 512], [B_hin[s]], [bHR[i2]])
                        tt(HM[:, t4, c2 * 512:(c2 + 1) * 512], bank(t4), HR[i2], ALU.add, [bPS[t4], bHR[i2]], [bHM[t4]])
                if s == 0:
                    for t4 in range(4):
                        ts(HM[:, t4, :], HM[:, t4, :], vcol(V_RM0 + t4), None, ALU.mult, None, [bHM[t4], bVEC], [bHM[t4]])
                for t4 in range(4):
                    rmsnorm_to_T(HM[:, t4, :], bHM[t4], XS2, bXS2, SS2, bSS2, V_FNG + l * 8, XNT, bXNTt[t4], t4, 4 + (t4 % 2))
                for (c0g, ng) in [(0, 8), (8, 8), (16, 6)]:
                    for cpair in range(ng // 2):
                        cA = c0g + cpair * 2
                        wg, bwg = load_w(wl_up[:, :, cA * 128:cA * 128 + 256], lambda t: t, [B_w["up"]])
                        wu, bwu = load_w(wl_up[:, :, FH + cA * 128:FH + cA * 128 + 256], lambda t: t, [B_w["up"]])
                        for j in range(2):
                            c = cA + j
                            res = []
                            for (which, wv_, wb_) in ((0, wg, bwg), (1, wu, bwu)):
                                psb = which * 2 + (c % 2)
                                cidx = c + which * 22
                                for kc in range(8):
                                    mm(bank(psb), wv_[:, kc, j * 128:(j + 1) * 128], XNT[:, kc, :], kc == 0, kc == 7,
                                       [wb_] + bXNTt, [bPS[psb]])
                                raw = RAW[which]
                                act(raw[:, 2:514], bank(psb), AF.Copy, [bPS[psb]], [bRAW[which]])
                                cp(raw[:, 0:2], FHL[:, cidx, :], [bFHL], [bRAW[which]])
                                cg = CGU[which]
                                wofs = V_FDW + (l * 44 + cidx) * 3
                                ts(cg, raw[:, 2:514], vcol(wofs + 2), vcol(V_FDB + l * 44 + cidx), ALU.mult, ALU.add,
                                   [bRAW[which], bVEC], [bCGU[which]])
                                stt(cg, raw[:, 1:513], vcol(wofs + 1), cg, ALU.mult, ALU.add, [bRAW[which], bVEC, bCGU[which]], [bCGU[which]])
                                stt(cg, raw[:, 0:512], vcol(wofs + 0), cg, ALU.mult, ALU.add, [bRAW[which], bVEC, bCGU[which]], [bCGU[which]])
                                cp(FHL[:, cidx, :], raw[:, 512:514], [bRAW[which]], [bFHL])
                            act(SGF, CGU[0], AF.Silu, [bCGU[0]], [bSGF])
                            tt(ACTT[:, c - c0g, :], SGF, CGU[1], ALU.mult, [bSGF, bCGU[1]], [bACTT])
                    for c2 in range(2):
                        for hf in range(2):
                            col = c2 * 512 + hf * 256
                            wv, wbuf = load_w(wl_dn[:, c0g:c0g + ng, col:col + 256], lambda t, ng=ng: t[:, 0:ng, :], [B_w["dn"]])
                            for t4 in range(4):
                                psb = 4 + t4
                                for kc in range(ng):
                                    mm(bank(psb, 256, hf * 256), ACTT[:, kc, t4 * 128:(t4 + 1) * 128], wv[:, kc, :], kc == 0, kc == ng - 1,
                                       [bACTT, wbuf], [bPS[psb]])
                        for t4 in range(4):
                            tt(HM[:, t4, c2 * 512:(c2 + 1) * 512], HM[:, t4, c2 * 512:(c2 + 1) * 512], bank(4 + t4), ALU.add,
                               [bPS[4 + t4], bHM[t4]], [bHM[t4]])
                if last:
                    dma(FING, fing_d, [B_ext], [bFING])
                for t4 in range(4):
                    if s == 0:
                        ts(HM[:, t4, :], HM[:, t4, :], vcol(V_RM0 + t4), None, ALU.mult, None, [bHM[t4], bVEC], [bHM[t4]])
                    rows = slice(p0 + t4 * 128, p0 + (t4 + 1) * 128)
                    if not last:
                        dma(hA[rows, :], HM[:, t4, :], [bHM[t4]], [B_hA[s]])
                    elif s >= 1:
                        act(XS2, HM[:, t4, :], AF.Square, [bHM[t4]], [bXS2, bSS2], accum=SS2[:, 0:1])
                        act(SS2[:, 1:2], SS2[:, 0:1], AF.Sqrt, [bSS2], [bSS2], bias=1e-6, scale=1.0 / 1024.0)
                        recip(SS2[:, 2:3], SS2[:, 1:2], [bSS2], [bSS2])
                        ts(HM[:, t4, :], HM[:, t4, :], SS2[:, 2:3], None, ALU.mult, None, [bHM[t4], bSS2], [bHM[t4]])
                        tt(HM[:, t4, :], HM[:, t4, :], FING, ALU.mult, [bHM[t4], bFING], [bHM[t4]])
                        dma(out_d[p0 - 512 + t4 * 128:p0 - 512 + (t4 + 1) * 128, :], HM[:, t4, :], [bHM[t4]], [B_out])
        P.emit(final_bufs=[B_out, B_dbg])
    return nc


def host_consts(NS):
    TP = NS * 512
    pos = (np.arange(TP) - PADN).astype(np.float32)
    inv_freq = np.power(np.float32(500000.0), -np.arange(0, 16, 2, dtype=np.float32) / np.float32(16)).astype(np.float32)
    ang = (pos[:, None] * inv_freq[None, :]).astype(np.float32)
    cs = np.concatenate([np.cos(ang), np.sin(ang)], axis=1).astype(np.float32)
    t = np.arange(128)
    caus = np.where(t[None, :] <= t[:, None], 0.0, NEG).astype(np.float32)
    ident = np.eye(128, dtype=np.float32)
    return cs, caus, ident


def pack_vec(inp):
    vec = np.zeros((128, NV), np.float32)

    def pp(a):
        a = np.asarray(a, np.float32)
        lead = a.shape[:-1]
        n = a.shape[-1] // 128
        return np.moveaxis(a.reshape(*lead, n, 128), -1, 0)

    vec[:, V_ANG:V_ANG + 16] = pp(inp["attn_norm_g"]).reshape(128, 16)
    vec[:, V_FNG:V_FNG + 16] = pp(inp["ffn_norm_g"]).reshape(128, 16)
    vec[:, V_CDB:V_CDB + 16] = pp(inp["conv_dw_b"]).reshape(128, 16)
    vec[:, V_LNG:V_LNG + 16] = pp(inp["conv_ln_g"]).reshape(128, 16)
    vec[:, V_LNB:V_LNB + 16] = pp(inp["conv_ln_b"]).reshape(128, 16)
    cdw = pp(inp["conv_dw_w"])
    vec[:, V_CDW:V_CDW + 496] = np.transpose(cdw, (0, 1, 3, 2)).reshape(128, 496)
    fdw = pp(inp["ffn_dw_w"])
    vec[:, V_FDW:V_FDW + 264] = np.transpose(fdw, (0, 1, 3, 2)).reshape(128, 264)
    vec[:, V_FDB:V_FDB + 88] = pp(inp["ffn_dw_b"]).reshape(128, 88)
    rm = np.ones((128, 4), np.float32)
    p = np.arange(512).reshape(4, 128).T
    rm[p < PADN] = 0.0
    vec[:, V_RM0:V_RM0 + 4] = rm
    vec[:, V_BIS:V_BIS + NBIS] = (2.0 ** -(np.arange(NBIS) + 1.0)).astype(np.float32)[None, :]
    return vec


def make_in_map(inp, b, NS):
    TP = NS * 512
    nreal = TP - 512
    xpad = np.zeros((TP, D), np.float32)
    xpad[PADN:PADN + NMETA] = np.asarray(inp["meta_tokens"], np.float32)
    xpad[512:] = np.asarray(inp["x"][b, :nreal], np.float32)
    cs, caus, ident = host_consts(NS)
    m = {
        "xp": xpad, "cs": cs, "vec": pack_vec(inp), "caus": caus, "ident": ident,
        "fing": np.ascontiguousarray(np.broadcast_to(np.asarray(inp["final_norm_g"], np.float32)[None, :], (128, D))),
        "w_in": np.asarray(inp["w_in"], np.float32), "w_attn_out": np.asarray(inp["w_attn_out"], np.float32),
        "w_conv_out": np.asarray(inp["w_conv_out"], np.float32), "w_o": np.asarray(inp["w_o"], np.float32),
        "w_up": np.asarray(inp["w_up"], np.float32), "w_down": np.asarray(inp["w_down"], np.float32),
    }
    return m


_NC_CACHE = {}


def kernel(**inputs):
    NS = 17
    if NS not in _NC_CACHE:
        _NC_CACHE[NS] = build(NS)
    nc = _NC_CACHE[NS]
    B = inputs["x"].shape[0]
    in_maps = [make_in_map(inputs, (c // 2) % B, NS) for c in range(8)]
    res = run_bass_kernel_spmd(nc, in_maps, core_ids=list(range(8)))
    out = np.stack([res.results[2 * b]["out"] for b in range(B)], axis=0)
    return out.astype(np.float32)
```

```python
from contextlib import ExitStack
import os
import numpy as np
import concourse.bass as bass
import concourse.mybir as mybir
from concourse.bass_utils import run_bass_kernel_spmd

F32 = mybir.dt.float32
BF16 = mybir.dt.bfloat16
U8 = mybir.dt.uint8
AF = mybir.ActivationFunctionType
ALU = mybir.AluOpType
AX = mybir.AxisListType

EPOCH = 4096
NDMA_SLOTS = 12


class Buf:
    __slots__ = ("name", "w", "r")

    def __init__(self, name):
        self.name = name
        self.w = {}
        self.r = {}


def _merge(d, s):
    for k, v in s.items():
        if d.get(k, 0) < v:
            d[k] = v


class Prog:
    ENGS = ("pe", "act", "dve", "pool", "sp")

    def __init__(self, nc):
        self.nc = nc
        self.q = {e: [] for e in self.ENGS}
        self.count = {e: 0 for e in self.ENGS}
        self.known = {e: {} for e in self.ENGS}
        self.dma_next = {"sp": 0, "pool": 0}
        self.dma_uses = {}
        self.semkeys = set()

    def alias(self, new_bufs, old_bufs):
        d = {}
        for b in old_bufs:
            _merge(d, b.w)
            _merge(d, b.r)
        for b in new_bufs:
            _merge(b.w, d)

    def _add(self, eng, fn, reads, writes, tok, inc, extra_waits=()):
        deps = {}
        for b in reads:
            _merge(deps, b.w)
        for b in writes:
            _merge(deps, b.w)
            _merge(deps, b.r)
        for k, v in extra_waits:
            if deps.get(k, 0) < v:
                deps[k] = v
        waits = []
        kn = self.known[eng]
        for k, v in deps.items():
            if eng == "pe" and k[0] == "pe":
                continue
            if kn.get(k, 0) >= v:
                continue
            kn[k] = v
            waits.append((k, v))
        semkey, val = tok
        self.semkeys.add(semkey)
        for b in writes:
            b.w = {semkey: val}
            b.r = {}
        for b in reads:
            if b not in writes:
                if b.r.get(semkey, 0) < val:
                    b.r[semkey] = val
        self.q[eng].append((waits, fn, semkey, inc))

    def op(self, eng, fn, reads=(), writes=()):
        idx = self.count[eng]
        self.count[eng] += 1
        tok = ((eng, idx // EPOCH), idx % EPOCH + 1)
        self._add(eng, fn, reads, writes, tok, 1)

    def dma(self, queue, fn, reads=(), writes=()):
        slot = self.dma_next[queue]
        self.dma_next[queue] = (slot + 1) % NDMA_SLOTS
        semkey = ("dma" + queue, slot)
        prev = self.dma_uses.get(semkey, 0)
        val = prev + 16
        self.dma_uses[semkey] = val
        extra = [(semkey, prev)] if prev > 0 else []
        self._add(queue, fn, reads, writes, (semkey, val), 16, extra)

    def emit(self, final_bufs=()):
        nc = self.nc
        deps = {}
        for b in final_bufs:
            _merge(deps, b.w)
        fw = list(deps.items())
        with ExitStack() as es:
            sems = {}
            for k in sorted(self.semkeys, key=str):
                nm = "s_" + "_".join(str(x) for x in k)
                sems[k] = es.enter_context(nc.semaphore(nm))
            block = es.enter_context(nc.Block())
            q = self.q

            def body_for(engname):
                def body(e):
                    for waits, fn, semkey, inc in q[engname]:
                        for (k, v) in waits:
                            e.wait_ge(sems[k], v)
                        inst = fn(e)
                        inst.then_inc(sems[semkey], inc)
                    if engname == "sp":
                        for (k, v) in fw:
                            e.wait_ge(sems[k], v)
                return body

            block.tensor(body_for("pe"))
            block.scalar(body_for("act"))
            block.vector(body_for("dve"))
            block.gpsimd(body_for("pool"))
            block.sync(body_for("sp"))


D = 1024
NCOL = 6216
FH = 2816
PADN = 496
NMETA = 16
C_Q, C_K, C_V, C_QI, C_KI, C_WI, C_A, C_B, C_GA, C_GC = 0, 1024, 1280, 1536, 2048, 2112, 2120, 3144, 4168, 5192
NBIS = 16
NEG = -1.0e30

V_ANG = 0
V_FNG = 16
V_CDB = 32
V_LNG = 48
V_LNB = 64
V_CDW = 80
V_FDW = 576
V_FDB = 840
V_RM0 = 928
V_BIS = 932
NV = 960


def build(NS, NL=2, dbg=None):
    TP = NS * 512
    KSEL = min(256, (NMETA + TP - 512) // 4)
    NT = NS * 4
    nc = bass.Bass("TRN2", target_bir_lowering=False)
    P = Prog(nc)

    def din(name, shape, dt=F32):
        return nc.dram_tensor(name, list(shape), dt, kind="ExternalInput").ap()

    def dint(name, shape, dt=F32):
        return nc.dram_tensor(name, list(shape), dt, kind="Internal").ap()

    xp = din("xp", [TP, D])
    cs_d = din("cs", [TP, 16])
    vec_d = din("vec", [128, NV])
    caus_d = din("caus", [128, 128])
    ident_d = din("ident", [128, 128])
    fing_d = din("fing", [128, D])
    w_in = din("w_in", [2, D, NCOL])
    w_ao = din("w_attn_out", [2, D, D])
    w_co = din("w_conv_out", [2, D, D])
    w_o = din("w_o", [2, D, D])
    w_up = din("w_up", [2, D, 2 * FH])
    w_dn = din("w_down", [2, FH, D])
    out_d = nc.dram_tensor("out", [TP - 512, D], F32, kind="ExternalOutput").ap()

    b_in = dint("b_in", [2, D, NCOL], BF16)
    b_ao = dint("b_ao", [2, D, D], BF16)
    b_co = dint("b_co", [2, D, D], BF16)
    b_o = dint("b_o", [2, D, D], BF16)
    b_up = dint("b_up", [2, D, 2 * FH], BF16)
    b_dn = dint("b_dn", [2, FH, D], BF16)
    hA = dint("hA", [TP, D])
    ga_d = dint("ga_s", [8, 128, 512])
    mc_d = dint("mc_s", [8, 128, 512])

    dbg_out = {}
    if dbg:
        for name, shape in dbg.items():
            dbg_out[name] = nc.dram_tensor("dbg_" + name, list(shape), F32, kind="ExternalOutput").ap()

    B_w = {k: Buf("w_" + k) for k in ["in", "ao", "co", "o", "up", "dn"]}
    B_hA = [Buf("hA%d" % s) for s in range(NS)]
    B_gad = [Buf("gad%d" % i) for i in range(8)]
    B_mcd = [Buf("mcd%d" % i) for i in range(8)]
    B_out = Buf("out")
    B_dbg = Buf("dbg")
    B_ext = Buf("ext")

    es = ExitStack()
    with es:
        def sb(name, shape, dt):
            return es.enter_context(nc.sbuf_tensor(name, list(shape), dt))[:]

        KT = sb("KT", [128, 2, TP], BF16)
        VV = sb("VV", [128, NT, 4, 65], BF16)
        KIT = sb("KIT", [128, TP], BF16)
        VEC = sb("VEC", [128, NV], F32)
        IDF = sb("IDF", [128, 128], F32)
        IDB = sb("IDB", [128, 128], BF16)
        CAUS = sb("CAUS", [128, 128], F32)
        ONESF = sb("ONESF", [128, 128], F32)
        ONES1 = sb("ONES1", [128, 128], F32)
        XNT = sb("XNT", [128, 8, 512], BF16)
        QTZ = [sb("QTZ%d" % i, [128, 8, 512], BF16) for i in range(2)]
        QITZ = [sb("QITZ%d" % i, [128, 4, 512], BF16) for i in range(2)]
        WI = sb("WI", [128, 4, 8], F32)
        UH = sb("UH", [128, 8, 30], BF16)
        FHL = sb("FHL", [128, 44, 2], F32)
        NWB = 4
        WBs = [sb("WB%d" % i, [128, 8, 256], BF16) for i in range(NWB)]
        ARW = 16640
        ARENA = sb("ARENA", [128, ARW], F32)
        PS = es.enter_context(nc.psum_tensor("PS", [128, 4096], F32))[:]

        bKT, bVV, bKIT, bVEC, bID, bXNT, bUH, bFHL = [Buf(n) for n in "KT VV KIT VEC ID XNT UH FHL".split()]
        bQT = [Buf("QT%d" % i) for i in range(4)]
        bQIT = [Buf("QIT%d" % i) for i in range(4)]
        bWI = [Buf("WI%d" % i) for i in range(4)]
        bXNTt = [Buf("XNT%d" % i) for i in range(4)]
        bWB = [Buf("WB%d" % i) for i in range(NWB)]
        bPS = [Buf("PS%d" % i) for i in range(8)]
        wb_rr = [0]

        def bank(i, w=512, off=0):
            return PS[:, i * 512 + off:i * 512 + off + w]

        def bank_bf(i):
            return PS[:, i * 512:(i + 1) * 512].bitcast(BF16)

        def mm(out, lhsT, rhs, st, sp_, R, W):
            P.op("pe", lambda e: e.matmul(out, lhsT=lhsT, rhs=rhs, start=st, stop=sp_), R, W)

        def tr(out, in_, ident, R, W):
            P.op("pe", lambda e: e.transpose(out=out, in_=in_, identity=ident), R, W)

        def act(out, in_, func, R, W, bias=None, scale=None, accum=None):
            kw = {}
            if bias is not None:
                kw["bias"] = bias
            if scale is not None:
                kw["scale"] = scale
            if accum is not None:
                kw["accum_out"] = accum
            P.op("act", lambda e: e.activation(out=out, in_=in_, func=func, **kw), R, W)

        def ts(out, in0, s1, s2, op0, op1, R, W, accum=None, eng="dve"):
            if op1 is None:
                P.op(eng, lambda e: e.tensor_scalar(out=out, in0=in0, scalar1=s1, scalar2=None, op0=op0), R, W)
            elif accum is None:
                P.op(eng, lambda e: e.tensor_scalar(out=out, in0=in0, scalar1=s1, scalar2=s2, op0=op0, op1=op1), R, W)
            else:
                P.op(eng, lambda e: e.tensor_scalar(out=out, in0=in0, scalar1=s1, scalar2=s2, op0=op0, op1=op1, accum_out=accum), R, W)

        def tt(out, in0, in1, op, R, W, eng="dve"):
            P.op(eng, lambda e: e.tensor_tensor(out=out, in0=in0, in1=in1, op=op), R, W)

        def stt(out, in0, scalar, in1, op0, op1, R, W):
            P.op("dve", lambda e: e.scalar_tensor_tensor(out=out, in0=in0, scalar=scalar, in1=in1, op0=op0, op1=op1), R, W)

        def cp(out, in_, R, W, eng="dve"):
            if eng == "act":
                P.op(eng, lambda e: e.activation(out=out, in_=in_, func=AF.Copy), R, W)
            else:
                P.op(eng, lambda e: e.tensor_copy(out=out, in_=in_), R, W)

        def recip(out, in_, R, W):
            P.op("dve", lambda e: e.reciprocal(out=out, in_=in_), R, W)

        def memset(ap, val, W, eng="dve"):
            P.op(eng, lambda e: e.memset(ap, val), [], W)

        def dma(out, in_, R, W, queue="sp", maxlast=None):
            if maxlast is None:
                P.dma(queue, lambda e: e.dma_start(out=out, in_=in_), R, W)
            else:
                P.dma(queue, lambda e: e.dma_start(out=out, in_=in_, max_dma_last_dim=maxlast), R, W)

        def load_w(src_ap, shape_view, R):
            i = wb_rr[0]
            wb_rr[0] = (i + 1) % NWB
            v = shape_view(WBs[i])
            dma(v, src_ap, R, [bWB[i]])
            return v, bWB[i]

        def dump(name, ap_sb, R, rows=None):
            if name in dbg_out:
                dst = dbg_out[name]
                dma(dst, ap_sb, R, [B_dbg])

        def carve(off, nwords):
            assert off + nwords <= ARW, (off, nwords, ARW)
            return ARENA[:, off:off + nwords]

        dma(VEC, vec_d, [B_ext], [bVEC])
        dma(IDF, ident_d, [B_ext], [bID])
        dma(CAUS, caus_d, [B_ext], [bID])
        cp(IDB, IDF, [bID], [bID])
        memset(ONESF, 1.0 / 1024.0, [bID])
        memset(ONES1, 1.0, [bID])
        for _i in range(2):
            memset(QTZ[_i], 0.0, bQT)
            memset(QITZ[_i], 0.0, bQIT)
        memset(VV, 1.0, [bVV])
        for (src, dst, key, rows) in [(w_in, b_in, "in", D), (w_ao, b_ao, "ao", D), (w_co, b_co, "co", D),
                                      (w_o, b_o, "o", D), (w_up, b_up, "up", D), (w_dn, b_dn, "dn", FH)]:
            for l in range(NL):
                for r0 in range(0, rows, 128):
                    dma(dst[l, r0:r0 + 128, :], src[l, r0:r0 + 128, :], [B_ext], [B_w[key]], queue="pool", maxlast=4096)

        o = 0
        HT = carve(o, 1024); o += 1024
        XS = carve(o, 512).bitcast(BF16); o += 512
        TM = [carve(o, 512), carve(o + 512, 512)]; o += 1024
        TMB = carve(o, 256).bitcast(BF16); o += 256
        CS = carve(o, 16); o += 16
        RT = carve(o, 384).rearrange("p (a b) -> p a b", b=128); o += 384
        SSv = carve(o, 8); o += 8
        UT = carve(o, 8 * 272).bitcast(BF16).rearrange("p (a b) -> p a b", b=544); o += 8 * 272
        CV = carve(o, 4096).rearrange("p (a b) -> p a b", b=512); o += 4096
        SGP = [carve(o, 512), carve(o + 512, 512)]; o += 1024
        MEAN = carve(o, 512); o += 512
        RSD = carve(o, 512); o += 512
        SQ = [carve(o, 512), carve(o + 512, 512)]; o += 1024
        DGCs = []
        for _i in range(2):
            DGCs.append(carve(o, 31 * 64).bitcast(BF16).rearrange("p (a b) -> p a b", b=128)); o += 31 * 64
        MCT = SQ
        P_END = o
        CVN = UT
        bHT, bXS, bTMB, bCS, bRT, bSS, bMEAN, bRSD = [Buf(n) for n in "HT XS TMB CS RT SS MEAN RSD".split()]
        bDGCs = [Buf("DGC0"), Buf("DGC1")]
        bTM = [Buf("TM0"), Buf("TM1")]
        bUT = [Buf("UT%d" % i) for i in range(8)]
        bCV = [Buf("CV%d" % i) for i in range(8)]
        bSGP = [Buf("SGP0"), Buf("SGP1")]
        bSQ = [Buf("SQ0"), Buf("SQ1")]
        bMCT = bSQ
        P_BUFS = [bHT, bXS, bTMB, bCS, bRT, bSS, bMEAN, bRSD] + bDGCs + bTM + bUT + bCV + bSGP + bSQ
        o = 0
        SC = carve(o, TP); o += TP
        RR_ = carve(o, 2048).bitcast(BF16).rearrange("p (a b) -> p a b", b=512); o += 2048
        JUNK = carve(o - 2048, (TP + 3) // 4).bitcast(U8)
        if (TP + 3) // 4 > 2048:
            o = o - 2048 + (TP + 3) // 4
        PT = [carve(o + i * 256, 256).bitcast(BF16) for i in range(3)]; o += 768
        MT = [carve(o + i * 256, 256).bitcast(BF16).rearrange("p (a b) -> p a b", b=128) for i in range(2)]; o += 512
        DG = carve(o, 512).bitcast(BF16).rearrange("p (a b) -> p a b", b=128); o += 512
        THB = carve(o, 128); o += 128
        DLO = carve(o, 128); o += 128
        AT4 = carve(o, 1024).bitcast(BF16).rearrange("p (a b) -> p a b", b=128); o += 1024
        RRW = carve(o, 512); o += 512
        RBS = RRW
        SM = carve(o, 64); o += 64
        A_END = o
        bSC = Buf("SC")
        bRR = [Buf("RR%d" % i) for i in range(4)]
        bPT = [Buf("PT%d" % i) for i in range(3)]
        bMT = [Buf("MT0"), Buf("MT1")]
        bDG, bTHB, bAT4, bRRW, bRBS, bSM = [Buf(n) for n in "DG THB AT4 RRW RBS SM".split()]
        A_BUFS = [bSC] + bRR + bPT + bMT + [bDG, bTHB, bAT4, bRRW, bRBS, bSM]
        o = 0
        HM = carve(o, 4096).rearrange("p (a b) -> p a b", b=1024); o += 4096
        HR = [carve(o, 512), carve(o + 512, 512)]; o += 1024
        GAT = [carve(o, 512), carve(o + 512, 512)]; o += 1024
        MCL = [carve(o, 512), carve(o + 512, 512)]; o += 1024
        TMPF = carve(o, 512); o += 512
        XS2 = carve(o, 512).bitcast(BF16); o += 512
        RAW = [[carve(o + (2 * a + b) * 520, 520) for b in range(2)] for a in range(2)]; o += 2080
        CGU = [carve(o, 512), carve(o + 512, 512)]; o += 1024
        SGF = carve(o, 512); o += 512
        ACTT = carve(o, 2048).bitcast(BF16).rearrange("p (a b) -> p a b", b=512); o += 2048
        SS2 = carve(o, 8); o += 8
        FING = carve(o, 1024); o += 1024
        F_END = o
        bHM = [Buf("HM%d" % i) for i in range(4)]
        bHR = [Buf("HR0"), Buf("HR1")]
        bGAT = [Buf("GAT0"), Buf("GAT1")]
        bMCL = [Buf("MCL0"), Buf("MCL1")]
        bRAW = [[Buf("RAW%d%d" % (a, b)) for b in range(2)] for a in range(2)]
        bCGU = [Buf("CGU0"), Buf("CGU1")]
        bTMPF, bXS2, bSGF, bACTT, bSS2, bFING = [Buf(n) for n in "TMPF XS2 SGF ACTT SS2 FING".split()]
        F_BUFS = bHM + bHR + bGAT + bMCL + bRAW[0] + bRAW[1] + bCGU + [bTMPF, bXS2, bSGF, bACTT, bSS2, bFING]
        assert max(P_END, A_END, F_END) <= ARW, (P_END, A_END, F_END)

        def vcol(off, n=1):
            return VEC[:, off:off + n]


        def rmsnorm_to_T(src_tile, bsrc, xs, bxs, ssv, bss, g_off, dst_T, bdst, tt_i, psb):
            act(xs, src_tile, AF.Square, [bsrc], [bxs, bss], accum=ssv[:, 0:1])
            act(ssv[:, 1:2], ssv[:, 0:1], AF.Sqrt, [bss], [bss], bias=1e-6, scale=1.0 / 1024.0)
            recip(ssv[:, 2:3], ssv[:, 1:2], [bss], [bss])
            ts(xs, src_tile, ssv[:, 2:3], None, ALU.mult, None, [bsrc, bss], [bxs])
            pv = bank_bf(psb).rearrange("p (a b) -> p a b", b=128)
            for kc in range(8):
                tr(pv[:, kc, :], xs[:, kc * 128:(kc + 1) * 128], IDB, [bxs, bID], [bPS[psb]])
            gb = VEC[:, g_off:g_off + 8].unsqueeze(2).to_broadcast([128, 8, 128])
            tt(dst_T[:, :, tt_i * 128:(tt_i + 1) * 128], pv, gb, ALU.mult, [bPS[psb], bVEC], [bdst])

        def rope(v, nh, bv, cs):
            x1 = v[:, :, 0:8]
            x2 = v[:, :, 8:16]
            c = cs[:, 0:8].unsqueeze(1).to_broadcast([128, nh, 8])
            s = cs[:, 8:16].unsqueeze(1).to_broadcast([128, nh, 8])
            ta = RT[:, 0, 0:nh * 8].rearrange("p (a b) -> p a b", b=8)
            tb = RT[:, 1, 0:nh * 8].rearrange("p (a b) -> p a b", b=8)
            tc_ = RT[:, 2, 0:nh * 8].rearrange("p (a b) -> p a b", b=8)
            tt(ta, x1, c, ALU.mult, [bv, bCS], [bRT])
            tt(tb, x2, s, ALU.mult, [bv, bCS], [bRT])
            tt(tc_, x1, s, ALU.mult, [bv, bCS], [bRT])
            tt(x1, ta, tb, ALU.subtract, [bRT], [bv])
            tt(x2, x2, c, ALU.mult, [bv, bCS], [bv])
            tt(x2, x2, tc_, ALU.add, [bv, bRT], [bv])

        for l in range(NL):
            h_in = xp if l == 0 else hA
            B_hin = [B_ext] * NS if l == 0 else B_hA
            last = (l == NL - 1)
            memset(UH, 0.0, [bUH])
            memset(FHL, 0.0, [bFHL])
            wl_in = b_in[l].rearrange("(kc p) n -> p kc n", p=128)
            wl_co = b_co[l].rearrange("(kc p) n -> p kc n", p=128)
            wl_o = b_o[l].rearrange("(kc p) n -> p kc n", p=128)
            wl_up = b_up[l].rearrange("(kc p) n -> p kc n", p=128)
            wl_dn = b_dn[l].rearrange("(kc p) n -> p kc n", p=128)
            wl_ao = b_ao[l].rearrange("(h d) n -> d h n", d=64)

            for s in range(NS):
                p0 = s * 512
                P.alias(P_BUFS, F_BUFS + A_BUFS)
                for t4 in range(4):
                    dma(HT, h_in[p0 + t4 * 128:p0 + (t4 + 1) * 128, :], [B_hin[s]], [bHT])
                    rmsnorm_to_T(HT, bHT, XS, bXS, SSv, bSS, V_ANG + l * 8, XNT, bXNTt[t4], t4, 6 + (t4 % 2))
                chunks = [("q", C_Q, 512), ("q", C_Q + 512, 512), ("kv", C_K, 512), ("qi", C_QI, 512), ("kw", C_KI, 72)]
                for ci, (kind, c0, wd) in enumerate(chunks):
                    halves = [(0, min(256, wd))] + ([(256, wd - 256)] if wd > 256 else [])
                    for (ho, hw) in halves:
                        wv, wbuf = load_w(wl_in[:, :, c0 + ho:c0 + ho + hw], lambda t, hw=hw: t[:, :, 0:hw], [B_w["in"]])
                        for t4 in range(4):
                            for kc in range(8):
                                mm(bank(t4, hw, ho), XNT[:, kc, t4 * 128:(t4 + 1) * 128], wv[:, kc, :], kc == 0, kc == 7,
                                   [bXNTt[t4], wbuf], [bPS[t4]])
                    for t4 in range(4):
                        pos = p0 + t4 * 128
                        tile_i = s * 4 + t4
                        tm = TM[t4 % 2]
                        btm = bTM[t4 % 2]
                        psb = 4 + (t4 % 2)
                        pvT = bank_bf(psb).rearrange("p (a b) -> p a b", b=128)
                        if kind == "q":
                            src = bank(t4).rearrange("p (a b d) -> p a b d", a=2, b=4)
                            dstv = tm.rearrange("p (b a d) -> p a b d", a=2, b=4)
                            act(dstv, src, AF.Copy, [bPS[t4]], [btm], scale=0.125)
                            dma(CS, cs_d[pos:pos + 128, :], [B_ext], [bCS])
                            rope(tm.rearrange("p (h d) -> p h d", d=64), 8, btm, CS)
                            cp(TMB, tm, [btm], [bTMB])
                            for j in range(4):
                                tr(pvT[:, j, :], TMB[:, j * 128:(j + 1) * 128], IDB, [bTMB, bID], [bPS[psb]])
                            blk0 = (c0 // 512) * 4
                            cp(QTZ[0][0:64, blk0:blk0 + 4, t4 * 128:(t4 + 1) * 128], pvT[0:64, 0:4, :], [bPS[psb]], [bQT[t4]], eng="act")
                            cp(QTZ[1][64:128, blk0:blk0 + 4, t4 * 128:(t4 + 1) * 128], pvT[64:128, 0:4, :], [bPS[psb]], [bQT[t4]], eng="act")
                        elif kind == "kv":
                            act(tm[:, 0:256], bank(t4, 256, 0), AF.Copy, [bPS[t4]], [btm])
                            act(VV[:, tile_i, :, 0:64], bank(t4, 256, 256).rearrange("p (g d) -> p g d", d=64), AF.Copy,
                                [bPS[t4]], [bVV])
                            dma(CS, cs_d[pos:pos + 128, :], [B_ext], [bCS])
                            rope(tm[:, 0:256].rearrange("p (h d) -> p h d", d=64), 4, btm, CS)
                            cp(TMB[:, 0:256], tm[:, 0:256], [btm], [bTMB])
                            for j in range(2):
                                tr(pvT[:, j, :], TMB[:, j * 128:(j + 1) * 128], IDB, [bTMB, bID], [bPS[psb]])
                            cp(KT[:, :, pos:pos + 128], pvT[:, 0:2, :], [bPS[psb]], [bKT], eng="act")
                        elif kind == "qi":
                            act(tm, bank(t4), AF.Copy, [bPS[t4]], [btm])
                            dma(CS, cs_d[pos:pos + 128, :], [B_ext], [bCS])
                            rope(tm.rearrange("p (h d) -> p h d", d=64), 8, btm, CS)
                            cp(TMB, tm, [btm], [bTMB])
                            for j in range(4):
                                tr(pvT[:, j, :], TMB[:, j * 128:(j + 1) * 128], IDB, [bTMB, bID], [bPS[psb]])
                            cp(QITZ[0][0:64, :, t4 * 128:(t4 + 1) * 128], pvT[0:64, 0:4, :], [bPS[psb]], [bQIT[t4]], eng="act")
                            cp(QITZ[1][64:128, :, t4 * 128:(t4 + 1) * 128], pvT[64:128, 0:4, :], [bPS[psb]], [bQIT[t4]], eng="act")
                        else:
                            act(tm[:, 0:64], bank(t4, 64, 0), AF.Copy, [bPS[t4]], [btm])
                            act(tm[:, 64:128], bank(t4, 64, 0), AF.Copy, [bPS[t4]], [btm])
                            act(WI[:, t4, :], bank(t4, 8, 64), AF.Copy, [bPS[t4]], [bWI[t4]], scale=0.125 * (8.0 ** -0.5))
                            dma(CS, cs_d[pos:pos + 128, :], [B_ext], [bCS])
                            rope(tm[:, 0:128].rearrange("p (h d) -> p h d", d=64), 2, btm, CS)
                            cp(TMB[:, 0:128], tm[:, 0:128], [btm], [bTMB])
                            tr(pvT[:, 0, :], TMB[:, 0:128], IDB, [bTMB, bID], [bPS[psb]])
                            cp(KIT[:, pos:pos + 128], pvT[:, 0, :], [bPS[psb]], [bKIT], eng="act")
                if dbg and s == dbg.get("_s", 0) and l == dbg.get("_l", 0):
                    pass
                cp(UT[:, :, 0:30], UH, [bUH], bUT)
                for j2 in range(4):
                    wa, ba_ = load_w(wl_in[:, :, C_A + j2 * 256:C_A + (j2 + 1) * 256], lambda t: t, [B_w["in"]])
                    wbv, bb_ = load_w(wl_in[:, :, C_B + j2 * 256:C_B + (j2 + 1) * 256], lambda t: t, [B_w["in"]])
                    for j in range(2):
                        oc = j2 * 2 + j
                        pa, pb = (0, 1) if oc % 2 == 0 else (2, 3)
                        for kc in range(8):
                            mm(bank(pa), wa[:, kc, j * 128:(j + 1) * 128], XNT[:, kc, :], kc == 0, kc == 7, [ba_] + bXNTt, [bPS[pa]])
                        for kc in range(8):
                            mm(bank(pb), wbv[:, kc, j * 128:(j + 1) * 128], XNT[:, kc, :], kc == 0, kc == 7, [bb_] + bXNTt, [bPS[pb]])
                        sg = SGP[oc % 2]
                        act(sg, bank(pb), AF.Sigmoid, [bPS[pb]], [bSGP[oc % 2]])
                        tt(UT[:, oc, 30:542], bank(pa), sg, ALU.mult, [bPS[pa], bSGP[oc % 2]], [bUT[oc]])
                cp(UH, UT[:, :, 512:542], bUT, [bUH])
                for j2 in range(4):
                    wg, bg_ = load_w(wl_in[:, :, C_GA + j2 * 256:C_GA + (j2 + 1) * 256], lambda t: t, [B_w["in"]])
                    for j in range(2):
                        oc = j2 * 2 + j
                        pg = 4 + (oc % 2)
                        for kc in range(8):
                            mm(bank(pg), wg[:, kc, j * 128:(j + 1) * 128], XNT[:, kc, :], kc == 0, kc == 7, [bg_] + bXNTt, [bPS[pg]])
                        sg = SGP[oc % 2]
                        act(sg, bank(pg), AF.Sigmoid, [bPS[pg]], [bSGP[oc % 2]])
                        dma(ga_d[oc], sg, [bSGP[oc % 2]], [B_gad[oc]])
                for oc in range(8):
                    DGC = DGCs[oc % 2]
                    bDGC = bDGCs[oc % 2]
                    for k in range(31):
                        ts(DGC[:, k, :], IDB, vcol(V_CDW + (l * 8 + oc) * 31 + k), None, ALU.mult, None, [bID, bVEC], [bDGC])
                    pc = 6 + (oc % 2)
                    for k in range(31):
                        mm(bank(pc), DGC[:, k, :], UT[:, oc, k:k + 512], k == 0, k == 30, [bDGC, bUT[oc]], [bPS[pc]])
                    act(CV[:, oc, :], bank(pc), AF.Identity, [bPS[pc], bVEC], [bCV[oc]], bias=vcol(V_CDB + l * 8 + oc))
                for oc in range(8):
                    mm(bank(0), ONESF, CV[:, oc, :], oc == 0, oc == 7, [bID, bCV[oc]], [bPS[0]])
                for oc in range(8):
                    sq = SQ[oc % 2]
                    act(sq, CV[:, oc, :], AF.Square, [bCV[oc]], [bSQ[oc % 2]])
                    mm(bank(1), ONESF, sq, oc == 0, oc == 7, [bID, bSQ[oc % 2]], [bPS[1]])
                act(MEAN, bank(0), AF.Copy, [bPS[0]], [bMEAN])
                act(SQ[0], bank(0), AF.Square, [bPS[0]], [bSQ[0]])
                tt(RSD, bank(1), SQ[0], ALU.subtract, [bPS[1], bSQ[0]], [bRSD])
                act(RSD, RSD, AF.Sqrt, [bRSD], [bRSD], bias=1e-5, scale=1.0)
                recip(RSD, RSD, [bRSD], [bRSD])
                for oc in range(8):
                    tt(CV[:, oc, :], CV[:, oc, :], MEAN, ALU.subtract, [bCV[oc], bMEAN], [bCV[oc]])
                    tt(CV[:, oc, :], CV[:, oc, :], RSD, ALU.mult, [bCV[oc], bRSD], [bCV[oc]])
                    act(CVN[:, oc, 0:512], CV[:, oc, :], AF.Silu, [bCV[oc], bVEC], [bUT[oc]],
                        bias=vcol(V_LNB + l * 8 + oc), scale=vcol(V_LNG + l * 8 + oc))
                for j2 in range(4):
                    wy, by_ = load_w(wl_co[:, :, j2 * 256:(j2 + 1) * 256], lambda t: t, [B_w["co"]])
                    wg, bg_ = load_w(wl_in[:, :, C_GC + j2 * 256:C_GC + (j2 + 1) * 256], lambda t: t, [B_w["in"]])
                    for j in range(2):
                        oc = j2 * 2 + j
                        py, pg = (0, 1) if oc % 2 == 0 else (2, 3)
                        for kc in range(8):
                            mm(bank(py), wy[:, kc, j * 128:(j + 1) * 128], CVN[:, kc, 0:512], kc == 0, kc == 7, [by_] + bUT, [bPS[py]])
                        for kc in range(8):
                            mm(bank(pg), wg[:, kc, j * 128:(j + 1) * 128], XNT[:, kc, :], kc == 0, kc == 7, [bg_] + bXNTt, [bPS[pg]])
                        sg = SGP[oc % 2]
                        act(sg, bank(pg), AF.Sigmoid, [bPS[pg]], [bSGP[oc % 2]])
                        mt_ = MCT[oc % 2]
                        tt(mt_, bank(py), sg, ALU.mult, [bPS[py], bSGP[oc % 2]], [bMCT[oc % 2]])
                        dma(mc_d[oc], mt_, [bMCT[oc % 2]], [B_mcd[oc]])

                P.alias(A_BUFS, P_BUFS)
                bYAT = bXNTt
                for qb in range(4 if 'A' not in os.environ.get('KSKIP', '') else 0):
                    ti = s * 4 + qb
                    L = (ti + 1) * 128
                    nch = (L + 511) // 512
                    for h in range(8):
                        ts(DG[:, h, :], IDB, WI[:, qb, h:h + 1], None, ALU.mult, None, [bID, bWI[qb]], [bDG])
                    for c in range(nch):
                        k0 = c * 512
                        w = min(512, L - k0)
                        pacc = 6 + (c % 2)
                        for hp in range(4):
                            pbk = [(0, 1), (2, 3), (4, 5)][(c * 4 + hp) % 3]
                            for j in range(2):
                                h = hp * 2 + j
                                mm(bank(pbk[j], w), QITZ[h % 2][:, h // 2, qb * 128:(qb + 1) * 128], KIT[:, k0:k0 + w],
                                   True, True, [bQIT[qb], bKIT], [bPS[pbk[j]]])
                            src = PS[:, pbk[0] * 512:pbk[0] * 512 + 1024].rearrange("p (a b) -> p a b", b=512)[:, :, 0:w]
                            act(RR_[:, hp * 2:hp * 2 + 2, 0:w], src, AF.Relu, [bPS[pbk[0]], bPS[pbk[1]]], [bRR[hp]])
                        for h in range(8):
                            mm(bank(pacc, w), DG[:, h, :], RR_[:, h, 0:w], h == 0, h == 7, [bDG, bRR[h // 2]], [bPS[pacc]])
                        cp(SC[:, k0:k0 + w], bank(pacc, w), [bPS[pacc]], [bSC])
                    P.op("dve", lambda e, L=L: e.tensor_reduce(out=SM[:, 0:1], in_=SC[:, 0:L], axis=AX.X, op=ALU.max,
                                                               apply_absolute_value=True), [bSC], [bSM])
                    if ti * 128 < PADN + NMETA:
                        pass
                    memset(SC[:, 0:PADN], NEG, [bSC])
                    tt(SC[:, L - 128:L], SC[:, L - 128:L], CAUS, ALU.add, [bSC, bID], [bSC])
                    amax, lo, mid, cnt, ge = SM[:, 0:1], SM[:, 1:2], SM[:, 2:3], SM[:, 3:4], SM[:, 4:5]
                    STP = SM[:, 8:8 + NBIS]
                    ts(lo, amax, -1.0, -1e-6, ALU.mult, ALU.add, [bSM], [bSM])
                    ts(STP, VEC[:, V_BIS:V_BIS + NBIS], amax, 2.0, ALU.mult, ALU.mult, [bSM, bVEC], [bSM])
                    for it in range(NBIS if 'B' not in os.environ.get('KSKIP', '') else 0):
                        tt(mid, lo, STP[:, it:it + 1], ALU.add, [bSM], [bSM])
                        ts(JUNK[:, 0:L], SC[:, 0:L], mid, 0.0, ALU.is_ge, ALU.add, [bSC, bSM], bRR + [bSM], accum=cnt)
                        ts(ge, cnt, KSEL - 0.5, None, ALU.is_ge, None, [bSM], [bSM])
                        stt(lo, ge, STP[:, it:it + 1], lo, ALU.mult, ALU.add, [bSM], [bSM])
                    ts(DLO, IDF, lo, None, ALU.mult, None, [bID, bSM], [bTHB])
                    mm(bank(7, 128), ONES1, DLO, True, True, [bID, bTHB], [bPS[7]])
                    cp(THB, bank(7, 128), [bPS[7]], [bTHB])
                    nblk = ti + 1
                    steps = []
                    for c in range(nch):
                        nb = min(4, nblk - c * 4)
                        for j in range(nb):
                            for g in range(4):
                                steps.append((c, j, c * 4 + j, g))

                    def issue_mask(c):
                        nb = min(4, nblk - c * 4)
                        pT = bank(7).rearrange("p (a b) -> p a b", b=128)
                        for j in range(nb):
                            kb = c * 4 + j
                            tr(pT[:, j, :], SC[:, kb * 128:(kb + 1) * 128], IDF, [bSC, bID], [bPS[7]])
                        mt = MT[c % 2]
                        tt(mt[:, 0:nb, :], pT[:, 0:nb, :], THB.unsqueeze(1).to_broadcast([128, nb, 128]), ALU.is_ge,
                           [bPS[7], bTHB], [bMT[c % 2]])

                    if 'S' in os.environ.get('KSKIP', ''):
                        steps = steps[:4]
                    issue_mask(0)
                    DEPTH = 2
                    for n in range(len(steps) + DEPTH):
                        if n < len(steps):
                            c, j, kb, g = steps[n]
                            if j == 0 and g == 0 and c + 1 < nch:
                                issue_mask(c + 1)
                            mt = MT[c % 2]
                            e_, gp = g % 2, g // 2
                            psb = 4 + (n % 3)
                            ptb = n % 3
                            mm(bank(psb), KT[:, gp, kb * 128:(kb + 1) * 128],
                               QTZ[e_][:, gp * 4:gp * 4 + 4, qb * 128:(qb + 1) * 128],
                               True, True, [bKT, bQT[qb]], [bPS[psb]])
                            act(PT[ptb], bank(psb), AF.Exp, [bPS[psb]], [bPT[ptb]])
                            pv4 = PT[ptb].rearrange("p (a b) -> p a b", b=128)
                            tt(pv4, pv4, mt[:, j, :].unsqueeze(1).to_broadcast([128, 4, 128]), ALU.mult,
                               [bPT[ptb], bMT[c % 2]], [bPT[ptb]], eng="dve")
                        if n >= DEPTH:
                            c2, j2, kb2, g2 = steps[n - DEPTH]
                            ptb2 = (n - DEPTH) % 3
                            mm(PS[0:65, g2 * 512:(g2 + 1) * 512], VV[:, kb2, g2, :], PT[ptb2], kb2 == 0, (kb2 == nblk - 1) or ('S' in os.environ.get('KSKIP', '')),
                               [bVV, bPT[ptb2]], [bPS[g2]])
                    for g in range(4):
                        ts(RRW[64:65, :], PS[64:65, g * 512:(g + 1) * 512], 1e-30, None, ALU.add, None, [bPS[g]], [bRRW])
                        recip(RRW[64:65, :], RRW[64:65, :], [bRRW], [bRRW])
                        psb = 4 + (g % 2)
                        mm(PS[0:64, psb * 512:(psb + 1) * 512], ONES1[64:65, 0:64], RRW[64:65, :], True, True, [bID, bRRW], [bPS[psb]])
                        act(RBS[0:64, :], PS[0:64, psb * 512:(psb + 1) * 512], AF.Copy, [bPS[psb]], [bRBS])
                        tt(AT4[0:64, g * 4:(g + 1) * 4, :], PS[0:64, g * 512:(g + 1) * 512].rearrange("p (a b) -> p a b", b=128),
                           RBS[0:64, :].rearrange("p (a b) -> p a b", b=128), ALU.mult, [bPS[g], bRBS], [bAT4])
                    for oc in range(8):
                        wv, wbuf = load_w(wl_ao[:, :, oc * 128:(oc + 1) * 128],
                                          lambda t: t.rearrange("p a b -> p (a b)")[0:64, :].rearrange("p (h c) -> p h c", c=128),
                                          [B_w["ao"]])
                        psb = 6 + (oc // 4) % 2
                        po = (oc % 4) * 128
                        for h in range(16):
                            mm(bank(psb, 128, po), wv[:, h, :], AT4[0:64, h, :], h == 0, h == 15, [wbuf, bAT4], [bPS[psb]])
                        if oc % 4 == 3:
                            o4 = oc - 3
                            cp(XNT[:, o4:o4 + 4, qb * 128:(qb + 1) * 128], bank(psb).rearrange("p (a b) -> p a b", b=128),
                               [bPS[psb]], [bYAT[qb]], eng="act")

                P.alias(F_BUFS, A_BUFS)
                for oc in range(8):
                    i2 = oc % 2
                    dma(GAT[i2], ga_d[oc], [B_gad[oc]], [bGAT[i2]])
                    dma(MCL[i2], mc_d[oc], [B_mcd[oc]], [bMCL[i2]])
                    tt(TMPF, XNT[:, oc, :], GAT[i2], ALU.mult, bYAT + [bGAT[i2]], [bTMPF])
                    tt(XNT[:, oc, :], TMPF, MCL[i2], ALU.add, [bTMPF, bMCL[i2]], bYAT)
                for c2 in range(2):
                    for hf in range(2):
                        col = c2 * 512 + hf * 256
                        wv, wbuf = load_w(wl_o[:, :, col:col + 256], lambda t: t, [B_w["o"]])
                        for t4 in range(4):
                            for kc in range(8):
                                mm(bank(t4, 256, hf * 256), XNT[:, kc, t4 * 128:(t4 + 1) * 128], wv[:, kc, :], kc == 0, kc == 7,
                                   [bYAT[t4], wbuf], [bPS[t4]])
                    for t4 in range(4):
                        i2 = t4 % 2
                        dma(HR[i2], h_in[p0 + t4 * 128:p0 + (t4 + 1) * 128, c2 * 512:(c2 + 1) * 512], [B_hin[s]], [bHR[i2]])
                        tt(HM[:, t4, c2 * 512:(c2 + 1) * 512], bank(t4), HR[i2], ALU.add, [bPS[t4], bHR[i2]], [bHM[t4]])
                if s == 0:
                    for t4 in range(4):
                        ts(HM[:, t4, :], HM[:, t4, :], vcol(V_RM0 + t4), None, ALU.mult, None, [bHM[t4], bVEC], [bHM[t4]])
                for t4 in range(4):
                    rmsnorm_to_T(HM[:, t4, :], bHM[t4], XS2, bXS2, SS2, bSS2, V_FNG + l * 8, XNT, bXNTt[t4], t4, 4 + (t4 % 2))
                for (c0g, ng) in ([(0, 8), (8, 8), (16, 6)] if 'F' not in os.environ.get('KSKIP', '') else []):
                    for cpair in range(ng // 2):
                        cA = c0g + cpair * 2
                        wg, bwg = load_w(wl_up[:, :, cA * 128:cA * 128 + 256], lambda t: t, [B_w["up"]])
                        wu, bwu = load_w(wl_up[:, :, FH + cA * 128:FH + cA * 128 + 256], lambda t: t, [B_w["up"]])
                        for j in range(2):
                            c = cA + j
                            res = []
                            for (which, wv_, wb_) in ((0, wg, bwg), (1, wu, bwu)):
                                psb = which * 2 + (c % 2)
                                cidx = c + which * 22
                                for kc in range(8):
                                    mm(bank(psb), wv_[:, kc, j * 128:(j + 1) * 128], XNT[:, kc, :], kc == 0, kc == 7,
                                       [wb_] + bXNTt, [bPS[psb]])
                                raw = RAW[which][c % 2]
                                braw = bRAW[which][c % 2]
                                act(raw[:, 2:514], bank(psb), AF.Copy, [bPS[psb]], [braw])
                                cp(raw[:, 0:2], FHL[:, cidx, :], [bFHL], [braw])
                                cg = CGU[which]
                                wofs = V_FDW + (l * 44 + cidx) * 3
                                ts(cg, raw[:, 2:514], vcol(wofs + 2), vcol(V_FDB + l * 44 + cidx), ALU.mult, ALU.add,
                                   [braw, bVEC], [bCGU[which]])
                                stt(cg, raw[:, 1:513], vcol(wofs + 1), cg, ALU.mult, ALU.add, [braw, bVEC, bCGU[which]], [bCGU[which]])
                                stt(cg, raw[:, 0:512], vcol(wofs + 0), cg, ALU.mult, ALU.add, [braw, bVEC, bCGU[which]], [bCGU[which]])
                                cp(FHL[:, cidx, :], raw[:, 512:514], [braw], [bFHL])
                            act(SGF, CGU[0], AF.Silu, [bCGU[0]], [bSGF])
                            tt(ACTT[:, c - c0g, :], SGF, CGU[1], ALU.mult, [bSGF, bCGU[1]], [bACTT])
                    for c2 in range(2):
                        for hf in range(2):
                            col = c2 * 512 + hf * 256
                            wv, wbuf = load_w(wl_dn[:, c0g:c0g + ng, col:col + 256], lambda t, ng=ng: t[:, 0:ng, :], [B_w["dn"]])
                            for t4 in range(4):
                                psb = 4 + t4
                                for kc in range(ng):
                                    mm(bank(psb, 256, hf * 256), ACTT[:, kc, t4 * 128:(t4 + 1) * 128], wv[:, kc, :], kc == 0, kc == ng - 1,
                                       [bACTT, wbuf], [bPS[psb]])
                        for t4 in range(4):
                            tt(HM[:, t4, c2 * 512:(c2 + 1) * 512], HM[:, t4, c2 * 512:(c2 + 1) * 512], bank(4 + t4), ALU.add,
                               [bPS[4 + t4], bHM[t4]], [bHM[t4]])
                if last:
                    dma(FING, fing_d, [B_ext], [bFING])
                for t4 in range(4):
                    if s == 0:
                        ts(HM[:, t4, :], HM[:, t4, :], vcol(V_RM0 + t4), None, ALU.mult, None, [bHM[t4], bVEC], [bHM[t4]])
                    rows = slice(p0 + t4 * 128, p0 + (t4 + 1) * 128)
                    if not last:
                        dma(hA[rows, :], HM[:, t4, :], [bHM[t4]], [B_hA[s]])
                    elif s >= 1:
                        act(XS2, HM[:, t4, :], AF.Square, [bHM[t4]], [bXS2, bSS2], accum=SS2[:, 0:1])
                        act(SS2[:, 1:2], SS2[:, 0:1], AF.Sqrt, [bSS2], [bSS2], bias=1e-6, scale=1.0 / 1024.0)
                        recip(SS2[:, 2:3], SS2[:, 1:2], [bSS2], [bSS2])
                        ts(HM[:, t4, :], HM[:, t4, :], SS2[:, 2:3], None, ALU.mult, None, [bHM[t4], bSS2], [bHM[t4]])
                        tt(HM[:, t4, :], HM[:, t4, :], FING, ALU.mult, [bHM[t4], bFING], [bHM[t4]])
                        dma(out_d[p0 - 512 + t4 * 128:p0 - 512 + (t4 + 1) * 128, :], HM[:, t4, :], [bHM[t4]], [B_out])
        P.emit(final_bufs=[B_out, B_dbg])
    return nc


def host_consts(NS):
    TP = NS * 512
    pos = (np.arange(TP) - PADN).astype(np.float32)
    inv_freq = np.power(np.float32(500000.0), -np.arange(0, 16, 2, dtype=np.float32) / np.float32(16)).astype(np.float32)
    ang = (pos[:, None] * inv_freq[None, :]).astype(np.float32)
    cs = np.concatenate([np.cos(ang), np.sin(ang)], axis=1).astype(np.float32)
    t = np.arange(128)
    caus = np.where(t[None, :] <= t[:, None], 0.0, NEG).astype(np.float32)
    ident = np.eye(128, dtype=np.float32)
    return cs, caus, ident


def pack_vec(inp):
    vec = np.zeros((128, NV), np.float32)

    def pp(a):
        a = np.asarray(a, np.float32)
        lead = a.shape[:-1]
        n = a.shape[-1] // 128
        return np.moveaxis(a.reshape(*lead, n, 128), -1, 0)

    vec[:, V_ANG:V_ANG + 16] = pp(inp["attn_norm_g"]).reshape(128, 16)
    vec[:, V_FNG:V_FNG + 16] = pp(inp["ffn_norm_g"]).reshape(128, 16)
    vec[:, V_CDB:V_CDB + 16] = pp(inp["conv_dw_b"]).reshape(128, 16)
    vec[:, V_LNG:V_LNG + 16] = pp(inp["conv_ln_g"]).reshape(128, 16)
    vec[:, V_LNB:V_LNB + 16] = pp(inp["conv_ln_b"]).reshape(128, 16)
    cdw = pp(inp["conv_dw_w"])
    vec[:, V_CDW:V_CDW + 496] = np.transpose(cdw, (0, 1, 3, 2)).reshape(128, 496)
    fdw = pp(inp["ffn_dw_w"])
    vec[:, V_FDW:V_FDW + 264] = np.transpose(fdw, (0, 1, 3, 2)).reshape(128, 264)
    vec[:, V_FDB:V_FDB + 88] = pp(inp["ffn_dw_b"]).reshape(128, 88)
    rm = np.ones((128, 4), np.float32)
    p = np.arange(512).reshape(4, 128).T
    rm[p < PADN] = 0.0
    vec[:, V_RM0:V_RM0 + 4] = rm
    vec[:, V_BIS:V_BIS + NBIS] = (2.0 ** -(np.arange(NBIS) + 1.0)).astype(np.float32)[None, :]
    return vec


def make_in_map(inp, b, NS):
    TP = NS * 512
    nreal = TP - 512
    xpad = np.zeros((TP, D), np.float32)
    xpad[PADN:PADN + NMETA] = np.asarray(inp["meta_tokens"], np.float32)
    xpad[512:] = np.asarray(inp["x"][b, :nreal], np.float32)
    cs, caus, ident = host_consts(NS)
    m = {
        "xp": xpad, "cs": cs, "vec": pack_vec(inp), "caus": caus, "ident": ident,
        "fing": np.ascontiguousarray(np.broadcast_to(np.asarray(inp["final_norm_g"], np.float32)[None, :], (128, D))),
        "w_in": np.asarray(inp["w_in"], np.float32), "w_attn_out": np.asarray(inp["w_attn_out"], np.float32),
        "w_conv_out": np.asarray(inp["w_conv_out"], np.float32), "w_o": np.asarray(inp["w_o"], np.float32),
        "w_up": np.asarray(inp["w_up"], np.float32), "w_down": np.asarray(inp["w_down"], np.float32),
    }
    return m


_NC_CACHE = {}


def kernel(**inputs):
    NS = 17
    if NS not in _NC_CACHE:
        _NC_CACHE[NS] = build(NS)
    nc = _NC_CACHE[NS]
    B = inputs["x"].shape[0]
    in_maps = [make_in_map(inputs, (c // 2) % B, NS) for c in range(8)]
    res = run_bass_kernel_spmd(nc, in_maps, core_ids=list(range(8)))
    out = np.stack([res.results[2 * b]["out"] for b in range(B)], axis=0)
    return out.astype(np.float32)
```

```python
from contextlib import ExitStack
import os
import numpy as np
import concourse.bass as bass
import concourse.mybir as mybir
from concourse.bass_utils import run_bass_kernel_spmd

F32 = mybir.dt.float32
BF16 = mybir.dt.bfloat16
U8 = mybir.dt.uint8
AF = mybir.ActivationFunctionType
ALU = mybir.AluOpType
AX = mybir.AxisListType

EPOCH = 4096
NDMA_SLOTS = 12


class Buf:
    __slots__ = ("name", "w", "r")

    def __init__(self, name):
        self.name = name
        self.w = {}
        self.r = {}


def _merge(d, s):
    for k, v in s.items():
        if d.get(k, 0) < v:
            d[k] = v


class Prog:
    ENGS = ("pe", "act", "dve", "pool", "sp")

    def __init__(self, nc):
        self.nc = nc
        self.q = {e: [] for e in self.ENGS}
        self.count = {e: 0 for e in self.ENGS}
        self.known = {e: {} for e in self.ENGS}
        self.dma_next = {"sp": 0, "pool": 0}
        self.dma_uses = {}
        self.semkeys = set()

    def alias(self, new_bufs, old_bufs):
        d = {}
        for b in old_bufs:
            _merge(d, b.w)
            _merge(d, b.r)
        for b in new_bufs:
            _merge(b.w, d)

    def _add(self, eng, fn, reads, writes, tok, inc, extra_waits=()):
        deps = {}
        for b in reads:
            _merge(deps, b.w)
        for b in writes:
            _merge(deps, b.w)
            _merge(deps, b.r)
        for k, v in extra_waits:
            if deps.get(k, 0) < v:
                deps[k] = v
        waits = []
        kn = self.known[eng]
        for k, v in deps.items():
            if eng == "pe" and k[0] == "pe":
                continue
            if kn.get(k, 0) >= v:
                continue
            kn[k] = v
            waits.append((k, v))
        semkey, val = tok
        self.semkeys.add(semkey)
        for b in writes:
            b.w = {semkey: val}
            b.r = {}
        for b in reads:
            if b not in writes:
                if b.r.get(semkey, 0) < val:
                    b.r[semkey] = val
        self.q[eng].append((waits, fn, semkey, inc))

    def op(self, eng, fn, reads=(), writes=()):
        idx = self.count[eng]
        self.count[eng] += 1
        tok = ((eng, idx // EPOCH), idx % EPOCH + 1)
        self._add(eng, fn, reads, writes, tok, 1)

    def dma(self, queue, fn, reads=(), writes=()):
        slot = self.dma_next[queue]
        self.dma_next[queue] = (slot + 1) % NDMA_SLOTS
        semkey = ("dma" + queue, slot)
        prev = self.dma_uses.get(semkey, 0)
        val = prev + 16
        self.dma_uses[semkey] = val
        extra = [(semkey, prev)] if prev > 0 else []
        self._add(queue, fn, reads, writes, (semkey, val), 16, extra)

    def emit(self, final_bufs=()):
        nc = self.nc
        deps = {}
        for b in final_bufs:
            _merge(deps, b.w)
        fw = list(deps.items())
        with ExitStack() as es:
            sems = {}
            for k in sorted(self.semkeys, key=str):
                nm = "s_" + "_".join(str(x) for x in k)
                sems[k] = es.enter_context(nc.semaphore(nm))
            block = es.enter_context(nc.Block())
            q = self.q

            def body_for(engname):
                def body(e):
                    for waits, fn, semkey, inc in q[engname]:
                        for (k, v) in waits:
                            e.wait_ge(sems[k], v)
                        inst = fn(e)
                        inst.then_inc(sems[semkey], inc)
                    if engname == "sp":
                        for (k, v) in fw:
                            e.wait_ge(sems[k], v)
                return body

            block.tensor(body_for("pe"))
            block.scalar(body_for("act"))
            block.vector(body_for("dve"))
            block.gpsimd(body_for("pool"))
            block.sync(body_for("sp"))


D = 1024
NCOL = 6216
FH = 2816
PADN = 496
NMETA = 16
C_Q, C_K, C_V, C_QI, C_KI, C_WI, C_A, C_B, C_GA, C_GC = 0, 1024, 1280, 1536, 2048, 2112, 2120, 3144, 4168, 5192
NBIS = 16
NEG = -1.0e30

V_ANG = 0
V_FNG = 16
V_CDB = 32
V_LNG = 48
V_LNB = 64
V_CDW = 80
V_FDW = 576
V_FDB = 840
V_RM0 = 928
V_BIS = 932
NV = 960


def build(NS, NL=2, dbg=None):
    TP = NS * 512
    KSEL = min(256, (NMETA + TP - 512) // 4)
    NT = NS * 4
    nc = bass.Bass("TRN2", target_bir_lowering=False)
    P = Prog(nc)

    def din(name, shape, dt=F32):
        return nc.dram_tensor(name, list(shape), dt, kind="ExternalInput").ap()

    def dint(name, shape, dt=F32):
        return nc.dram_tensor(name, list(shape), dt, kind="Internal").ap()

    xp = din("xp", [TP, D])
    cs_d = din("cs", [TP, 16])
    vec_d = din("vec", [128, NV])
    caus_d = din("caus", [128, 128])
    ident_d = din("ident", [128, 128])
    fing_d = din("fing", [128, D])
    w_in = din("w_in", [2, D, NCOL])
    w_ao = din("w_attn_out", [2, D, D])
    w_co = din("w_conv_out", [2, D, D])
    w_o = din("w_o", [2, D, D])
    w_up = din("w_up", [2, D, 2 * FH])
    w_dn = din("w_down", [2, FH, D])
    out_d = nc.dram_tensor("out", [TP - 512, D], F32, kind="ExternalOutput").ap()

    b_in = dint("b_in", [2, D, NCOL], BF16)
    b_ao = dint("b_ao", [2, D, D], BF16)
    b_co = dint("b_co", [2, D, D], BF16)
    b_o = dint("b_o", [2, D, D], BF16)
    b_up = dint("b_up", [2, D, 2 * FH], BF16)
    b_dn = dint("b_dn", [2, FH, D], BF16)
    hA = dint("hA", [TP, D])
    ga_d = dint("ga_s", [8, 128, 512])
    mc_d = dint("mc_s", [8, 128, 512])

    dbg_out = {}
    if dbg:
        for name, shape in dbg.items():
            dbg_out[name] = nc.dram_tensor("dbg_" + name, list(shape), F32, kind="ExternalOutput").ap()

    B_w = {k: Buf("w_" + k) for k in ["in", "ao", "co", "o", "up", "dn"]}
    B_hA = [Buf("hA%d" % s) for s in range(NS)]
    B_gad = [Buf("gad%d" % i) for i in range(8)]
    B_mcd = [Buf("mcd%d" % i) for i in range(8)]
    B_out = Buf("out")
    B_dbg = Buf("dbg")
    B_ext = Buf("ext")

    es = ExitStack()
    with es:
        def sb(name, shape, dt):
            return es.enter_context(nc.sbuf_tensor(name, list(shape), dt))[:]

        KT = sb("KT", [128, 2, TP], BF16)
        VV = sb("VV", [128, NT, 4, 65], BF16)
        KIT = sb("KIT", [128, TP], BF16)
        VEC = sb("VEC", [128, NV], F32)
        IDF = sb("IDF", [128, 128], F32)
        IDB = sb("IDB", [128, 128], BF16)
        CAUS = sb("CAUS", [128, 128], F32)
        ONESF = sb("ONESF", [128, 128], F32)
        ONES1 = sb("ONES1", [128, 128], F32)
        XNT = sb("XNT", [128, 8, 512], BF16)
        QTZ = [sb("QTZ%d" % i, [128, 8, 512], BF16) for i in range(2)]
        QITZ = [sb("QITZ%d" % i, [128, 4, 512], BF16) for i in range(2)]
        WI = sb("WI", [128, 4, 8], F32)
        UH = sb("UH", [128, 8, 30], BF16)
        FHL = sb("FHL", [128, 44, 2], F32)
        NWB = 4
        WBA = sb("WBA", [128, NWB, 8, 256], BF16)
        WBs = [WBA[:, i] for i in range(NWB)]
        ARW = 16640
        ARENA = sb("ARENA", [128, ARW], F32)
        PS = es.enter_context(nc.psum_tensor("PS", [128, 4096], F32))[:]

        bKT, bVV, bKIT, bVEC, bID, bXNT, bUH, bFHL = [Buf(n) for n in "KT VV KIT VEC ID XNT UH FHL".split()]
        bQT = [Buf("QT%d" % i) for i in range(4)]
        bQIT = [Buf("QIT%d" % i) for i in range(4)]
        bWI = [Buf("WI%d" % i) for i in range(4)]
        bXNTt = [Buf("XNT%d" % i) for i in range(4)]
        bWB = [Buf("WB%d" % i) for i in range(NWB)]
        bPS = [Buf("PS%d" % i) for i in range(8)]
        wb_rr = [0]

        def bank(i, w=512, off=0):
            return PS[:, i * 512 + off:i * 512 + off + w]

        def bank_bf(i):
            return PS[:, i * 512:(i + 1) * 512].bitcast(BF16)

        def mm(out, lhsT, rhs, st, sp_, R, W):
            P.op("pe", lambda e: e.matmul(out, lhsT=lhsT, rhs=rhs, start=st, stop=sp_), R, W)

        def tr(out, in_, ident, R, W):
            P.op("pe", lambda e: e.transpose(out=out, in_=in_, identity=ident), R, W)

        def act(out, in_, func, R, W, bias=None, scale=None, accum=None):
            kw = {}
            if bias is not None:
                kw["bias"] = bias
            if scale is not None:
                kw["scale"] = scale
            if accum is not None:
                kw["accum_out"] = accum
            P.op("act", lambda e: e.activation(out=out, in_=in_, func=func, **kw), R, W)

        def ts(out, in0, s1, s2, op0, op1, R, W, accum=None, eng="dve"):
            if op1 is None:
                P.op(eng, lambda e: e.tensor_scalar(out=out, in0=in0, scalar1=s1, scalar2=None, op0=op0), R, W)
            elif accum is None:
                P.op(eng, lambda e: e.tensor_scalar(out=out, in0=in0, scalar1=s1, scalar2=s2, op0=op0, op1=op1), R, W)
            else:
                P.op(eng, lambda e: e.tensor_scalar(out=out, in0=in0, scalar1=s1, scalar2=s2, op0=op0, op1=op1, accum_out=accum), R, W)

        def tt(out, in0, in1, op, R, W, eng="dve"):
            P.op(eng, lambda e: e.tensor_tensor(out=out, in0=in0, in1=in1, op=op), R, W)

        def stt(out, in0, scalar, in1, op0, op1, R, W):
            P.op("dve", lambda e: e.scalar_tensor_tensor(out=out, in0=in0, scalar=scalar, in1=in1, op0=op0, op1=op1), R, W)

        def cp(out, in_, R, W, eng="dve"):
            if eng == "act":
                P.op(eng, lambda e: e.activation(out=out, in_=in_, func=AF.Copy), R, W)
            else:
                P.op(eng, lambda e: e.tensor_copy(out=out, in_=in_), R, W)

        def recip(out, in_, R, W):
            P.op("dve", lambda e: e.reciprocal(out=out, in_=in_), R, W)

        def memset(ap, val, W, eng="dve"):
            P.op(eng, lambda e: e.memset(ap, val), [], W)

        def dma(out, in_, R, W, queue="sp", maxlast=None):
            if maxlast is None:
                P.dma(queue, lambda e: e.dma_start(out=out, in_=in_), R, W)
            else:
                P.dma(queue, lambda e: e.dma_start(out=out, in_=in_, max_dma_last_dim=maxlast), R, W)

        def load_w(src_ap, shape_view, R):
            i = wb_rr[0]
            wb_rr[0] = (i + 1) % NWB
            v = shape_view(WBs[i])
            dma(v, src_ap, R, [bWB[i]])
            return v, bWB[i]

        def load_w2(src_ap, nk, R):
            if wb_rr[0] % 2 == 1:
                wb_rr[0] = (wb_rr[0] + 1) % NWB
            i = wb_rr[0]
            wb_rr[0] = (i + 2) % NWB
            dma(WBA[:, i, 0:nk, :], src_ap[:, :, 0:256], R, [bWB[i]])
            dma(WBA[:, i + 1, 0:nk, :], src_ap[:, :, 256:512], R, [bWB[i + 1]])
            return (lambda kc: WBA[:, i:i + 2, kc, :]), [bWB[i], bWB[i + 1]]

        def dump(name, ap_sb, R, rows=None):
            if name in dbg_out:
                dst = dbg_out[name]
                dma(dst, ap_sb, R, [B_dbg])

        def carve(off, nwords):
            assert off + nwords <= ARW, (off, nwords, ARW)
            return ARENA[:, off:off + nwords]

        dma(VEC, vec_d, [B_ext], [bVEC])
        dma(IDF, ident_d, [B_ext], [bID])
        dma(CAUS, caus_d, [B_ext], [bID])
        cp(IDB, IDF, [bID], [bID])
        memset(ONESF, 1.0 / 1024.0, [bID])
        memset(ONES1, 1.0, [bID])
        for _i in range(2):
            memset(QTZ[_i], 0.0, bQT)
            memset(QITZ[_i], 0.0, bQIT)
        memset(VV, 1.0, [bVV])
        for (src, dst, key, rows) in [(w_in, b_in, "in", D), (w_ao, b_ao, "ao", D), (w_co, b_co, "co", D),
                                      (w_o, b_o, "o", D), (w_up, b_up, "up", D), (w_dn, b_dn, "dn", FH)]:
            for l in range(NL):
                for r0 in range(0, rows, 128):
                    dma(dst[l, r0:r0 + 128, :], src[l, r0:r0 + 128, :], [B_ext], [B_w[key]], queue="pool", maxlast=4096)

        o = 0
        HT = carve(o, 1024); o += 1024
        XS = carve(o, 512).bitcast(BF16); o += 512
        TM = [carve(o, 512), carve(o + 512, 512)]; o += 1024
        TMB = carve(o, 256).bitcast(BF16); o += 256
        CS = carve(o, 16); o += 16
        RT = carve(o, 384).rearrange("p (a b) -> p a b", b=128); o += 384
        SSv = carve(o, 8); o += 8
        UT = carve(o, 8 * 272).bitcast(BF16).rearrange("p (a b) -> p a b", b=544); o += 8 * 272
        CV = carve(o, 4096).rearrange("p (a b) -> p a b", b=512); o += 4096
        SGP = [carve(o, 512), carve(o + 512, 512)]; o += 1024
        MEAN = carve(o, 512); o += 512
        RSD = carve(o, 512); o += 512
        SQ = [carve(o, 512), carve(o + 512, 512)]; o += 1024
        DGCs = []
        for _i in range(2):
            DGCs.append(carve(o, 31 * 64).bitcast(BF16).rearrange("p (a b) -> p a b", b=128)); o += 31 * 64
        MCT = SQ
        P_END = o
        CVN = UT
        bHT, bXS, bTMB, bCS, bRT, bSS, bMEAN, bRSD = [Buf(n) for n in "HT XS TMB CS RT SS MEAN RSD".split()]
        bDGCs = [Buf("DGC0"), Buf("DGC1")]
        bTM = [Buf("TM0"), Buf("TM1")]
        bUT = [Buf("UT%d" % i) for i in range(8)]
        bCV = [Buf("CV%d" % i) for i in range(8)]
        bSGP = [Buf("SGP0"), Buf("SGP1")]
        bSQ = [Buf("SQ0"), Buf("SQ1")]
        bMCT = bSQ
        P_BUFS = [bHT, bXS, bTMB, bCS, bRT, bSS, bMEAN, bRSD] + bDGCs + bTM + bUT + bCV + bSGP + bSQ
        o = 0
        SC = carve(o, TP); o += TP
        RR_ = carve(o, 2048).bitcast(BF16).rearrange("p (a b) -> p a b", b=512); o += 2048
        JUNK = carve(o - 2048, (TP + 3) // 4).bitcast(U8)
        if (TP + 3) // 4 > 2048:
            o = o - 2048 + (TP + 3) // 4
        PT = [carve(o + i * 256, 256).bitcast(BF16) for i in range(3)]; o += 768
        MT = [carve(o + i * 256, 256).bitcast(BF16).rearrange("p (a b) -> p a b", b=128) for i in range(2)]; o += 512
        DG = carve(o, 512).bitcast(BF16).rearrange("p (a b) -> p a b", b=128); o += 512
        THB = carve(o, 128); o += 128
        DLO = carve(o, 128); o += 128
        AT4 = carve(o, 1024).bitcast(BF16).rearrange("p (a b) -> p a b", b=128); o += 1024
        RRWs = [carve(o, 512), carve(o + 512, 512)]; o += 1024
        JUNKA = carve(o, 1408).bitcast(U8); o += 1408
        SM = carve(o, 64); o += 64
        A_END = o
        bSC = Buf("SC")
        bRR = [Buf("RR%d" % i) for i in range(4)]
        bPT = [Buf("PT%d" % i) for i in range(3)]
        bMT = [Buf("MT0"), Buf("MT1")]
        bDG, bTHB, bAT4, bSM, bSMc, bSMa, bJA = [Buf(n) for n in "DG THB AT4 SM SMc SMa JA".split()]
        bRRWs = [Buf("RRW0"), Buf("RRW1")]
        bRBSs = [Buf("RBS0"), Buf("RBS1")]
        A_BUFS = [bSC] + bRR + bPT + bMT + [bDG, bTHB, bAT4, bSM, bSMc, bSMa, bJA] + bRRWs + bRBSs
        o = 0
        HM = carve(o, 4096).rearrange("p (a b) -> p a b", b=1024); o += 4096
        HR = [carve(o, 512), carve(o + 512, 512)]; o += 1024
        GAT = [carve(o, 512), carve(o + 512, 512)]; o += 1024
        MCL = [carve(o, 512), carve(o + 512, 512)]; o += 1024
        TMPF = carve(o, 512); o += 512
        XS2 = carve(o, 512).bitcast(BF16); o += 512
        RAW = [[carve(o + (2 * a + b) * 520, 520) for b in range(2)] for a in range(2)]; o += 2080
        CGU = [carve(o, 512), carve(o + 512, 512)]; o += 1024
        SGF = carve(o, 512); o += 512
        ACTT = carve(o, 2048).bitcast(BF16).rearrange("p (a b) -> p a b", b=512); o += 2048
        SS2 = carve(o, 8); o += 8
        FING = carve(o, 1024); o += 1024
        F_END = o
        bHM = [Buf("HM%d" % i) for i in range(4)]
        bHR = [Buf("HR0"), Buf("HR1")]
        bGAT = [Buf("GAT0"), Buf("GAT1")]
        bMCL = [Buf("MCL0"), Buf("MCL1")]
        bRAW = [[Buf("RAW%d%d" % (a, b)) for b in range(2)] for a in range(2)]
        bCGU = [Buf("CGU0"), Buf("CGU1")]
        bTMPF, bXS2, bSGF, bACTT, bSS2, bFING = [Buf(n) for n in "TMPF XS2 SGF ACTT SS2 FING".split()]
        F_BUFS = bHM + bHR + bGAT + bMCL + bRAW[0] + bRAW[1] + bCGU + [bTMPF, bXS2, bSGF, bACTT, bSS2, bFING]
        assert max(P_END, A_END, F_END) <= ARW, (P_END, A_END, F_END)

        def vcol(off, n=1):
            return VEC[:, off:off + n]


        def rmsnorm_to_T(src_tile, bsrc, xs, bxs, ssv, bss, g_off, dst_T, bdst, tt_i, psb):
            act(xs, src_tile, AF.Square, [bsrc], [bxs, bss], accum=ssv[:, 0:1])
            act(ssv[:, 1:2], ssv[:, 0:1], AF.Sqrt, [bss], [bss], bias=1e-6, scale=1.0 / 1024.0)
            recip(ssv[:, 2:3], ssv[:, 1:2], [bss], [bss])
            ts(xs, src_tile, ssv[:, 2:3], None, ALU.mult, None, [bsrc, bss], [bxs])
            pv = bank_bf(psb).rearrange("p (a b) -> p a b", b=128)
            for kc in range(8):
                tr(pv[:, kc, :], xs[:, kc * 128:(kc + 1) * 128], IDB, [bxs, bID], [bPS[psb]])
            gb = VEC[:, g_off:g_off + 8].unsqueeze(2).to_broadcast([128, 8, 128])
            tt(dst_T[:, :, tt_i * 128:(tt_i + 1) * 128], pv, gb, ALU.mult, [bPS[psb], bVEC], [bdst])

        def rope(v, nh, bv, cs):
            x1 = v[:, :, 0:8]
            x2 = v[:, :, 8:16]
            c = cs[:, 0:8].unsqueeze(1).to_broadcast([128, nh, 8])
            s = cs[:, 8:16].unsqueeze(1).to_broadcast([128, nh, 8])
            ta = RT[:, 0, 0:nh * 8].rearrange("p (a b) -> p a b", b=8)
            tb = RT[:, 1, 0:nh * 8].rearrange("p (a b) -> p a b", b=8)
            tc_ = RT[:, 2, 0:nh * 8].rearrange("p (a b) -> p a b", b=8)
            tt(ta, x1, c, ALU.mult, [bv, bCS], [bRT])
            tt(tb, x2, s, ALU.mult, [bv, bCS], [bRT])
            tt(tc_, x1, s, ALU.mult, [bv, bCS], [bRT])
            tt(x1, ta, tb, ALU.subtract, [bRT], [bv])
            tt(x2, x2, c, ALU.mult, [bv, bCS], [bv])
            tt(x2, x2, tc_, ALU.add, [bv, bRT], [bv])

        for l in range(NL):
            h_in = xp if l == 0 else hA
            B_hin = [B_ext] * NS if l == 0 else B_hA
            last = (l == NL - 1)
            memset(UH, 0.0, [bUH])
            memset(FHL, 0.0, [bFHL])
            wl_in = b_in[l].rearrange("(kc p) n -> p kc n", p=128)
            wl_co = b_co[l].rearrange("(kc p) n -> p kc n", p=128)
            wl_o = b_o[l].rearrange("(kc p) n -> p kc n", p=128)
            wl_up = b_up[l].rearrange("(kc p) n -> p kc n", p=128)
            wl_dn = b_dn[l].rearrange("(kc p) n -> p kc n", p=128)
            wl_ao = b_ao[l].rearrange("(h d) n -> d h n", d=64)

            for s in range(NS):
                p0 = s * 512
                P.alias(P_BUFS, F_BUFS + A_BUFS)
                for t4 in range(4):
                    dma(HT, h_in[p0 + t4 * 128:p0 + (t4 + 1) * 128, :], [B_hin[s]], [bHT])
                    rmsnorm_to_T(HT, bHT, XS, bXS, SSv, bSS, V_ANG + l * 8, XNT, bXNTt[t4], t4, 6 + (t4 % 2))
                chunks = [("q", C_Q, 512), ("q", C_Q + 512, 512), ("kv", C_K, 512), ("qi", C_QI, 512), ("kw", C_KI, 72)]
                for ci, (kind, c0, wd) in enumerate(chunks):
                    if wd == 512:
                        rfn, wbufs = load_w2(wl_in[:, :, c0:c0 + 512], 8, [B_w["in"]])
                        for t4 in range(4):
                            for kc in range(8):
                                mm(bank(t4), XNT[:, kc, t4 * 128:(t4 + 1) * 128], rfn(kc), kc == 0, kc == 7,
                                   [bXNTt[t4]] + wbufs, [bPS[t4]])
                    else:
                        wv, wbuf = load_w(wl_in[:, :, c0:c0 + wd], lambda t, hw=wd: t[:, :, 0:hw], [B_w["in"]])
                        for t4 in range(4):
                            for kc in range(8):
                                mm(bank(t4, wd, 0), XNT[:, kc, t4 * 128:(t4 + 1) * 128], wv[:, kc, :], kc == 0, kc == 7,
                                   [bXNTt[t4], wbuf], [bPS[t4]])
                    for t4 in range(4):
                        pos = p0 + t4 * 128
                        tile_i = s * 4 + t4
                        tm = TM[t4 % 2]
                        btm = bTM[t4 % 2]
                        psb = 4 + (t4 % 2)
                        pvT = bank_bf(psb).rearrange("p (a b) -> p a b", b=128)
                        if kind == "q":
                            src = bank(t4).rearrange("p (a b d) -> p a b d", a=2, b=4)
                            dstv = tm.rearrange("p (b a d) -> p a b d", a=2, b=4)
                            act(dstv, src, AF.Copy, [bPS[t4]], [btm], scale=0.125)
                            dma(CS, cs_d[pos:pos + 128, :], [B_ext], [bCS])
                            rope(tm.rearrange("p (h d) -> p h d", d=64), 8, btm, CS)
                            cp(TMB, tm, [btm], [bTMB])
                            for j in range(4):
                                tr(pvT[:, j, :], TMB[:, j * 128:(j + 1) * 128], IDB, [bTMB, bID], [bPS[psb]])
                            blk0 = (c0 // 512) * 4
                            cp(QTZ[0][0:64, blk0:blk0 + 4, t4 * 128:(t4 + 1) * 128], pvT[0:64, 0:4, :], [bPS[psb]], [bQT[t4]], eng="act")
                            cp(QTZ[1][64:128, blk0:blk0 + 4, t4 * 128:(t4 + 1) * 128], pvT[64:128, 0:4, :], [bPS[psb]], [bQT[t4]], eng="act")
                        elif kind == "kv":
                            act(tm[:, 0:256], bank(t4, 256, 0), AF.Copy, [bPS[t4]], [btm])
                            act(VV[:, tile_i, :, 0:64], bank(t4, 256, 256).rearrange("p (g d) -> p g d", d=64), AF.Copy,
                                [bPS[t4]], [bVV])
                            dma(CS, cs_d[pos:pos + 128, :], [B_ext], [bCS])
                            rope(tm[:, 0:256].rearrange("p (h d) -> p h d", d=64), 4, btm, CS)
                            cp(TMB[:, 0:256], tm[:, 0:256], [btm], [bTMB])
                            for j in range(2):
                                tr(pvT[:, j, :], TMB[:, j * 128:(j + 1) * 128], IDB, [bTMB, bID], [bPS[psb]])
                            cp(KT[:, :, pos:pos + 128], pvT[:, 0:2, :], [bPS[psb]], [bKT], eng="act")
                        elif kind == "qi":
                            act(tm, bank(t4), AF.Copy, [bPS[t4]], [btm])
                            dma(CS, cs_d[pos:pos + 128, :], [B_ext], [bCS])
                            rope(tm.rearrange("p (h d) -> p h d", d=64), 8, btm, CS)
                            cp(TMB, tm, [btm], [bTMB])
                            for j in range(4):
                                tr(pvT[:, j, :], TMB[:, j * 128:(j + 1) * 128], IDB, [bTMB, bID], [bPS[psb]])
                            cp(QITZ[0][0:64, :, t4 * 128:(t4 + 1) * 128], pvT[0:64, 0:4, :], [bPS[psb]], [bQIT[t4]], eng="act")
                            cp(QITZ[1][64:128, :, t4 * 128:(t4 + 1) * 128], pvT[64:128, 0:4, :], [bPS[psb]], [bQIT[t4]], eng="act")
                        else:
                            act(tm[:, 0:64], bank(t4, 64, 0), AF.Copy, [bPS[t4]], [btm])
                            act(tm[:, 64:128], bank(t4, 64, 0), AF.Copy, [bPS[t4]], [btm])
                            act(WI[:, t4, :], bank(t4, 8, 64), AF.Copy, [bPS[t4]], [bWI[t4]], scale=0.125 * (8.0 ** -0.5))
                            dma(CS, cs_d[pos:pos + 128, :], [B_ext], [bCS])
                            rope(tm[:, 0:128].rearrange("p (h d) -> p h d", d=64), 2, btm, CS)
                            cp(TMB[:, 0:128], tm[:, 0:128], [btm], [bTMB])
                            tr(pvT[:, 0, :], TMB[:, 0:128], IDB, [bTMB, bID], [bPS[psb]])
                            cp(KIT[:, pos:pos + 128], pvT[:, 0, :], [bPS[psb]], [bKIT], eng="act")
                if dbg and s == dbg.get("_s", 0) and l == dbg.get("_l", 0):
                    pass
                cp(UT[:, :, 0:30], UH, [bUH], bUT)
                for j2 in range(4):
                    wa, ba_ = load_w(wl_in[:, :, C_A + j2 * 256:C_A + (j2 + 1) * 256], lambda t: t, [B_w["in"]])
                    wbv, bb_ = load_w(wl_in[:, :, C_B + j2 * 256:C_B + (j2 + 1) * 256], lambda t: t, [B_w["in"]])
                    for j in range(2):
                        oc = j2 * 2 + j
                        pa, pb = (0, 1) if oc % 2 == 0 else (2, 3)
                        for kc in range(8):
                            mm(bank(pa), wa[:, kc, j * 128:(j + 1) * 128], XNT[:, kc, :], kc == 0, kc == 7, [ba_] + bXNTt, [bPS[pa]])
                        for kc in range(8):
                            mm(bank(pb), wbv[:, kc, j * 128:(j + 1) * 128], XNT[:, kc, :], kc == 0, kc == 7, [bb_] + bXNTt, [bPS[pb]])
                        sg = SGP[oc % 2]
                        act(sg, bank(pb), AF.Sigmoid, [bPS[pb]], [bSGP[oc % 2]])
                        tt(UT[:, oc, 30:542], bank(pa), sg, ALU.mult, [bPS[pa], bSGP[oc % 2]], [bUT[oc]])
                cp(UH, UT[:, :, 512:542], bUT, [bUH])
                for j2 in range(4):
                    wg, bg_ = load_w(wl_in[:, :, C_GA + j2 * 256:C_GA + (j2 + 1) * 256], lambda t: t, [B_w["in"]])
                    for j in range(2):
                        oc = j2 * 2 + j
                        pg = 4 + (oc % 2)
                        for kc in range(8):
                            mm(bank(pg), wg[:, kc, j * 128:(j + 1) * 128], XNT[:, kc, :], kc == 0, kc == 7, [bg_] + bXNTt, [bPS[pg]])
                        sg = SGP[oc % 2]
                        act(sg, bank(pg), AF.Sigmoid, [bPS[pg]], [bSGP[oc % 2]])
                        dma(ga_d[oc], sg, [bSGP[oc % 2]], [B_gad[oc]])
                for oc in range(8):
                    DGC = DGCs[oc % 2]
                    bDGC = bDGCs[oc % 2]
                    for k in range(31):
                        ts(DGC[:, k, :], IDB, vcol(V_CDW + (l * 8 + oc) * 31 + k), None, ALU.mult, None, [bID, bVEC], [bDGC])
                    pc = 6 + (oc % 2)
                    for k in range(31):
                        mm(bank(pc), DGC[:, k, :], UT[:, oc, k:k + 512], k == 0, k == 30, [bDGC, bUT[oc]], [bPS[pc]])
                    act(CV[:, oc, :], bank(pc), AF.Identity, [bPS[pc], bVEC], [bCV[oc]], bias=vcol(V_CDB + l * 8 + oc))
                for oc in range(8):
                    mm(bank(0), ONESF, CV[:, oc, :], oc == 0, oc == 7, [bID, bCV[oc]], [bPS[0]])
                for oc in range(8):
                    sq = SQ[oc % 2]
                    act(sq, CV[:, oc, :], AF.Square, [bCV[oc]], [bSQ[oc % 2]])
                    mm(bank(1), ONESF, sq, oc == 0, oc == 7, [bID, bSQ[oc % 2]], [bPS[1]])
                act(MEAN, bank(0), AF.Copy, [bPS[0]], [bMEAN])
                act(SQ[0], bank(0), AF.Square, [bPS[0]], [bSQ[0]])
                tt(RSD, bank(1), SQ[0], ALU.subtract, [bPS[1], bSQ[0]], [bRSD])
                act(RSD, RSD, AF.Sqrt, [bRSD], [bRSD], bias=1e-5, scale=1.0)
                recip(RSD, RSD, [bRSD], [bRSD])
                for oc in range(8):
                    tt(CV[:, oc, :], CV[:, oc, :], MEAN, ALU.subtract, [bCV[oc], bMEAN], [bCV[oc]])
                    tt(CV[:, oc, :], CV[:, oc, :], RSD, ALU.mult, [bCV[oc], bRSD], [bCV[oc]])
                    act(CVN[:, oc, 0:512], CV[:, oc, :], AF.Silu, [bCV[oc], bVEC], [bUT[oc]],
                        bias=vcol(V_LNB + l * 8 + oc), scale=vcol(V_LNG + l * 8 + oc))
                for j2 in range(4):
                    wy, by_ = load_w(wl_co[:, :, j2 * 256:(j2 + 1) * 256], lambda t: t, [B_w["co"]])
                    wg, bg_ = load_w(wl_in[:, :, C_GC + j2 * 256:C_GC + (j2 + 1) * 256], lambda t: t, [B_w["in"]])
                    for j in range(2):
                        oc = j2 * 2 + j
                        py, pg = (0, 1) if oc % 2 == 0 else (2, 3)
                        for kc in range(8):
                            mm(bank(py), wy[:, kc, j * 128:(j + 1) * 128], CVN[:, kc, 0:512], kc == 0, kc == 7, [by_] + bUT, [bPS[py]])
                        for kc in range(8):
                            mm(bank(pg), wg[:, kc, j * 128:(j + 1) * 128], XNT[:, kc, :], kc == 0, kc == 7, [bg_] + bXNTt, [bPS[pg]])
                        sg = SGP[oc % 2]
                        act(sg, bank(pg), AF.Sigmoid, [bPS[pg]], [bSGP[oc % 2]])
                        mt_ = MCT[oc % 2]
                        tt(mt_, bank(py), sg, ALU.mult, [bPS[py], bSGP[oc % 2]], [bMCT[oc % 2]])
                        dma(mc_d[oc], mt_, [bMCT[oc % 2]], [B_mcd[oc]])

                P.alias(A_BUFS, P_BUFS)
                bYAT = bXNTt
                def emit_M1(qb):
                    for oc in range(8):
                        wv, wbuf = load_w(wl_ao[:, :, oc * 128:(oc + 1) * 128],
                                          lambda t: t.rearrange("p a b -> p (a b)")[0:64, :].rearrange("p (h c) -> p h c", c=128),
                                          [B_w["ao"]])
                        psb = 6 + (oc // 4) % 2
                        po = (oc % 4) * 128
                        for h in range(16):
                            mm(bank(psb, 128, po), wv[:, h, :], AT4[0:64, h, :], h == 0, h == 15, [wbuf, bAT4], [bPS[psb]])
                        if oc % 4 == 3:
                            o4 = oc - 3
                            cp(XNT[:, o4:o4 + 4, qb * 128:(qb + 1) * 128], bank(psb).rearrange("p (a b) -> p a b", b=128),
                               [bPS[psb]], [bYAT[qb]], eng="act")

                for qb in range(4 if 'A' not in os.environ.get('KSKIP', '') else 0):
                    ti = s * 4 + qb
                    L = (ti + 1) * 128
                    nch = (L + 511) // 512
                    for h in range(8):
                        ts(DG[:, h, :], IDB, WI[:, qb, h:h + 1], None, ALU.mult, None, [bID, bWI[qb]], [bDG])
                    for c in range(nch):
                        k0 = c * 512
                        w = min(512, L - k0)
                        pacc = 6 + (c % 2)
                        for hp in range(4):
                            pbk = [(0, 1), (2, 3), (4, 5)][(c * 4 + hp) % 3]
                            for j in range(2):
                                h = hp * 2 + j
                                mm(bank(pbk[j], w), QITZ[h % 2][:, h // 2, qb * 128:(qb + 1) * 128], KIT[:, k0:k0 + w],
                                   True, True, [bQIT[qb], bKIT], [bPS[pbk[j]]])
                            src = PS[:, pbk[0] * 512:pbk[0] * 512 + 1024].rearrange("p (a b) -> p a b", b=512)[:, :, 0:w]
                            act(RR_[:, hp * 2:hp * 2 + 2, 0:w], src, AF.Relu, [bPS[pbk[0]], bPS[pbk[1]]], [bRR[hp]])
                        for h in range(8):
                            mm(bank(pacc, w), DG[:, h, :], RR_[:, h, 0:w], h == 0, h == 7, [bDG, bRR[h // 2]], [bPS[pacc]])
                        cp(SC[:, k0:k0 + w], bank(pacc, w), [bPS[pacc]], [bSC])
                    if qb >= 1:
                        emit_M1(qb - 1)
                    P.op("dve", lambda e, L=L: e.tensor_reduce(out=SM[:, 0:1], in_=SC[:, 0:L], axis=AX.X, op=ALU.max,
                                                               apply_absolute_value=True), [bSC], [bSM])
                    if ti * 128 < PADN + NMETA:
                        pass
                    memset(SC[:, 0:PADN], NEG, [bSC])
                    tt(SC[:, L - 128:L], SC[:, L - 128:L], CAUS, ALU.add, [bSC, bID], [bSC])
                    amax, lo, mid, ge, tcm = SM[:, 0:1], SM[:, 1:2], SM[:, 2:3], SM[:, 4:5], SM[:, 6:7]
                    cnt = SM[:, 3:4]
                    sA = SM[:, 5:6]
                    STP = SM[:, 8:8 + NBIS]
                    La = (int(0.55 * L) // 128) * 128 if (L >= 1024 and 'H' not in os.environ.get('KSKIP', '')) else 0
                    ts(lo, amax, -1.0, -1e-6, ALU.mult, ALU.add, [bSM], [bSM])
                    ts(STP, VEC[:, V_BIS:V_BIS + NBIS], amax, 2.0, ALU.mult, ALU.mult, [bSM, bVEC], [bSM])
                    for it in range(NBIS if 'B' not in os.environ.get('KSKIP', '') else 0):
                        tt(mid, lo, STP[:, it:it + 1], ALU.add, [bSM], [bSM])
                        if La > 0:
                            act(JUNKA[:, 0:La], SC[:, 0:La], AF.Sign, [bSC, bSM], [bJA, bSMa], bias=mid, scale=-1.0, accum=sA)
                        ts(JUNK[:, 0:L - La], SC[:, La:L], mid, 0.0, ALU.is_ge, ALU.add, [bSC, bSM], bRR + [bSMc], accum=cnt)
                        if La > 0:
                            stt(tcm, sA, -0.5, cnt, ALU.mult, ALU.add, [bSMa, bSMc], [bSM])
                            ts(ge, tcm, KSEL - 0.5 - La / 2.0, None, ALU.is_ge, None, [bSM], [bSM])
                        else:
                            ts(ge, cnt, KSEL - 0.5, None, ALU.is_ge, None, [bSMc], [bSM])
                        stt(lo, ge, STP[:, it:it + 1], lo, ALU.mult, ALU.add, [bSM], [bSM])
                    ts(DLO, IDF, lo, None, ALU.mult, None, [bID, bSM], [bTHB])
                    mm(bank(7, 128), ONES1, DLO, True, True, [bID, bTHB], [bPS[7]])
                    cp(THB, bank(7, 128), [bPS[7]], [bTHB])
                    nblk = ti + 1
                    steps = []
                    for c in range(nch):
                        nb = min(4, nblk - c * 4)
                        for j in range(nb):
                            for g in range(4):
                                steps.append((c, j, c * 4 + j, g))

                    def issue_mask(c):
                        nb = min(4, nblk - c * 4)
                        pT = bank(7).rearrange("p (a b) -> p a b", b=128)
                        for j in range(nb):
                            kb = c * 4 + j
                            tr(pT[:, j, :], SC[:, kb * 128:(kb + 1) * 128], IDF, [bSC, bID], [bPS[7]])
                        mt = MT[c % 2]
                        tt(mt[:, 0:nb, :], pT[:, 0:nb, :], THB.unsqueeze(1).to_broadcast([128, nb, 128]), ALU.is_ge,
                           [bPS[7], bTHB], [bMT[c % 2]])

                    if 'S' in os.environ.get('KSKIP', ''):
                        steps = steps[:4]
                    issue_mask(0)
                    DEPTH = 2
                    for n in range(len(steps) + DEPTH):
                        if n < len(steps):
                            c, j, kb, g = steps[n]
                            if j == 0 and g == 0 and c + 1 < nch:
                                issue_mask(c + 1)
                            mt = MT[c % 2]
                            e_, gp = g % 2, g // 2
                            psb = 4 + (n % 3)
                            ptb = n % 3
                            mm(bank(psb), KT[:, gp, kb * 128:(kb + 1) * 128],
                               QTZ[e_][:, gp * 4:gp * 4 + 4, qb * 128:(qb + 1) * 128],
                               True, True, [bKT, bQT[qb]], [bPS[psb]])
                            act(PT[ptb], bank(psb), AF.Exp, [bPS[psb]], [bPT[ptb]])
                            pv4 = PT[ptb].rearrange("p (a b) -> p a b", b=128)
                            tt(pv4, pv4, mt[:, j, :].unsqueeze(1).to_broadcast([128, 4, 128]), ALU.mult,
                               [bPT[ptb], bMT[c % 2]], [bPT[ptb]], eng="dve")
                        if n >= DEPTH:
                            c2, j2, kb2, g2 = steps[n - DEPTH]
                            ptb2 = (n - DEPTH) % 3
                            mm(PS[0:65, g2 * 512:(g2 + 1) * 512], VV[:, kb2, g2, :], PT[ptb2], kb2 == 0, (kb2 == nblk - 1) or ('S' in os.environ.get('KSKIP', '')),
                               [bVV, bPT[ptb2]], [bPS[g2]])
                    for g in range(4):
                        RRW = RRWs[g % 2]
                        RBS = RRW
                        bRRW, bRBS = bRRWs[g % 2], bRBSs[g % 2]
                        ts(RRW[64:65, :], PS[64:65, g * 512:(g + 1) * 512], 1e-30, None, ALU.add, None, [bPS[g]], [bRRW])
                        recip(RRW[64:65, :], RRW[64:65, :], [bRRW], [bRRW])
                        psb = 4 + (g % 2)
                        mm(PS[0:64, psb * 512:(psb + 1) * 512], ONES1[64:65, 0:64], RRW[64:65, :], True, True, [bID, bRRW], [bPS[psb]])
                        act(RBS[0:64, :], PS[0:64, psb * 512:(psb + 1) * 512], AF.Copy, [bPS[psb]], [bRBS])
                        tt(AT4[0:64, g * 4:(g + 1) * 4, :], PS[0:64, g * 512:(g + 1) * 512].rearrange("p (a b) -> p a b", b=128),
                           RBS[0:64, :].rearrange("p (a b) -> p a b", b=128), ALU.mult, [bPS[g], bRBS], [bAT4])
                if 'A' not in os.environ.get('KSKIP', ''):
                    emit_M1(3)

                P.alias(F_BUFS, A_BUFS)
                for oc in range(8):
                    i2 = oc % 2
                    dma(GAT[i2], ga_d[oc], [B_gad[oc]], [bGAT[i2]])
                    dma(MCL[i2], mc_d[oc], [B_mcd[oc]], [bMCL[i2]])
                    tt(TMPF, XNT[:, oc, :], GAT[i2], ALU.mult, bYAT + [bGAT[i2]], [bTMPF])
                    tt(XNT[:, oc, :], TMPF, MCL[i2], ALU.add, [bTMPF, bMCL[i2]], bYAT)
                for c2 in range(2):
                    rfn, wbufs = load_w2(wl_o[:, :, c2 * 512:(c2 + 1) * 512], 8, [B_w["o"]])
                    for t4 in range(4):
                        for kc in range(8):
                            mm(bank(t4), XNT[:, kc, t4 * 128:(t4 + 1) * 128], rfn(kc), kc == 0, kc == 7,
                               [bYAT[t4]] + wbufs, [bPS[t4]])
                    for t4 in range(4):
                        i2 = t4 % 2
                        dma(HR[i2], h_in[p0 + t4 * 128:p0 + (t4 + 1) * 128, c2 * 512:(c2 + 1) * 512], [B_hin[s]], [bHR[i2]])
                        tt(HM[:, t4, c2 * 512:(c2 + 1) * 512], bank(t4), HR[i2], ALU.add, [bPS[t4], bHR[i2]], [bHM[t4]])
                if s == 0:
                    for t4 in range(4):
                        ts(HM[:, t4, :], HM[:, t4, :], vcol(V_RM0 + t4), None, ALU.mult, None, [bHM[t4], bVEC], [bHM[t4]])
                for t4 in range(4):
                    rmsnorm_to_T(HM[:, t4, :], bHM[t4], XS2, bXS2, SS2, bSS2, V_FNG + l * 8, XNT, bXNTt[t4], t4, 4 + (t4 % 2))
                for (c0g, ng) in ([(0, 8), (8, 8), (16, 6)] if 'F' not in os.environ.get('KSKIP', '') else []):
                    for cpair in range(ng // 2):
                        cA = c0g + cpair * 2
                        wg, bwg = load_w(wl_up[:, :, cA * 128:cA * 128 + 256], lambda t: t, [B_w["up"]])
                        wu, bwu = load_w(wl_up[:, :, FH + cA * 128:FH + cA * 128 + 256], lambda t: t, [B_w["up"]])
                        for j in range(2):
                            c = cA + j
                            res = []
                            for (which, wv_, wb_) in ((0, wg, bwg), (1, wu, bwu)):
                                psb = which * 2 + (c % 2)
                                cidx = c + which * 22
                                for kc in range(8):
                                    mm(bank(psb), wv_[:, kc, j * 128:(j + 1) * 128], XNT[:, kc, :], kc == 0, kc == 7,
                                       [wb_] + bXNTt, [bPS[psb]])
                                raw = RAW[which][c % 2]
                                braw = bRAW[which][c % 2]
                                act(raw[:, 2:514], bank(psb), AF.Copy, [bPS[psb]], [braw])
                                cp(raw[:, 0:2], FHL[:, cidx, :], [bFHL], [braw])
                                cg = CGU[which]
                                wofs = V_FDW + (l * 44 + cidx) * 3
                                ts(cg, raw[:, 2:514], vcol(wofs + 2), vcol(V_FDB + l * 44 + cidx), ALU.mult, ALU.add,
                                   [braw, bVEC], [bCGU[which]])
                                stt(cg, raw[:, 1:513], vcol(wofs + 1), cg, ALU.mult, ALU.add, [braw, bVEC, bCGU[which]], [bCGU[which]])
                                stt(cg, raw[:, 0:512], vcol(wofs + 0), cg, ALU.mult, ALU.add, [braw, bVEC, bCGU[which]], [bCGU[which]])
                                cp(FHL[:, cidx, :], raw[:, 512:514], [braw], [bFHL])
                            act(SGF, CGU[0], AF.Silu, [bCGU[0]], [bSGF])
                            tt(ACTT[:, c - c0g, :], SGF, CGU[1], ALU.mult, [bSGF, bCGU[1]], [bACTT])
                    for c2 in range(2):
                        rfn, wbufs = load_w2(wl_dn[:, c0g:c0g + ng, c2 * 512:(c2 + 1) * 512], ng, [B_w["dn"]])
                        for t4 in range(4):
                            psb = 4 + t4
                            for kc in range(ng):
                                mm(bank(psb), ACTT[:, kc, t4 * 128:(t4 + 1) * 128], rfn(kc), kc == 0, kc == ng - 1,
                                   [bACTT] + wbufs, [bPS[psb]])
                        for t4 in range(4):
                            tt(HM[:, t4, c2 * 512:(c2 + 1) * 512], HM[:, t4, c2 * 512:(c2 + 1) * 512], bank(4 + t4), ALU.add,
                               [bPS[4 + t4], bHM[t4]], [bHM[t4]])
                if last:
                    dma(FING, fing_d, [B_ext], [bFING])
                for t4 in range(4):
                    if s == 0:
                        ts(HM[:, t4, :], HM[:, t4, :], vcol(V_RM0 + t4), None, ALU.mult, None, [bHM[t4], bVEC], [bHM[t4]])
                    rows = slice(p0 + t4 * 128, p0 + (t4 + 1) * 128)
                    if not last:
                        dma(hA[rows, :], HM[:, t4, :], [bHM[t4]], [B_hA[s]])
                    elif s >= 1:
                        act(XS2, HM[:, t4, :], AF.Square, [bHM[t4]], [bXS2, bSS2], accum=SS2[:, 0:1])
                        act(SS2[:, 1:2], SS2[:, 0:1], AF.Sqrt, [bSS2], [bSS2], bias=1e-6, scale=1.0 / 1024.0)
                        recip(SS2[:, 2:3], SS2[:, 1:2], [bSS2], [bSS2])
                        ts(HM[:, t4, :], HM[:, t4, :], SS2[:, 2:3], None, ALU.mult, None, [bHM[t4], bSS2], [bHM[t4]])
                        tt(HM[:, t4, :], HM[:, t4, :], FING, ALU.mult, [bHM[t4], bFING], [bHM[t4]])
                        dma(out_d[p0 - 512 + t4 * 128:p0 - 512 + (t4 + 1) * 128, :], HM[:, t4, :], [bHM[t4]], [B_out])
        P.emit(final_bufs=[B_out, B_dbg])
    return nc


def host_consts(NS):
    TP = NS * 512
    pos = (np.arange(TP) - PADN).astype(np.float32)
    inv_freq = np.power(np.float32(500000.0), -np.arange(0, 16, 2, dtype=np.float32) / np.float32(16)).astype(np.float32)
    ang = (pos[:, None] * inv_freq[None, :]).astype(np.float32)
    cs = np.concatenate([np.cos(ang), np.sin(ang)], axis=1).astype(np.float32)
    t = np.arange(128)
    caus = np.where(t[None, :] <= t[:, None], 0.0, NEG).astype(np.float32)
    ident = np.eye(128, dtype=np.float32)
    return cs, caus, ident


def pack_vec(inp):
    vec = np.zeros((128, NV), np.float32)

    def pp(a):
        a = np.asarray(a, np.float32)
        lead = a.shape[:-1]
        n = a.shape[-1] // 128
        return np.moveaxis(a.reshape(*lead, n, 128), -1, 0)

    vec[:, V_ANG:V_ANG + 16] = pp(inp["attn_norm_g"]).reshape(128, 16)
    vec[:, V_FNG:V_FNG + 16] = pp(inp["ffn_norm_g"]).reshape(128, 16)
    vec[:, V_CDB:V_CDB + 16] = pp(inp["conv_dw_b"]).reshape(128, 16)
    vec[:, V_LNG:V_LNG + 16] = pp(inp["conv_ln_g"]).reshape(128, 16)
    vec[:, V_LNB:V_LNB + 16] = pp(inp["conv_ln_b"]).reshape(128, 16)
    cdw = pp(inp["conv_dw_w"])
    vec[:, V_CDW:V_CDW + 496] = np.transpose(cdw, (0, 1, 3, 2)).reshape(128, 496)
    fdw = pp(inp["ffn_dw_w"])
    vec[:, V_FDW:V_FDW + 264] = np.transpose(fdw, (0, 1, 3, 2)).reshape(128, 264)
    vec[:, V_FDB:V_FDB + 88] = pp(inp["ffn_dw_b"]).reshape(128, 88)
    rm = np.ones((128, 4), np.float32)
    p = np.arange(512).reshape(4, 128).T
    rm[p < PADN] = 0.0
    vec[:, V_RM0:V_RM0 + 4] = rm
    vec[:, V_BIS:V_BIS + NBIS] = (2.0 ** -(np.arange(NBIS) + 1.0)).astype(np.float32)[None, :]
    return vec


def make_in_map(inp, b, NS):
    TP = NS * 512
    nreal = TP - 512
    xpad = np.zeros((TP, D), np.float32)
    xpad[PADN:PADN + NMETA] = np.asarray(inp["meta_tokens"], np.float32)
    xpad[512:] = np.asarray(inp["x"][b, :nreal], np.float32)
    cs, caus, ident = host_consts(NS)
    m = {
        "xp": xpad, "cs": cs, "vec": pack_vec(inp), "caus": caus, "ident": ident,
        "fing": np.ascontiguousarray(np.broadcast_to(np.asarray(inp["final_norm_g"], np.float32)[None, :], (128, D))),
        "w_in": np.asarray(inp["w_in"], np.float32), "w_attn_out": np.asarray(inp["w_attn_out"], np.float32),
        "w_conv_out": np.asarray(inp["w_conv_out"], np.float32), "w_o": np.asarray(inp["w_o"], np.float32),
        "w_up": np.asarray(inp["w_up"], np.float32), "w_down": np.asarray(inp["w_down"], np.float32),
    }
    return m


_NC_CACHE = {}


def kernel(**inputs):
    NS = 17
    if NS not in _NC_CACHE:
        _NC_CACHE[NS] = build(NS)
    nc = _NC_CACHE[NS]
    B = inputs["x"].shape[0]
    in_maps = [make_in_map(inputs, (c // 2) % B, NS) for c in range(8)]
    res = run_bass_kernel_spmd(nc, in_maps, core_ids=list(range(8)))
    out = np.stack([res.results[2 * b]["out"] for b in range(B)], axis=0)
    return out.astype(np.float32)
```

```python
from contextlib import ExitStack
import os
import numpy as np
import concourse.bass as bass
import concourse.mybir as mybir
from concourse.bass_utils import run_bass_kernel_spmd

F32 = mybir.dt.float32
BF16 = mybir.dt.bfloat16
U8 = mybir.dt.uint8
AF = mybir.ActivationFunctionType
ALU = mybir.AluOpType
AX = mybir.AxisListType

EPOCH = 4096
NDMA_SLOTS = 12


class Buf:
    __slots__ = ("name", "w", "r")

    def __init__(self, name):
        self.name = name
        self.w = {}
        self.r = {}


def _merge(d, s):
    for k, v in s.items():
        if d.get(k, 0) < v:
            d[k] = v


class Prog:
    ENGS = ("pe", "act", "dve", "pool", "sp")

    def __init__(self, nc):
        self.nc = nc
        self.q = {e: [] for e in self.ENGS}
        self.count = {e: 0 for e in self.ENGS}
        self.known = {e: {} for e in self.ENGS}
        self.dma_next = {"sp": 0, "pool": 0}
        self.dma_uses = {}
        self.semkeys = set()

    def alias(self, new_bufs, old_bufs):
        d = {}
        for b in old_bufs:
            _merge(d, b.w)
            _merge(d, b.r)
        for b in new_bufs:
            _merge(b.w, d)

    def _add(self, eng, fn, reads, writes, tok, inc, extra_waits=()):
        deps = {}
        for b in reads:
            _merge(deps, b.w)
        for b in writes:
            _merge(deps, b.w)
            _merge(deps, b.r)
        for k, v in extra_waits:
            if deps.get(k, 0) < v:
                deps[k] = v
        waits = []
        kn = self.known[eng]
        for k, v in deps.items():
            if eng == "pe" and k[0] == "pe":
                continue
            if kn.get(k, 0) >= v:
                continue
            kn[k] = v
            waits.append((k, v))
        semkey, val = tok
        self.semkeys.add(semkey)
        for b in writes:
            b.w = {semkey: val}
            b.r = {}
        for b in reads:
            if b not in writes:
                if b.r.get(semkey, 0) < val:
                    b.r[semkey] = val
        self.q[eng].append((waits, fn, semkey, inc))

    def op(self, eng, fn, reads=(), writes=()):
        idx = self.count[eng]
        self.count[eng] += 1
        tok = ((eng, idx // EPOCH), idx % EPOCH + 1)
        self._add(eng, fn, reads, writes, tok, 1)

    def dma(self, queue, fn, reads=(), writes=()):
        slot = self.dma_next[queue]
        self.dma_next[queue] = (slot + 1) % NDMA_SLOTS
        semkey = ("dma" + queue, slot)
        prev = self.dma_uses.get(semkey, 0)
        val = prev + 16
        self.dma_uses[semkey] = val
        extra = [(semkey, prev)] if prev > 0 else []
        self._add(queue, fn, reads, writes, (semkey, val), 16, extra)

    def emit(self, final_bufs=()):
        nc = self.nc
        deps = {}
        for b in final_bufs:
            _merge(deps, b.w)
        fw = list(deps.items())
        with ExitStack() as es:
            sems = {}
            for k in sorted(self.semkeys, key=str):
                nm = "s_" + "_".join(str(x) for x in k)
                sems[k] = es.enter_context(nc.semaphore(nm))
            block = es.enter_context(nc.Block())
            q = self.q

            def body_for(engname):
                def body(e):
                    for waits, fn, semkey, inc in q[engname]:
                        for (k, v) in waits:
                            e.wait_ge(sems[k], v)
                        inst = fn(e)
                        inst.then_inc(sems[semkey], inc)
                    if engname == "sp":
                        for (k, v) in fw:
                            e.wait_ge(sems[k], v)
                return body

            block.tensor(body_for("pe"))
            block.scalar(body_for("act"))
            block.vector(body_for("dve"))
            block.gpsimd(body_for("pool"))
            block.sync(body_for("sp"))


D = 1024
NCOL = 6216
FH = 2816
PADN = 496
NMETA = 16
C_Q, C_K, C_V, C_QI, C_KI, C_WI, C_A, C_B, C_GA, C_GC = 0, 1024, 1280, 1536, 2048, 2112, 2120, 3144, 4168, 5192
NBIS = 16
NEG = -1.0e30

V_ANG = 0
V_FNG = 16
V_CDB = 32
V_LNG = 48
V_LNB = 64
V_CDW = 80
V_FDW = 576
V_FDB = 840
V_RM0 = 928
V_BIS = 932
NV = 960


def build(NS, NL=2, dbg=None):
    TP = NS * 512
    KSEL = min(256, (NMETA + TP - 512) // 4)
    NT = NS * 4
    nc = bass.Bass("TRN2", target_bir_lowering=False)
    P = Prog(nc)

    def din(name, shape, dt=F32):
        return nc.dram_tensor(name, list(shape), dt, kind="ExternalInput").ap()

    def dint(name, shape, dt=F32):
        return nc.dram_tensor(name, list(shape), dt, kind="Internal").ap()

    xp = din("xp", [TP, D])
    cs_d = din("cs", [TP, 16])
    vec_d = din("vec", [128, NV])
    caus_d = din("caus", [128, 128])
    ident_d = din("ident", [128, 128])
    fing_d = din("fing", [128, D])
    w_in = din("w_in", [2, D, NCOL])
    w_ao = din("w_attn_out", [2, D, D])
    w_co = din("w_conv_out", [2, D, D])
    w_o = din("w_o", [2, D, D])
    w_up = din("w_up", [2, D, 2 * FH])
    w_dn = din("w_down", [2, FH, D])
    out_d = nc.dram_tensor("out", [TP - 512, D], F32, kind="ExternalOutput").ap()

    b_in = dint("b_in", [2, D, NCOL], BF16)
    b_ao = dint("b_ao", [2, D, D], BF16)
    b_co = dint("b_co", [2, D, D], BF16)
    b_o = dint("b_o", [2, D, D], BF16)
    b_up = dint("b_up", [2, D, 2 * FH], BF16)
    b_dn = dint("b_dn", [2, FH, D], BF16)
    hA = dint("hA", [TP, D])
    ga_d = dint("ga_s", [8, 128, 512])
    mc_d = dint("mc_s", [8, 128, 512])

    dbg_out = {}
    if dbg:
        for name, shape in dbg.items():
            dbg_out[name] = nc.dram_tensor("dbg_" + name, list(shape), F32, kind="ExternalOutput").ap()

    B_w = {k: Buf("w_" + k) for k in ["in", "ao", "co", "o", "up", "dn"]}
    B_hA = [Buf("hA%d" % s) for s in range(NS)]
    B_gad = [Buf("gad%d" % i) for i in range(8)]
    B_mcd = [Buf("mcd%d" % i) for i in range(8)]
    B_out = Buf("out")
    B_dbg = Buf("dbg")
    B_ext = Buf("ext")

    es = ExitStack()
    with es:
        def sb(name, shape, dt):
            return es.enter_context(nc.sbuf_tensor(name, list(shape), dt))[:]

        KT = sb("KT", [128, 2, TP], BF16)
        VV = sb("VV", [128, NT, 4, 65], BF16)
        KIT = sb("KIT", [128, TP], BF16)
        VEC = sb("VEC", [128, NV], F32)
        IDF = sb("IDF", [128, 128], F32)
        IDB = sb("IDB", [128, 128], BF16)
        CAUS = sb("CAUS", [128, 128], F32)
        ONESF = sb("ONESF", [128, 128], F32)
        ONES1 = sb("ONES1", [128, 128], F32)
        XNT = sb("XNT", [128, 8, 512], BF16)
        QTZ = [sb("QTZ%d" % i, [128, 8, 512], BF16) for i in range(2)]
        QITZ = [sb("QITZ%d" % i, [128, 4, 512], BF16) for i in range(2)]
        WI = sb("WI", [128, 4, 8], F32)
        UH = sb("UH", [128, 8, 30], BF16)
        FHL = sb("FHL", [128, 44, 2], F32)
        NWB = 4
        WBA = sb("WBA", [128, NWB, 8, 256], BF16)
        WBs = [WBA[:, i] for i in range(NWB)]
        ARW = 16640
        ARENA = sb("ARENA", [128, ARW], F32)
        PS = es.enter_context(nc.psum_tensor("PS", [128, 4096], F32))[:]

        bKT, bVV, bKIT, bVEC, bID, bXNT, bUH, bFHL = [Buf(n) for n in "KT VV KIT VEC ID XNT UH FHL".split()]
        bQT = [Buf("QT%d" % i) for i in range(4)]
        bQIT = [Buf("QIT%d" % i) for i in range(4)]
        bWI = [Buf("WI%d" % i) for i in range(4)]
        bXNTt = [Buf("XNT%d" % i) for i in range(4)]
        bWB = [Buf("WB%d" % i) for i in range(NWB)]
        bPS = [Buf("PS%d" % i) for i in range(8)]
        wb_rr = [0]

        def bank(i, w=512, off=0):
            return PS[:, i * 512 + off:i * 512 + off + w]

        def bank_bf(i):
            return PS[:, i * 512:(i + 1) * 512].bitcast(BF16)

        def mm(out, lhsT, rhs, st, sp_, R, W):
            P.op("pe", lambda e: e.matmul(out, lhsT=lhsT, rhs=rhs, start=st, stop=sp_), R, W)

        def tr(out, in_, ident, R, W):
            P.op("pe", lambda e: e.transpose(out=out, in_=in_, identity=ident), R, W)

        def act(out, in_, func, R, W, bias=None, scale=None, accum=None):
            kw = {}
            if bias is not None:
                kw["bias"] = bias
            if scale is not None:
                kw["scale"] = scale
            if accum is not None:
                kw["accum_out"] = accum
            P.op("act", lambda e: e.activation(out=out, in_=in_, func=func, **kw), R, W)

        def ts(out, in0, s1, s2, op0, op1, R, W, accum=None, eng="dve"):
            if op1 is None:
                P.op(eng, lambda e: e.tensor_scalar(out=out, in0=in0, scalar1=s1, scalar2=None, op0=op0), R, W)
            elif accum is None:
                P.op(eng, lambda e: e.tensor_scalar(out=out, in0=in0, scalar1=s1, scalar2=s2, op0=op0, op1=op1), R, W)
            else:
                P.op(eng, lambda e: e.tensor_scalar(out=out, in0=in0, scalar1=s1, scalar2=s2, op0=op0, op1=op1, accum_out=accum), R, W)

        def tt(out, in0, in1, op, R, W, eng="dve"):
            P.op(eng, lambda e: e.tensor_tensor(out=out, in0=in0, in1=in1, op=op), R, W)

        def stt(out, in0, scalar, in1, op0, op1, R, W):
            P.op("dve", lambda e: e.scalar_tensor_tensor(out=out, in0=in0, scalar=scalar, in1=in1, op0=op0, op1=op1), R, W)

        def cp(out, in_, R, W, eng="dve"):
            if eng == "act":
                P.op(eng, lambda e: e.activation(out=out, in_=in_, func=AF.Copy), R, W)
            else:
                P.op(eng, lambda e: e.tensor_copy(out=out, in_=in_), R, W)

        def recip(out, in_, R, W):
            P.op("dve", lambda e: e.reciprocal(out=out, in_=in_), R, W)

        def memset(ap, val, W, eng="dve"):
            P.op(eng, lambda e: e.memset(ap, val), [], W)

        def dma(out, in_, R, W, queue="sp", maxlast=None):
            if maxlast is None:
                P.dma(queue, lambda e: e.dma_start(out=out, in_=in_), R, W)
            else:
                P.dma(queue, lambda e: e.dma_start(out=out, in_=in_, max_dma_last_dim=maxlast), R, W)

        def load_w(src_ap, shape_view, R):
            i = wb_rr[0]
            wb_rr[0] = (i + 1) % NWB
            v = shape_view(WBs[i])
            dma(v, src_ap, R, [bWB[i]])
            return v, bWB[i]

        def load_w2(src_ap, nk, R):
            if wb_rr[0] % 2 == 1:
                wb_rr[0] = (wb_rr[0] + 1) % NWB
            i = wb_rr[0]
            wb_rr[0] = (i + 2) % NWB
            dma(WBA[:, i, 0:nk, :], src_ap[:, :, 0:256], R, [bWB[i]])
            dma(WBA[:, i + 1, 0:nk, :], src_ap[:, :, 256:512], R, [bWB[i + 1]])
            return (lambda kc: WBA[:, i:i + 2, kc, :]), [bWB[i], bWB[i + 1]]

        def dump(name, ap_sb, R, rows=None):
            if name in dbg_out:
                dst = dbg_out[name]
                dma(dst, ap_sb, R, [B_dbg])

        def carve(off, nwords):
            assert off + nwords <= ARW, (off, nwords, ARW)
            return ARENA[:, off:off + nwords]

        dma(VEC, vec_d, [B_ext], [bVEC])
        dma(IDF, ident_d, [B_ext], [bID])
        dma(CAUS, caus_d, [B_ext], [bID])
        cp(IDB, IDF, [bID], [bID])
        memset(ONESF, 1.0 / 1024.0, [bID])
        memset(ONES1, 1.0, [bID])
        for _i in range(2):
            memset(QTZ[_i], 0.0, bQT)
            memset(QITZ[_i], 0.0, bQIT)
        memset(VV, 1.0, [bVV])
        for (src, dst, key, rows) in [(w_in, b_in, "in", D), (w_ao, b_ao, "ao", D), (w_co, b_co, "co", D),
                                      (w_o, b_o, "o", D), (w_up, b_up, "up", D), (w_dn, b_dn, "dn", FH)]:
            for l in range(NL):
                for r0 in range(0, rows, 128):
                    dma(dst[l, r0:r0 + 128, :], src[l, r0:r0 + 128, :], [B_ext], [B_w[key]], queue="pool", maxlast=4096)

        o = 0
        HT = carve(o, 1024); o += 1024
        XS = carve(o, 512).bitcast(BF16); o += 512
        TM = [carve(o, 512), carve(o + 512, 512)]; o += 1024
        TMB = carve(o, 256).bitcast(BF16); o += 256
        CS = carve(o, 16); o += 16
        RT = carve(o, 384).rearrange("p (a b) -> p a b", b=128); o += 384
        SSv = carve(o, 8); o += 8
        UT = carve(o, 8 * 272).bitcast(BF16).rearrange("p (a b) -> p a b", b=544); o += 8 * 272
        CV = carve(o, 4096).rearrange("p (a b) -> p a b", b=512); o += 4096
        SGP = [carve(o, 512), carve(o + 512, 512)]; o += 1024
        MEAN = carve(o, 512); o += 512
        RSD = carve(o, 512); o += 512
        SQ = [carve(o, 512), carve(o + 512, 512)]; o += 1024
        DGCs = []
        for _i in range(2):
            DGCs.append(carve(o, 31 * 64).bitcast(BF16).rearrange("p (a b) -> p a b", b=128)); o += 31 * 64
        MCT = SQ
        P_END = o
        CVN = UT
        bHT, bXS, bTMB, bCS, bRT, bSS, bMEAN, bRSD = [Buf(n) for n in "HT XS TMB CS RT SS MEAN RSD".split()]
        bDGCs = [Buf("DGC0"), Buf("DGC1")]
        bTM = [Buf("TM0"), Buf("TM1")]
        bUT = [Buf("UT%d" % i) for i in range(8)]
        bCV = [Buf("CV%d" % i) for i in range(8)]
        bSGP = [Buf("SGP0"), Buf("SGP1")]
        bSQ = [Buf("SQ0"), Buf("SQ1")]
        bMCT = bSQ
        P_BUFS = [bHT, bXS, bTMB, bCS, bRT, bSS, bMEAN, bRSD] + bDGCs + bTM + bUT + bCV + bSGP + bSQ
        o = 0
        SC = carve(o, TP); o += TP
        RR_ = carve(o, 2048).bitcast(BF16).rearrange("p (a b) -> p a b", b=512); o += 2048
        JUNK = carve(o - 2048, (TP + 3) // 4).bitcast(U8)
        if (TP + 3) // 4 > 2048:
            o = o - 2048 + (TP + 3) // 4
        PT = [carve(o + i * 256, 256).bitcast(BF16) for i in range(3)]; o += 768
        MT = [carve(o + i * 256, 256).bitcast(BF16).rearrange("p (a b) -> p a b", b=128) for i in range(2)]; o += 512
        DG = carve(o, 512).bitcast(BF16).rearrange("p (a b) -> p a b", b=128); o += 512
        THB = carve(o, 128); o += 128
        DLO = carve(o, 128); o += 128
        AT4 = carve(o, 1024).bitcast(BF16).rearrange("p (a b) -> p a b", b=128); o += 1024
        RRWs = [carve(o, 512), carve(o + 512, 512)]; o += 1024
        JUNKA = carve(o, 1408).bitcast(U8); o += 1408
        SM = carve(o, 64); o += 64
        A_END = o
        bSC = Buf("SC")
        bRR = [Buf("RR%d" % i) for i in range(4)]
        bPT = [Buf("PT%d" % i) for i in range(3)]
        bMT = [Buf("MT0"), Buf("MT1")]
        bDG, bTHB, bAT4, bSM, bSMc, bSMa, bJA = [Buf(n) for n in "DG THB AT4 SM SMc SMa JA".split()]
        bRRWs = [Buf("RRW0"), Buf("RRW1")]
        bRBSs = [Buf("RBS0"), Buf("RBS1")]
        A_BUFS = [bSC] + bRR + bPT + bMT + [bDG, bTHB, bAT4, bSM, bSMc, bSMa, bJA] + bRRWs + bRBSs
        o = 0
        HM = carve(o, 4096).rearrange("p (a b) -> p a b", b=1024); o += 4096
        HR = [carve(o, 512), carve(o + 512, 512)]; o += 1024
        GAT = [carve(o, 512), carve(o + 512, 512)]; o += 1024
        MCL = [carve(o, 512), carve(o + 512, 512)]; o += 1024
        TMPF = carve(o, 512); o += 512
        XS2 = carve(o, 512).bitcast(BF16); o += 512
        RAW = [[carve(o + (2 * a + b) * 520, 520) for b in range(2)] for a in range(2)]; o += 2080
        CGU = [carve(o, 512), carve(o + 512, 512)]; o += 1024
        SGF = carve(o, 512); o += 512
        ACTT = carve(o, 2048).bitcast(BF16).rearrange("p (a b) -> p a b", b=512); o += 2048
        SS2 = carve(o, 8); o += 8
        FING = carve(o, 1024); o += 1024
        F_END = o
        bHM = [Buf("HM%d" % i) for i in range(4)]
        bHR = [Buf("HR0"), Buf("HR1")]
        bGAT = [Buf("GAT0"), Buf("GAT1")]
        bMCL = [Buf("MCL0"), Buf("MCL1")]
        bRAW = [[Buf("RAW%d%d" % (a, b)) for b in range(2)] for a in range(2)]
        bCGU = [Buf("CGU0"), Buf("CGU1")]
        bTMPF, bXS2, bSGF, bACTT, bSS2, bFING = [Buf(n) for n in "TMPF XS2 SGF ACTT SS2 FING".split()]
        F_BUFS = bHM + bHR + bGAT + bMCL + bRAW[0] + bRAW[1] + bCGU + [bTMPF, bXS2, bSGF, bACTT, bSS2, bFING]
        assert max(P_END, A_END, F_END) <= ARW, (P_END, A_END, F_END)

        def vcol(off, n=1):
            return VEC[:, off:off + n]


        def rmsnorm_to_T(src_tile, bsrc, xs, bxs, ssv, bss, g_off, dst_T, bdst, tt_i, psb):
            act(xs, src_tile, AF.Square, [bsrc], [bxs, bss], accum=ssv[:, 0:1])
            act(ssv[:, 1:2], ssv[:, 0:1], AF.Sqrt, [bss], [bss], bias=1e-6, scale=1.0 / 1024.0)
            recip(ssv[:, 2:3], ssv[:, 1:2], [bss], [bss])
            ts(xs, src_tile, ssv[:, 2:3], None, ALU.mult, None, [bsrc, bss], [bxs])
            pv = bank_bf(psb).rearrange("p (a b) -> p a b", b=128)
            for kc in range(8):
                tr(pv[:, kc, :], xs[:, kc * 128:(kc + 1) * 128], IDB, [bxs, bID], [bPS[psb]])
            gb = VEC[:, g_off:g_off + 8].unsqueeze(2).to_broadcast([128, 8, 128])
            tt(dst_T[:, :, tt_i * 128:(tt_i + 1) * 128], pv, gb, ALU.mult, [bPS[psb], bVEC], [bdst])

        def rope(v, nh, bv, cs):
            x1 = v[:, :, 0:8]
            x2 = v[:, :, 8:16]
            c = cs[:, 0:8].unsqueeze(1).to_broadcast([128, nh, 8])
            s = cs[:, 8:16].unsqueeze(1).to_broadcast([128, nh, 8])
            ta = RT[:, 0, 0:nh * 8].rearrange("p (a b) -> p a b", b=8)
            tb = RT[:, 1, 0:nh * 8].rearrange("p (a b) -> p a b", b=8)
            tc_ = RT[:, 2, 0:nh * 8].rearrange("p (a b) -> p a b", b=8)
            tt(ta, x1, c, ALU.mult, [bv, bCS], [bRT])
            tt(tb, x2, s, ALU.mult, [bv, bCS], [bRT])
            tt(tc_, x1, s, ALU.mult, [bv, bCS], [bRT])
            tt(x1, ta, tb, ALU.subtract, [bRT], [bv])
            tt(x2, x2, c, ALU.mult, [bv, bCS], [bv])
            tt(x2, x2, tc_, ALU.add, [bv, bRT], [bv])

        for l in range(NL):
            h_in = xp if l == 0 else hA
            B_hin = [B_ext] * NS if l == 0 else B_hA
            last = (l == NL - 1)
            memset(UH, 0.0, [bUH])
            memset(FHL, 0.0, [bFHL])
            wl_in = b_in[l].rearrange("(kc p) n -> p kc n", p=128)
            wl_co = b_co[l].rearrange("(kc p) n -> p kc n", p=128)
            wl_o = b_o[l].rearrange("(kc p) n -> p kc n", p=128)
            wl_up = b_up[l].rearrange("(kc p) n -> p kc n", p=128)
            wl_dn = b_dn[l].rearrange("(kc p) n -> p kc n", p=128)
            wl_ao = b_ao[l].rearrange("(h d) n -> d h n", d=64)

            for s in range(NS):
                p0 = s * 512
                P.alias(P_BUFS, F_BUFS + A_BUFS)
                for t4 in range(4):
                    dma(HT, h_in[p0 + t4 * 128:p0 + (t4 + 1) * 128, :], [B_hin[s]], [bHT])
                    rmsnorm_to_T(HT, bHT, XS, bXS, SSv, bSS, V_ANG + l * 8, XNT, bXNTt[t4], t4, 6 + (t4 % 2))
                chunks = [("q", C_Q, 512), ("q", C_Q + 512, 512), ("kv", C_K, 512), ("qi", C_QI, 512), ("kw", C_KI, 72)]
                for ci, (kind, c0, wd) in enumerate(chunks):
                    if wd == 512:
                        rfn, wbufs = load_w2(wl_in[:, :, c0:c0 + 512], 8, [B_w["in"]])
                        for t4 in range(4):
                            for kc in range(8):
                                mm(bank(t4), XNT[:, kc, t4 * 128:(t4 + 1) * 128], rfn(kc), kc == 0, kc == 7,
                                   [bXNTt[t4]] + wbufs, [bPS[t4]])
                    else:
                        wv, wbuf = load_w(wl_in[:, :, c0:c0 + wd], lambda t, hw=wd: t[:, :, 0:hw], [B_w["in"]])
                        for t4 in range(4):
                            for kc in range(8):
                                mm(bank(t4, wd, 0), XNT[:, kc, t4 * 128:(t4 + 1) * 128], wv[:, kc, :], kc == 0, kc == 7,
                                   [bXNTt[t4], wbuf], [bPS[t4]])
                    for t4 in range(4):
                        pos = p0 + t4 * 128
                        tile_i = s * 4 + t4
                        tm = TM[t4 % 2]
                        btm = bTM[t4 % 2]
                        psb = 4 + (t4 % 2)
                        pvT = bank_bf(psb).rearrange("p (a b) -> p a b", b=128)
                        if kind == "q":
                            src = bank(t4).rearrange("p (a b d) -> p a b d", a=2, b=4)
                            dstv = tm.rearrange("p (b a d) -> p a b d", a=2, b=4)
                            act(dstv, src, AF.Copy, [bPS[t4]], [btm], scale=0.125)
                            dma(CS, cs_d[pos:pos + 128, :], [B_ext], [bCS])
                            rope(tm.rearrange("p (h d) -> p h d", d=64), 8, btm, CS)
                            cp(TMB, tm, [btm], [bTMB])
                            for j in range(4):
                                tr(pvT[:, j, :], TMB[:, j * 128:(j + 1) * 128], IDB, [bTMB, bID], [bPS[psb]])
                            blk0 = (c0 // 512) * 4
                            cp(QTZ[0][0:64, blk0:blk0 + 4, t4 * 128:(t4 + 1) * 128], pvT[0:64, 0:4, :], [bPS[psb]], [bQT[t4]], eng="act")
                            cp(QTZ[1][64:128, blk0:blk0 + 4, t4 * 128:(t4 + 1) * 128], pvT[64:128, 0:4, :], [bPS[psb]], [bQT[t4]], eng="act")
                        elif kind == "kv":
                            act(tm[:, 0:256], bank(t4, 256, 0), AF.Copy, [bPS[t4]], [btm])
                            act(VV[:, tile_i, :, 0:64], bank(t4, 256, 256).rearrange("p (g d) -> p g d", d=64), AF.Copy,
                                [bPS[t4]], [bVV])
                            dma(CS, cs_d[pos:pos + 128, :], [B_ext], [bCS])
                            rope(tm[:, 0:256].rearrange("p (h d) -> p h d", d=64), 4, btm, CS)
                            cp(TMB[:, 0:256], tm[:, 0:256], [btm], [bTMB])
                            for j in range(2):
                                tr(pvT[:, j, :], TMB[:, j * 128:(j + 1) * 128], IDB, [bTMB, bID], [bPS[psb]])
                            cp(KT[:, :, pos:pos + 128], pvT[:, 0:2, :], [bPS[psb]], [bKT], eng="act")
                        elif kind == "qi":
                            act(tm, bank(t4), AF.Copy, [bPS[t4]], [btm])
                            dma(CS, cs_d[pos:pos + 128, :], [B_ext], [bCS])
                            rope(tm.rearrange("p (h d) -> p h d", d=64), 8, btm, CS)
                            cp(TMB, tm, [btm], [bTMB])
                            for j in range(4):
                                tr(pvT[:, j, :], TMB[:, j * 128:(j + 1) * 128], IDB, [bTMB, bID], [bPS[psb]])
                            cp(QITZ[0][0:64, :, t4 * 128:(t4 + 1) * 128], pvT[0:64, 0:4, :], [bPS[psb]], [bQIT[t4]], eng="act")
                            cp(QITZ[1][64:128, :, t4 * 128:(t4 + 1) * 128], pvT[64:128, 0:4, :], [bPS[psb]], [bQIT[t4]], eng="act")
                        else:
                            act(tm[:, 0:64], bank(t4, 64, 0), AF.Copy, [bPS[t4]], [btm])
                            act(tm[:, 64:128], bank(t4, 64, 0), AF.Copy, [bPS[t4]], [btm])
                            act(WI[:, t4, :], bank(t4, 8, 64), AF.Copy, [bPS[t4]], [bWI[t4]], scale=0.125 * (8.0 ** -0.5))
                            dma(CS, cs_d[pos:pos + 128, :], [B_ext], [bCS])
                            rope(tm[:, 0:128].rearrange("p (h d) -> p h d", d=64), 2, btm, CS)
                            cp(TMB[:, 0:128], tm[:, 0:128], [btm], [bTMB])
                            tr(pvT[:, 0, :], TMB[:, 0:128], IDB, [bTMB, bID], [bPS[psb]])
                            cp(KIT[:, pos:pos + 128], pvT[:, 0, :], [bPS[psb]], [bKIT], eng="act")
                if dbg and s == dbg.get("_s", 0) and l == dbg.get("_l", 0):
                    pass
                cp(UT[:, :, 0:30], UH, [bUH], bUT)
                for j2 in range(4):
                    wa, ba_ = load_w(wl_in[:, :, C_A + j2 * 256:C_A + (j2 + 1) * 256], lambda t: t, [B_w["in"]])
                    wbv, bb_ = load_w(wl_in[:, :, C_B + j2 * 256:C_B + (j2 + 1) * 256], lambda t: t, [B_w["in"]])
                    for j in range(2):
                        oc = j2 * 2 + j
                        pa, pb = (0, 1) if oc % 2 == 0 else (2, 3)
                        for kc in range(8):
                            mm(bank(pa), wa[:, kc, j * 128:(j + 1) * 128], XNT[:, kc, :], kc == 0, kc == 7, [ba_] + bXNTt, [bPS[pa]])
                        for kc in range(8):
                            mm(bank(pb), wbv[:, kc, j * 128:(j + 1) * 128], XNT[:, kc, :], kc == 0, kc == 7, [bb_] + bXNTt, [bPS[pb]])
                        sg = SGP[oc % 2]
                        act(sg, bank(pb), AF.Sigmoid, [bPS[pb]], [bSGP[oc % 2]])
                        tt(UT[:, oc, 30:542], bank(pa), sg, ALU.mult, [bPS[pa], bSGP[oc % 2]], [bUT[oc]])
                cp(UH, UT[:, :, 512:542], bUT, [bUH])
                for j2 in range(4):
                    wg, bg_ = load_w(wl_in[:, :, C_GA + j2 * 256:C_GA + (j2 + 1) * 256], lambda t: t, [B_w["in"]])
                    for j in range(2):
                        oc = j2 * 2 + j
                        pg = 4 + (oc % 2)
                        for kc in range(8):
                            mm(bank(pg), wg[:, kc, j * 128:(j + 1) * 128], XNT[:, kc, :], kc == 0, kc == 7, [bg_] + bXNTt, [bPS[pg]])
                        sg = SGP[oc % 2]
                        act(sg, bank(pg), AF.Sigmoid, [bPS[pg]], [bSGP[oc % 2]])
                        dma(ga_d[oc], sg, [bSGP[oc % 2]], [B_gad[oc]])
                for oc in range(8):
                    DGC = DGCs[oc % 2]
                    bDGC = bDGCs[oc % 2]
                    for k in range(31):
                        ts(DGC[:, k, :], IDB, vcol(V_CDW + (l * 8 + oc) * 31 + k), None, ALU.mult, None, [bID, bVEC], [bDGC])
                    pc = 6 + (oc % 2)
                    for k in range(31):
                        mm(bank(pc), DGC[:, k, :], UT[:, oc, k:k + 512], k == 0, k == 30, [bDGC, bUT[oc]], [bPS[pc]])
                    act(CV[:, oc, :], bank(pc), AF.Identity, [bPS[pc], bVEC], [bCV[oc]], bias=vcol(V_CDB + l * 8 + oc))
                for oc in range(8):
                    mm(bank(0), ONESF, CV[:, oc, :], oc == 0, oc == 7, [bID, bCV[oc]], [bPS[0]])
                for oc in range(8):
                    sq = SQ[oc % 2]
                    act(sq, CV[:, oc, :], AF.Square, [bCV[oc]], [bSQ[oc % 2]])
                    mm(bank(1), ONESF, sq, oc == 0, oc == 7, [bID, bSQ[oc % 2]], [bPS[1]])
                act(MEAN, bank(0), AF.Copy, [bPS[0]], [bMEAN])
                act(SQ[0], bank(0), AF.Square, [bPS[0]], [bSQ[0]])
                tt(RSD, bank(1), SQ[0], ALU.subtract, [bPS[1], bSQ[0]], [bRSD])
                act(RSD, RSD, AF.Sqrt, [bRSD], [bRSD], bias=1e-5, scale=1.0)
                recip(RSD, RSD, [bRSD], [bRSD])
                for oc in range(8):
                    tt(CV[:, oc, :], CV[:, oc, :], MEAN, ALU.subtract, [bCV[oc], bMEAN], [bCV[oc]])
                    tt(CV[:, oc, :], CV[:, oc, :], RSD, ALU.mult, [bCV[oc], bRSD], [bCV[oc]])
                    act(CVN[:, oc, 0:512], CV[:, oc, :], AF.Silu, [bCV[oc], bVEC], [bUT[oc]],
                        bias=vcol(V_LNB + l * 8 + oc), scale=vcol(V_LNG + l * 8 + oc))
                for j2 in range(4):
                    wy, by_ = load_w(wl_co[:, :, j2 * 256:(j2 + 1) * 256], lambda t: t, [B_w["co"]])
                    wg, bg_ = load_w(wl_in[:, :, C_GC + j2 * 256:C_GC + (j2 + 1) * 256], lambda t: t, [B_w["in"]])
                    for j in range(2):
                        oc = j2 * 2 + j
                        py, pg = (0, 1) if oc % 2 == 0 else (2, 3)
                        for kc in range(8):
                            mm(bank(py), wy[:, kc, j * 128:(j + 1) * 128], CVN[:, kc, 0:512], kc == 0, kc == 7, [by_] + bUT, [bPS[py]])
                        for kc in range(8):
                            mm(bank(pg), wg[:, kc, j * 128:(j + 1) * 128], XNT[:, kc, :], kc == 0, kc == 7, [bg_] + bXNTt, [bPS[pg]])
                        sg = SGP[oc % 2]
                        act(sg, bank(pg), AF.Sigmoid, [bPS[pg]], [bSGP[oc % 2]])
                        mt_ = MCT[oc % 2]
                        tt(mt_, bank(py), sg, ALU.mult, [bPS[py], bSGP[oc % 2]], [bMCT[oc % 2]])
                        dma(mc_d[oc], mt_, [bMCT[oc % 2]], [B_mcd[oc]])

                P.alias(A_BUFS, P_BUFS)
                bYAT = bXNTt
                def emit_M1(qb):
                    for oc in range(8):
                        wv, wbuf = load_w(wl_ao[:, :, oc * 128:(oc + 1) * 128],
                                          lambda t: t.rearrange("p a b -> p (a b)")[0:64, :].rearrange("p (h c) -> p h c", c=128),
                                          [B_w["ao"]])
                        psb = 6 + (oc // 4) % 2
                        po = (oc % 4) * 128
                        for h in range(16):
                            mm(bank(psb, 128, po), wv[:, h, :], AT4[0:64, h, :], h == 0, h == 15, [wbuf, bAT4], [bPS[psb]])
                        if oc % 4 == 3:
                            o4 = oc - 3
                            cp(XNT[:, o4:o4 + 4, qb * 128:(qb + 1) * 128], bank(psb).rearrange("p (a b) -> p a b", b=128),
                               [bPS[psb]], [bYAT[qb]], eng="act")

                for qb in range(4 if 'A' not in os.environ.get('KSKIP', '') else 0):
                    ti = s * 4 + qb
                    if (ti + 1) * 128 <= PADN:
                        memset(XNT[:, :, qb * 128:(qb + 1) * 128], 0.0, [bYAT[qb]])
                        continue
                    L = (ti + 1) * 128
                    nch = (L + 511) // 512
                    for h in range(8):
                        ts(DG[:, h, :], IDB, WI[:, qb, h:h + 1], None, ALU.mult, None, [bID, bWI[qb]], [bDG])
                    for c in range(nch):
                        k0 = c * 512
                        w = min(512, L - k0)
                        pacc = 6 + (c % 2)
                        for hp in range(4):
                            pbk = [(0, 1), (2, 3), (4, 5)][(c * 4 + hp) % 3]
                            for j in range(2):
                                h = hp * 2 + j
                                mm(bank(pbk[j], w), QITZ[h % 2][:, h // 2, qb * 128:(qb + 1) * 128], KIT[:, k0:k0 + w],
                                   True, True, [bQIT[qb], bKIT], [bPS[pbk[j]]])
                            src = PS[:, pbk[0] * 512:pbk[0] * 512 + 1024].rearrange("p (a b) -> p a b", b=512)[:, :, 0:w]
                            act(RR_[:, hp * 2:hp * 2 + 2, 0:w], src, AF.Relu, [bPS[pbk[0]], bPS[pbk[1]]], [bRR[hp]])
                        for h in range(8):
                            mm(bank(pacc, w), DG[:, h, :], RR_[:, h, 0:w], h == 0, h == 7, [bDG, bRR[h // 2]], [bPS[pacc]])
                        cp(SC[:, k0:k0 + w], bank(pacc, w), [bPS[pacc]], [bSC])
                    if qb >= 1 and (s * 4 + qb) * 128 > PADN:
                        emit_M1(qb - 1)
                    P.op("dve", lambda e, L=L: e.tensor_reduce(out=SM[:, 0:1], in_=SC[:, 0:L], axis=AX.X, op=ALU.max,
                                                               apply_absolute_value=True), [bSC], [bSM])
                    if ti * 128 < PADN + NMETA:
                        pass
                    memset(SC[:, 0:PADN], NEG, [bSC])
                    tt(SC[:, L - 128:L], SC[:, L - 128:L], CAUS, ALU.add, [bSC, bID], [bSC])
                    amax, lo, mid, ge, tcm = SM[:, 0:1], SM[:, 1:2], SM[:, 2:3], SM[:, 4:5], SM[:, 6:7]
                    cnt = SM[:, 3:4]
                    sA = SM[:, 5:6]
                    STP = SM[:, 8:8 + NBIS]
                    La = (int(0.55 * L) // 128) * 128 if (L >= 1024 and 'H' not in os.environ.get('KSKIP', '')) else 0
                    ts(lo, amax, -1.0, -1e-6, ALU.mult, ALU.add, [bSM], [bSM])
                    ts(STP, VEC[:, V_BIS:V_BIS + NBIS], amax, 2.0, ALU.mult, ALU.mult, [bSM, bVEC], [bSM])
                    for it in range(NBIS if 'B' not in os.environ.get('KSKIP', '') else 0):
                        tt(mid, lo, STP[:, it:it + 1], ALU.add, [bSM], [bSM])
                        if La > 0:
                            act(JUNKA[:, 0:La], SC[:, 0:La], AF.Sign, [bSC, bSM], [bJA, bSMa], bias=mid, scale=-1.0, accum=sA)
                        ts(JUNK[:, 0:L - La], SC[:, La:L], mid, 0.0, ALU.is_ge, ALU.add, [bSC, bSM], bRR + [bSMc], accum=cnt)
                        if La > 0:
                            stt(tcm, sA, -0.5, cnt, ALU.mult, ALU.add, [bSMa, bSMc], [bSM])
                            ts(ge, tcm, KSEL - 0.5 - La / 2.0, None, ALU.is_ge, None, [bSM], [bSM])
                        else:
                            ts(ge, cnt, KSEL - 0.5, None, ALU.is_ge, None, [bSMc], [bSM])
                        stt(lo, ge, STP[:, it:it + 1], lo, ALU.mult, ALU.add, [bSM], [bSM])
                    ts(DLO, IDF, lo, None, ALU.mult, None, [bID, bSM], [bTHB])
                    mm(bank(7, 128), ONES1, DLO, True, True, [bID, bTHB], [bPS[7]])
                    cp(THB, bank(7, 128), [bPS[7]], [bTHB])
                    nblk = ti + 1
                    steps = []
                    for c in range(nch):
                        nb = min(4, nblk - c * 4)
                        for j in range(nb):
                            for g in range(4):
                                steps.append((c, j, c * 4 + j, g))

                    def issue_mask(c):
                        nb = min(4, nblk - c * 4)
                        pT = bank(7).rearrange("p (a b) -> p a b", b=128)
                        for j in range(nb):
                            kb = c * 4 + j
                            tr(pT[:, j, :], SC[:, kb * 128:(kb + 1) * 128], IDF, [bSC, bID], [bPS[7]])
                        mt = MT[c % 2]
                        tt(mt[:, 0:nb, :], pT[:, 0:nb, :], THB.unsqueeze(1).to_broadcast([128, nb, 128]), ALU.is_ge,
                           [bPS[7], bTHB], [bMT[c % 2]])

                    if 'S' in os.environ.get('KSKIP', ''):
                        steps = steps[:4]
                    issue_mask(0)
                    DEPTH = 2
                    for n in range(len(steps) + DEPTH):
                        if n < len(steps):
                            c, j, kb, g = steps[n]
                            if j == 0 and g == 0 and c + 1 < nch:
                                issue_mask(c + 1)
                            mt = MT[c % 2]
                            e_, gp = g % 2, g // 2
                            psb = 4 + (n % 3)
                            ptb = n % 3
                            mm(bank(psb), KT[:, gp, kb * 128:(kb + 1) * 128],
                               QTZ[e_][:, gp * 4:gp * 4 + 4, qb * 128:(qb + 1) * 128],
                               True, True, [bKT, bQT[qb]], [bPS[psb]])
                            act(PT[ptb], bank(psb), AF.Exp, [bPS[psb]], [bPT[ptb]])
                            pv4 = PT[ptb].rearrange("p (a b) -> p a b", b=128)
                            tt(pv4, pv4, mt[:, j, :].unsqueeze(1).to_broadcast([128, 4, 128]), ALU.mult,
                               [bPT[ptb], bMT[c % 2]], [bPT[ptb]], eng="dve")
                        if n >= DEPTH:
                            c2, j2, kb2, g2 = steps[n - DEPTH]
                            ptb2 = (n - DEPTH) % 3
                            mm(PS[0:65, g2 * 512:(g2 + 1) * 512], VV[:, kb2, g2, :], PT[ptb2], kb2 == 0, (kb2 == nblk - 1) or ('S' in os.environ.get('KSKIP', '')),
                               [bVV, bPT[ptb2]], [bPS[g2]])
                    for g in range(4):
                        RRW = RRWs[g % 2]
                        RBS = RRW
                        bRRW, bRBS = bRRWs[g % 2], bRBSs[g % 2]
                        ts(RRW[64:65, :], PS[64:65, g * 512:(g + 1) * 512], 1e-30, None, ALU.add, None, [bPS[g]], [bRRW])
                        recip(RRW[64:65, :], RRW[64:65, :], [bRRW], [bRRW])
                        psb = 4 + (g % 2)
                        mm(PS[0:64, psb * 512:(psb + 1) * 512], ONES1[64:65, 0:64], RRW[64:65, :], True, True, [bID, bRRW], [bPS[psb]])
                        act(RBS[0:64, :], PS[0:64, psb * 512:(psb + 1) * 512], AF.Copy, [bPS[psb]], [bRBS])
                        tt(AT4[0:64, g * 4:(g + 1) * 4, :], PS[0:64, g * 512:(g + 1) * 512].rearrange("p (a b) -> p a b", b=128),
                           RBS[0:64, :].rearrange("p (a b) -> p a b", b=128), ALU.mult, [bPS[g], bRBS], [bAT4])
                if 'A' not in os.environ.get('KSKIP', ''):
                    emit_M1(3)

                P.alias(F_BUFS, A_BUFS)
                for oc in range(8):
                    i2 = oc % 2
                    dma(GAT[i2], ga_d[oc], [B_gad[oc]], [bGAT[i2]])
                    dma(MCL[i2], mc_d[oc], [B_mcd[oc]], [bMCL[i2]])
                    tt(TMPF, XNT[:, oc, :], GAT[i2], ALU.mult, bYAT + [bGAT[i2]], [bTMPF])
                    tt(XNT[:, oc, :], TMPF, MCL[i2], ALU.add, [bTMPF, bMCL[i2]], bYAT)
                for c2 in range(2):
                    rfn, wbufs = load_w2(wl_o[:, :, c2 * 512:(c2 + 1) * 512], 8, [B_w["o"]])
                    for t4 in range(4):
                        for kc in range(8):
                            mm(bank(t4), XNT[:, kc, t4 * 128:(t4 + 1) * 128], rfn(kc), kc == 0, kc == 7,
                               [bYAT[t4]] + wbufs, [bPS[t4]])
                    for t4 in range(4):
                        i2 = t4 % 2
                        dma(HR[i2], h_in[p0 + t4 * 128:p0 + (t4 + 1) * 128, c2 * 512:(c2 + 1) * 512], [B_hin[s]], [bHR[i2]])
                        tt(HM[:, t4, c2 * 512:(c2 + 1) * 512], bank(t4), HR[i2], ALU.add, [bPS[t4], bHR[i2]], [bHM[t4]])
                if s == 0:
                    for t4 in range(4):
                        ts(HM[:, t4, :], HM[:, t4, :], vcol(V_RM0 + t4), None, ALU.mult, None, [bHM[t4], bVEC], [bHM[t4]])
                for t4 in range(4):
                    rmsnorm_to_T(HM[:, t4, :], bHM[t4], XS2, bXS2, SS2, bSS2, V_FNG + l * 8, XNT, bXNTt[t4], t4, 4 + (t4 % 2))
                for (c0g, ng) in ([(0, 8), (8, 8), (16, 6)] if 'F' not in os.environ.get('KSKIP', '') else []):
                    for cpair in range(ng // 2):
                        cA = c0g + cpair * 2
                        wg, bwg = load_w(wl_up[:, :, cA * 128:cA * 128 + 256], lambda t: t, [B_w["up"]])
                        wu, bwu = load_w(wl_up[:, :, FH + cA * 128:FH + cA * 128 + 256], lambda t: t, [B_w["up"]])
                        for j in range(2):
                            c = cA + j
                            res = []
                            for (which, wv_, wb_) in ((0, wg, bwg), (1, wu, bwu)):
                                psb = which * 2 + (c % 2)
                                cidx = c + which * 22
                                for kc in range(8):
                                    mm(bank(psb), wv_[:, kc, j * 128:(j + 1) * 128], XNT[:, kc, :], kc == 0, kc == 7,
                                       [wb_] + bXNTt, [bPS[psb]])
                                raw = RAW[which][c % 2]
                                braw = bRAW[which][c % 2]
                                act(raw[:, 2:514], bank(psb), AF.Copy, [bPS[psb]], [braw])
                                cp(raw[:, 0:2], FHL[:, cidx, :], [bFHL], [braw])
                                cg = CGU[which]
                                wofs = V_FDW + (l * 44 + cidx) * 3
                                ts(cg, raw[:, 2:514], vcol(wofs + 2), vcol(V_FDB + l * 44 + cidx), ALU.mult, ALU.add,
                                   [braw, bVEC], [bCGU[which]])
                                stt(cg, raw[:, 1:513], vcol(wofs + 1), cg, ALU.mult, ALU.add, [braw, bVEC, bCGU[which]], [bCGU[which]])
                                stt(cg, raw[:, 0:512], vcol(wofs + 0), cg, ALU.mult, ALU.add, [braw, bVEC, bCGU[which]], [bCGU[which]])
                                cp(FHL[:, cidx, :], raw[:, 512:514], [braw], [bFHL])
                            act(SGF, CGU[0], AF.Silu, [bCGU[0]], [bSGF])
                            tt(ACTT[:, c - c0g, :], SGF, CGU[1], ALU.mult, [bSGF, bCGU[1]], [bACTT])
                    for c2 in range(2):
                        rfn, wbufs = load_w2(wl_dn[:, c0g:c0g + ng, c2 * 512:(c2 + 1) * 512], ng, [B_w["dn"]])
                        for t4 in range(4):
                            psb = 4 + t4
                            for kc in range(ng):
                                mm(bank(psb), ACTT[:, kc, t4 * 128:(t4 + 1) * 128], rfn(kc), kc == 0, kc == ng - 1,
                                   [bACTT] + wbufs, [bPS[psb]])
                        for t4 in range(4):
                            tt(HM[:, t4, c2 * 512:(c2 + 1) * 512], HM[:, t4, c2 * 512:(c2 + 1) * 512], bank(4 + t4), ALU.add,
                               [bPS[4 + t4], bHM[t4]], [bHM[t4]])
                if last:
                    dma(FING, fing_d, [B_ext], [bFING])
                for t4 in range(4):
                    if s == 0:
                        ts(HM[:, t4, :], HM[:, t4, :], vcol(V_RM0 + t4), None, ALU.mult, None, [bHM[t4], bVEC], [bHM[t4]])
                    rows = slice(p0 + t4 * 128, p0 + (t4 + 1) * 128)
                    if not last:
                        dma(hA[rows, :], HM[:, t4, :], [bHM[t4]], [B_hA[s]])
                    elif s >= 1:
                        act(XS2, HM[:, t4, :], AF.Square, [bHM[t4]], [bXS2, bSS2], accum=SS2[:, 0:1])
                        act(SS2[:, 1:2], SS2[:, 0:1], AF.Sqrt, [bSS2], [bSS2], bias=1e-6, scale=1.0 / 1024.0)
                        recip(SS2[:, 2:3], SS2[:, 1:2], [bSS2], [bSS2])
                        ts(HM[:, t4, :], HM[:, t4, :], SS2[:, 2:3], None, ALU.mult, None, [bHM[t4], bSS2], [bHM[t4]])
                        tt(HM[:, t4, :], HM[:, t4, :], FING, ALU.mult, [bHM[t4], bFING], [bHM[t4]])
                        dma(out_d[p0 - 512 + t4 * 128:p0 - 512 + (t4 + 1) * 128, :], HM[:, t4, :], [bHM[t4]], [B_out])
        P.emit(final_bufs=[B_out, B_dbg])
    return nc


def host_consts(NS):
    TP = NS * 512
    pos = (np.arange(TP) - PADN).astype(np.float32)
    inv_freq = np.power(np.float32(500000.0), -np.arange(0, 16, 2, dtype=np.float32) / np.float32(16)).astype(np.float32)
    ang = (pos[:, None] * inv_freq[None, :]).astype(np.float32)
    cs = np.concatenate([np.cos(ang), np.sin(ang)], axis=1).astype(np.float32)
    t = np.arange(128)
    caus = np.where(t[None, :] <= t[:, None], 0.0, NEG).astype(np.float32)
    ident = np.eye(128, dtype=np.float32)
    return cs, caus, ident


def pack_vec(inp):
    vec = np.zeros((128, NV), np.float32)

    def pp(a):
        a = np.asarray(a, np.float32)
        lead = a.shape[:-1]
        n = a.shape[-1] // 128
        return np.moveaxis(a.reshape(*lead, n, 128), -1, 0)

    vec[:, V_ANG:V_ANG + 16] = pp(inp["attn_norm_g"]).reshape(128, 16)
    vec[:, V_FNG:V_FNG + 16] = pp(inp["ffn_norm_g"]).reshape(128, 16)
    vec[:, V_CDB:V_CDB + 16] = pp(inp["conv_dw_b"]).reshape(128, 16)
    vec[:, V_LNG:V_LNG + 16] = pp(inp["conv_ln_g"]).reshape(128, 16)
    vec[:, V_LNB:V_LNB + 16] = pp(inp["conv_ln_b"]).reshape(128, 16)
    cdw = pp(inp["conv_dw_w"])
    vec[:, V_CDW:V_CDW + 496] = np.transpose(cdw, (0, 1, 3, 2)).reshape(128, 496)
    fdw = pp(inp["ffn_dw_w"])
    vec[:, V_FDW:V_FDW + 264] = np.transpose(fdw, (0, 1, 3, 2)).reshape(128, 264)
    vec[:, V_FDB:V_FDB + 88] = pp(inp["ffn_dw_b"]).reshape(128, 88)
    rm = np.ones((128, 4), np.float32)
    p = np.arange(512).reshape(4, 128).T
    rm[p < PADN] = 0.0
    vec[:, V_RM0:V_RM0 + 4] = rm
    vec[:, V_BIS:V_BIS + NBIS] = (2.0 ** -(np.arange(NBIS) + 1.0)).astype(np.float32)[None, :]
    return vec


def make_in_map(inp, b, NS):
    TP = NS * 512
    nreal = TP - 512
    xpad = np.zeros((TP, D), np.float32)
    xpad[PADN:PADN + NMETA] = np.asarray(inp["meta_tokens"], np.float32)
    xpad[512:] = np.asarray(inp["x"][b, :nreal], np.float32)
    cs, caus, ident = host_consts(NS)
    m = {
        "xp": xpad, "cs": cs, "vec": pack_vec(inp), "caus": caus, "ident": ident,
        "fing": np.ascontiguousarray(np.broadcast_to(np.asarray(inp["final_norm_g"], np.float32)[None, :], (128, D))),
        "w_in": np.asarray(inp["w_in"], np.float32), "w_attn_out": np.asarray(inp["w_attn_out"], np.float32),
        "w_conv_out": np.asarray(inp["w_conv_out"], np.float32), "w_o": np.asarray(inp["w_o"], np.float32),
        "w_up": np.asarray(inp["w_up"], np.float32), "w_down": np.asarray(inp["w_down"], np.float32),
    }
    return m


_NC_CACHE = {}


def kernel(**inputs):
    NS = 17
    if NS not in _NC_CACHE:
        _NC_CACHE[NS] = build(NS)
    nc = _NC_CACHE[NS]
    B = inputs["x"].shape[0]
    in_maps = [make_in_map(inputs, (c // 2) % B, NS) for c in range(8)]
    res = run_bass_kernel_spmd(nc, in_maps, core_ids=list(range(8)))
    out = np.stack([res.results[2 * b]["out"] for b in range(B)], axis=0)
    return out.astype(np.float32)
```
